# Optimizing a Trainium2 kernel written in Bass

```python
import jax, jax.numpy as jnp
from jax import lax
import numpy as np

D_MODEL = 1024
BATCH = 8
SEQ = 4096
DEPTH = 4

N_A_LAYERS = DEPTH // 2
N_B_LAYERS = DEPTH - N_A_LAYERS
HEAD_DIM = 64
RWKV_HEADS = D_MODEL // HEAD_DIM
D_DECAY_LORA = max(32, int(round(1.8 * D_MODEL ** 0.5 / 32)) * 32)
D_AAA_LORA = max(32, int(round(1.8 * D_MODEL ** 0.5 / 32)) * 32)
D_MV_LORA = max(32, int(round(1.3 * D_MODEL ** 0.5 / 32)) * 32)
D_GATE_LORA = max(32, int(round(0.6 * D_MODEL ** 0.8 / 32)) * 32)
GN_EPS = 64e-5
N_Q_HEADS = D_MODEL // HEAD_DIM
N_KV_HEADS = 2
GROUP = N_Q_HEADS // N_KV_HEADS
WINDOW = 128
BLOCK = WINDOW
FFN_DIM = 4 * D_MODEL
RMS_EPS = 1e-6

kernel_name = 'yoco_rwkv7_swa_sink_hybrid'


def rms_norm(x, g):
    xf = x.astype(jnp.float32)
    y = xf * lax.rsqrt(jnp.mean(xf * xf, axis=-1, keepdims=True) + RMS_EPS)
    return (y * g.astype(jnp.float32)).astype(x.dtype)


def squared_relu_mlp(h, w1, w2):
    return jnp.square(jax.nn.relu(h @ w1)) @ w2


def token_shift(x):
    return jnp.pad(x, ((0, 0), (1, 0), (0, 0)))[:, :-1]


def wkv7(r, w, k, v, a, b):
    B, T, H, N = r.shape
    s0 = jnp.zeros((B, H, N, N), jnp.float32)

    def step(S, inp):
        r_t, w_t, k_t, v_t, a_t, b_t = inp
        sa = jnp.einsum('bhvk,bhk->bhv', S, a_t)
        S = S * w_t[:, :, None, :] + sa[..., None] * b_t[:, :, None, :] + v_t[..., None] * k_t[:, :, None, :]
        y = jnp.einsum('bhvk,bhk->bhv', S, r_t)
        return S, y

    xs = tuple(jnp.moveaxis(t, 1, 0) for t in (r, w, k, v, a, b))
    _, ys = lax.scan(step, s0, xs)
    return jnp.moveaxis(ys, 0, 1)


def rwkv7_time_mix(h, v_first, vres, mu, w_rkv, w0, w1, w2, a0, a1, a2, g1, g2, k_k, k_a, r_k, gn_g, gn_b, wo):
    f32 = jnp.float32
    B, T, C = h.shape
    H, N = RWKV_HEADS, HEAD_DIM
    xx = token_shift(h) - h
    xs = h[None] + xx[None] * mu[:, None, None, :]
    r, k, v = jnp.einsum('nbtc,ncd->nbtd', xs[:3], w_rkv)
    xv, xw, xa, xg = xs[2], xs[3], xs[4], xs[5]
    w_log = -jax.nn.softplus(-(w0 + jnp.tanh(xw @ w1) @ w2).astype(f32)) - 0.5
    decay = jnp.exp(-jnp.exp(w_log))
    if vres is None:
        v_first = v
    else:
        v0, v1, v2 = vres
        v = v + (v_first - v) * jax.nn.sigmoid(v0 + (xv @ v1) @ v2)
    a = jax.nn.sigmoid(a0 + (xa @ a1) @ a2)
    g = jax.nn.sigmoid(xg @ g1) @ g2

    def split(t):
        return t.reshape(B, T, H, N).astype(f32)

    kk = split(k * k_k)
    kk = kk / jnp.maximum(jnp.sqrt(jnp.sum(kk * kk, axis=-1, keepdims=True)), 1e-12)
    k_mod = k * (1.0 + (a - 1.0) * k_a)
    rh, kh, vh, ah, wh = split(r), split(k_mod), split(v), split(a), split(decay)
    y = wkv7(rh, wh, kh, vh, -kk, kk * ah)
    mean = jnp.mean(y, axis=-1, keepdims=True)
    var = jnp.mean(jnp.square(y - mean), axis=-1, keepdims=True)
    y = (y - mean) * lax.rsqrt(var + GN_EPS) * gn_g.reshape(H, N).astype(f32) + gn_b.reshape(H, N).astype(f32)
    y = y + jnp.sum(rh * kh * r_k.astype(f32), axis=-1, keepdims=True) * vh
    out = (y.reshape(B, T, C).astype(h.dtype) * g) @ wo
    return out, v_first


def shared_kv(x, kv_norm, w_kv, k_gain):
    B, T, _ = x.shape
    kv = (rms_norm(x, kv_norm) @ w_kv).reshape(B, T, 2, N_KV_HEADS, HEAD_DIM)
    k = rms_norm(kv[:, :, 0], k_gain)
    v = kv[:, :, 1]
    return k, v


def swa_sink_attention(q, k, v, sinks):
    B, T, _, D = q.shape
    nb = T // BLOCK
    qb = q.reshape(B, nb, BLOCK, N_KV_HEADS, GROUP, D)
    kb = k.reshape(B, nb, BLOCK, N_KV_HEADS, D)
    vb = v.reshape(B, nb, BLOCK, N_KV_HEADS, D)
    pad = ((0, 0), (1, 0), (0, 0), (0, 0), (0, 0))
    kcat = jnp.concatenate([jnp.pad(kb, pad)[:, :-1], kb], axis=2)
    vcat = jnp.concatenate([jnp.pad(vb, pad)[:, :-1], vb], axis=2)
    s = jnp.einsum('bnqhgd,bnshd->bnhgqs', qb, kcat, preferred_element_type=jnp.float32)
    qi = jnp.arange(BLOCK)[:, None]
    kj = jnp.arange(2 * BLOCK)[None, :]
    dist = qi + BLOCK - kj
    blk = jnp.arange(nb)[:, None, None]
    valid = (dist >= 0)[None] & (dist < WINDOW)[None] & ((blk > 0) | (kj >= BLOCK)[None])
    slopes = jnp.exp2(-8.0 * jnp.arange(1, N_Q_HEADS + 1, dtype=jnp.float32) / N_Q_HEADS)
    alibi = slopes.reshape(N_KV_HEADS, GROUP)[:, :, None, None] * dist.astype(jnp.float32)
    s = jnp.where(valid[None, :, None, None], s - alibi, -jnp.inf)
    sink = sinks.astype(jnp.float32).reshape(N_KV_HEADS, GROUP)[None, None, :, :, None, None]
    m = jnp.maximum(jnp.max(s, axis=-1, keepdims=True), sink)
    p = jnp.exp(s - m)
    p = p / (jnp.sum(p, axis=-1, keepdims=True) + jnp.exp(sink - m))
    o = jnp.einsum('bnhgqs,bnshd->bnqhgd', p.astype(v.dtype), vcat)
    return o.reshape(B, T, N_Q_HEADS * D)


def swa_layer(h, k_sh, v_sh, wq, q_gain, sinks, wo):
    B, T, _ = h.shape
    q = (h @ wq).reshape(B, T, N_Q_HEADS, HEAD_DIM)
    q = rms_norm(q, q_gain) * (HEAD_DIM ** -0.5)
    return swa_sink_attention(q, k_sh, v_sh, sinks) @ wo


def setup_inputs(seed: int = 0) -> dict:
    key = jax.random.key(seed)
    ks = iter(jax.random.split(key, 40))
    f32 = jnp.float32
    C, F, NA, NB = D_MODEL, FFN_DIM, N_A_LAYERS, N_B_LAYERS

    def nrm(shape, scale):
        return jax.random.normal(next(ks), shape, f32) * scale

    def gain(shape):
        return 1.0 + nrm(shape, 0.02)

    return {
        'x': nrm((BATCH, SEQ, C), 1.0),
        'ln_mix': gain((DEPTH, C)),
        'ln_mlp': gain((DEPTH, C)),
        'mlp_w1': nrm((DEPTH, C, F), C ** -0.5),
        'mlp_w2': nrm((DEPTH, F, C), F ** -0.5),
        'a_mu': jax.random.uniform(next(ks), (NA, 6, C), f32),
        'a_w_rkv': nrm((NA, 3, C, C), C ** -0.5),
        'a_w0': jax.random.uniform(next(ks), (NA, C), f32, -6.0, -0.5),
        'a_w1': nrm((NA, C, D_DECAY_LORA), C ** -0.5),
        'a_w2': nrm((NA, D_DECAY_LORA, C), 0.1 * D_DECAY_LORA ** -0.5),
        'a_a0': nrm((NA, C), 0.5),
        'a_a1': nrm((NA, C, D_AAA_LORA), C ** -0.5),
        'a_a2': nrm((NA, D_AAA_LORA, C), 0.1 * D_AAA_LORA ** -0.5),
        'a_g1': nrm((NA, C, D_GATE_LORA), C ** -0.5),
        'a_g2': nrm((NA, D_GATE_LORA, C), D_GATE_LORA ** -0.5),
        'a_k_k': 0.85 + nrm((NA, C), 0.05),
        'a_k_a': 1.0 + nrm((NA, C), 0.05),
        'a_r_k': nrm((NA, RWKV_HEADS, HEAD_DIM), 0.1),
        'a_gn_g': gain((NA, C)),
        'a_gn_b': nrm((NA, C), 0.02),
        'a_wo': nrm((NA, C, C), C ** -0.5),
        'a_v0': 1.0 + nrm((NA - 1, C), 0.1),
        'a_v1': nrm((NA - 1, C, D_MV_LORA), C ** -0.5),
        'a_v2': nrm((NA - 1, D_MV_LORA, C), 0.1 * D_MV_LORA ** -0.5),
        'kv_norm': gain((C,)),
        'w_kv': nrm((C, 2 * N_KV_HEADS * HEAD_DIM), C ** -0.5),
        'k_gain': gain((HEAD_DIM,)),
        'b_wq': nrm((NB, C, N_Q_HEADS * HEAD_DIM), C ** -0.5),
        'b_q_gain': gain((NB, HEAD_DIM)),
        'b_sinks': nrm((NB, N_Q_HEADS), 0.5),
        'b_wo': nrm((NB, N_Q_HEADS * HEAD_DIM, C), (N_Q_HEADS * HEAD_DIM) ** -0.5),
    }


def reference(x, ln_mix, ln_mlp, mlp_w1, mlp_w2, a_mu, a_w_rkv, a_w0, a_w1, a_w2, a_a0, a_a1, a_a2, a_g1, a_g2, a_k_k, a_k_a, a_r_k, a_gn_g, a_gn_b, a_wo, a_v0, a_v1, a_v2, kv_norm, w_kv, k_gain, b_wq, b_q_gain, b_sinks, b_wo):
    v_first = None
    k_sh = None
    v_sh = None
    for i in range(DEPTH):
        h = rms_norm(x, ln_mix[i])
        if i < N_A_LAYERS:
            j = i
            vres = None if j == 0 else (a_v0[j - 1], a_v1[j - 1], a_v2[j - 1])
            out, v_first = rwkv7_time_mix(h, v_first, vres, a_mu[j], a_w_rkv[j], a_w0[j], a_w1[j], a_w2[j], a_a0[j], a_a1[j], a_a2[j], a_g1[j], a_g2[j], a_k_k[j], a_k_a[j], a_r_k[j], a_gn_g[j], a_gn_b[j], a_wo[j])
        else:
            j = i - N_A_LAYERS
            out = swa_layer(h, k_sh, v_sh, b_wq[j], b_q_gain[j], b_sinks[j], b_wo[j])
        x = x + out
        x = x + squared_relu_mlp(rms_norm(x, ln_mlp[i]), mlp_w1[i], mlp_w2[i])
        if i == N_A_LAYERS - 1:
            k_sh, v_sh = shared_kv(x, kv_norm, w_kv, k_gain)
    return x
```

```python
import os
import numpy as np
from contextlib import ExitStack
import concourse.bass as bass
import concourse.mybir as mybir
from concourse.bass_utils import run_bass_kernel_spmd

F32 = mybir.dt.float32
BF16 = mybir.dt.bfloat16
F32R = mybir.dt.float32r
ALU = mybir.AluOpType
AF = mybir.ActivationFunctionType

C = 1024
NC_ = 8
TT = 512
FF = 4096
ENGS = ("pe", "act", "dve", "pool", "sp")


class Buf:
    __slots__ = ("name", "w", "r")

    def __init__(self, name):
        self.name = name
        self.w = None
        self.r = {}


class Tl:
    def __init__(self, h, name):
        self.h = h
        self.b = Buf(name)

    def __getitem__(self, k):
        return self.h[k]


class Sched:
    def __init__(self, nc):
        self.nc = nc
        self.q = {e: [] for e in ENGS}
        self.cnt = {}
        self.known = {e: {} for e in ENGS}
        self.pending = {e: False for e in ENGS}
        self.sems = {}
        for e in ENGS:
            self.sems[e] = nc.alloc_semaphore("sem_" + e)
        self.nins = 0
        self.dman = {}

    def _sem(self, k):
        if k not in self.sems:
            self.sems[k] = self.nc.alloc_semaphore(k.replace(":", "_"))
        return self.sems[k]

    def _need(self, eng, waits, tok):
        if tok is None:
            return
        k, v = tok
        if k == eng and eng in ("pe", "sp"):
            return
        if self.known[eng].get(k, 0) >= v:
            return
        if waits.get(k, 0) < v:
            waits[k] = v

    def op(self, eng, fn, R=(), W=(), inc=True, dma=None):
        waits = {}
        for t in R:
            self._need(eng, waits, t.b.w)
        for t in W:
            self._need(eng, waits, t.b.w)
            for k, v in t.b.r.items():
                self._need(eng, waits, (k, v))
        if dma is not None:
            K = 24 if eng == "sp" else 12
            n_ = self.dman.get(eng, 0)
            self.dman[eng] = n_ + 1
            key = f"dma:{eng}{n_ % K}"
            self._sem(key)
            if self.cnt.get(key, 0):
                self._need(eng, waits, (key, self.cnt[key]))
        for k, v in waits.items():
            self.known[eng][k] = v
        self.nins += 1
        if dma is not None:
            self.cnt[key] = self.cnt.get(key, 0) + 16
            tok = (key, self.cnt[key])
            self.q[eng].append((waits, fn, key, 16))
        elif inc:
            self.cnt[eng] = self.cnt.get(eng, 0) + 1
            tok = (eng, self.cnt[eng])
            self.pending[eng] = False
            self.q[eng].append((waits, fn, eng, 1))
        else:
            tok = (eng, self.cnt.get(eng, 0) + 1)
            self.pending[eng] = True
            self.q[eng].append((waits, fn, None, 0))
        for t in R:
            k, v = tok
            if t.b.r.get(k, 0) < v:
                t.b.r[k] = v
        for t in W:
            t.b.w = tok
            t.b.r = {}
        return tok

    def I(self, eng, meth, *args, R=(), W=(), inc=True, dma=None, **kw):
        return self.op(eng, lambda e: getattr(e, meth)(*args, **kw), R=R, W=W, inc=inc, dma=dma)

    def barrier(self):
        for e in ENGS:
            waits = {}
            for k, v in self.cnt.items():
                self._need(e, waits, (k, v))
            for k, v in waits.items():
                self.known[e][k] = v
            if waits:
                self.q[e].append((waits, None, None, 0))

    def flush(self, final=False):
        nc = self.nc
        for e in ENGS:
            assert not self.pending[e], e
        sems = self.sems
        q = self.q
        self.q = {e: [] for e in ENGS}
        cnt = dict(self.cnt)
        with nc.Block() as block:
            def run(engkey, engobj):
                for waits, fn, inck, incv in q[engkey]:
                    for k, v in waits.items():
                        engobj.wait_ge(sems[k], v)
                    if fn is None:
                        continue
                    ins = fn(engobj)
                    if inck is not None:
                        ins.then_inc(sems[inck], incv)
                if final and engkey == "sp":
                    for k, v in cnt.items():
                        if k.startswith("dma:"):
                            engobj.wait_ge(sems[k], v)

            @block.tensor
            def _(e):
                run("pe", e)

            @block.scalar
            def _(e):
                run("act", e)

            @block.vector
            def _(e):
                run("dve", e)

            @block.gpsimd
            def _(e):
                run("pool", e)

            @block.sync
            def _(e):
                run("sp", e)


VEC_NAMES = []


def _vec_index():
    names = []
    for i in range(4):
        names += [f"ln_mix{i}", f"ln_mlp{i}"]
    for j in range(2):
        names += [f"mu{j}_{n}" for n in range(6)]
        names += [f"w0_{j}", f"a0_{j}", f"kk_{j}", f"ka_{j}", f"rk_{j}", f"gng_{j}", f"gnb_{j}"]
    names += ["v0", "kvn"]
    return {n: i for i, n in enumerate(names)}


VIDX = _vec_index()
NV = len(VIDX)


def make_consts():
    c = {}
    c["ident"] = np.eye(128, dtype=np.float32)
    s = np.arange(128)[:, None]
    t = np.arange(128)[None, :]
    su = (s < t).astype(np.float32)
    ule = (s <= t).astype(np.float32)
    sl = (s > t).astype(np.float32)
    c["m_abrb"] = np.concatenate([su, ule], 1)
    c["m_akrk"] = np.concatenate([su, ule], 1)
    c["m_sl"] = sl
    bd = np.zeros((128, 128), np.float32)
    bd[:64, :64] = 1
    bd[64:, 64:] = 1
    c["bd1"] = bd
    c["bd64"] = bd / 64.0
    c["onesm"] = np.full((128, 128), 1.0 / 1024.0, np.float32)
    c["ones1"] = np.ones((128, 128), np.float32)
    slopes = np.exp2(-8.0 * np.arange(1, 17, dtype=np.float32) / 16.0).astype(np.float32)
    j = np.arange(128)[:, None].astype(np.float32)
    i = np.arange(128)[None, :].astype(np.float32)
    dcur = i - j
    dprev = i + 128 - j
    bc = np.where(dcur[:, None, :] >= 0, -slopes[None, :, None] * dcur[:, None, :], -30000.0)
    bp = np.where(dprev[:, None, :] < 128, -slopes[None, :, None] * dprev[:, None, :], -30000.0)
    bc = bc.reshape(128, 8, 2, 128).transpose(0, 2, 1, 3)
    bp = bp.reshape(128, 8, 2, 128).transpose(0, 2, 1, 3)
    c["alibi"] = np.ascontiguousarray(np.stack([bp, bc], 1)).astype(np.float32)
    return c


CONST_ORDER = ["ident", "m_abrb", "m_sl", "bd1", "bd64", "onesm", "ones1"]


class Builder:
    def __init__(self, T, phases, dbg=None):
        self.T = T
        self.NT = T // TT
        self.phases = phases
        self.dbg = dbg or {}
        nc = self.nc = bass.Bass("TRN2", target_bir_lowering=False)
        self.S = Sched(nc)
        d = self.d = {}

        def din(name, shape, dt=F32):
            d[name] = nc.dram_tensor(name, list(shape), dt, kind="ExternalInput").ap()

        din("xT", [C, T])
        din("vecs", [128, NV, 8])
        din("hvec", [128, 8])
        din("sinkrow", [1, 2, 2, 2, 4, 128])
        din("cmat", [128, len(CONST_ORDER) + 1, 128])
        din("alibi", [128, 2, 2, 8, 128])
        din("mlp_w1", [4, C, FF])
        din("mlp_w2", [4, FF, C])
        din("a_w_rkv", [2, 3, C, C])
        din("a_w1", [2, C, 64])
        din("a_w2", [2, 64, C])
        din("a_a1", [2, C, 64])
        din("a_a2", [2, 64, C])
        din("a_g1", [2, C, 160])
        din("a_g2", [2, 160, C])
        din("a_wo", [2, C, C])
        din("a_v1", [1, C, 32])
        din("a_v2", [1, 32, C])
        din("w_kv", [C, 256])
        din("b_wq", [2, C, C])
        din("b_wo", [2, C, C])
        d["yT"] = nc.dram_tensor("yT", [C, T], F32, kind="ExternalOutput").ap()
        d["vfirst"] = nc.dram_tensor("vfirst", [C, T], F32, kind="Internal").ap()
        d["wrkv_bf"] = nc.dram_tensor("wrkv_bf", [2, 3, C, C], BF16, kind="Internal").ap()
        self.WBF = Tl(None, "wrkv_bf")
        for name, shape in self.dbg.items():
            d[name] = nc.dram_tensor(name, list(shape), F32, kind="ExternalOutput").ap()
        self.XD = [Tl(None, f"xres_dram{i}") for i in range(self.NT)]
        self.VFD = Tl(None, "vfirst_dram")
        self.DBG = Tl(None, "dbg")
        self.bank_i = 0
        self.dmaq = 0

    def sb(self, st, name, shape, dt):
        self.uid = getattr(self, "uid", 0) + 1
        name = f"s{self.uid}_{name}"
        h = st.enter_context(self.nc.sbuf_tensor(name, list(shape), dt))
        return Tl(h, name)

    def bank(self):
        b = self.ps[self.bank_i % 8]
        self.bank_i += 1
        return b

    def xview(self, ap, t0, n=TT):
        return ap[:, t0:t0 + n].rearrange("(c p) t -> p c t", p=128)

    def load_cast(self, tl, out_ap, in_ap):
        self.S.I("pool", "dma_start", out=out_ap, in_=in_ap, W=[tl], dma=tl.b.name)

    def build(self):
        nc, S, d = self.nc, self.S, self.d
        with ExitStack() as g:
            self.ps = [Tl(g.enter_context(nc.psum_tensor(f"ps{i}", [128, 512], F32)), f"ps{i}") for i in range(8)]
            cm = self.sb(g, "cmat32", [128, len(CONST_ORDER) + 1, 128], F32)
            S.I("sp", "dma_start", out=cm[:], in_=d["cmat"], W=[cm], dma="cm")
            self.cb = self.sb(g, "cmatb", [128, len(CONST_ORDER) + 1, 128], BF16)
            S.I("dve", "tensor_copy", self.cb[:], cm[:], R=[cm], W=[self.cb])
            self.cm = cm
            self.vec = self.sb(g, "vecs", [128, NV + 4, 8], F32)
            S.I("sp", "dma_start", out=self.vec[:, 0:NV, :], in_=d["vecs"], W=[self.vec], dma="vec")
            self.hvec = self.sb(g, "hvec", [128, 8], F32)
            S.I("sp", "dma_start", out=self.hvec[:], in_=d["hvec"], W=[self.hvec], dma="hvec")
            self.eps_list = [1e-6, 1e-24, 64e-5, 1.0, float(np.log(0.6065306597126334))]
            self.epst = self.sb(g, "epst", [128, 8], F32)
            for i_, v_ in enumerate(self.eps_list):
                S.I("pool", "memset", self.epst[:, i_:i_ + 1], v_, W=[self.epst])
            self.kvstack = None
            for ti in range(self.NT):
                S.I("sp", "dma_start", out=d["yT"][:, ti * TT:(ti + 1) * TT], in_=d["xT"][:, ti * TT:(ti + 1) * TT], W=[self.XD[ti]], dma="xinit")
            for ph in self.phases:
                kind = ph[0]
                if kind == "rwkv":
                    self.phase_rwkv(ph[1])
                elif kind == "mlp":
                    self.phase_mlp(ph[1])
                elif kind == "kv":
                    self.kvstack = g
                    self.phase_kv()
                elif kind == "attn":
                    self.phase_attn(ph[1], ph[2])
                S.barrier()
                S.flush()
            S.barrier()
            S.flush(final=True)
        return nc

    def cmat(self, name):
        i = CONST_ORDER.index(name)
        return self.cb[:, i, :]

    def vcol(self, name, c):
        i = VIDX[name]
        return self.vec[:, i, c:c + 1]

    def rms_rstd(self, x, sq, rstd):
        S = self.S
        S.I("act", "activation", out=sq[:], in_=x[:], func=AF.Square, R=[x], W=[sq])
        pb = self.bank()
        om = self.cmat("onesm")
        for c in range(NC_):
            S.I("pe", "matmul", pb[:, :], lhsT=om, rhs=sq[:, c, :], start=(c == 0), stop=(c == NC_ - 1),
                 R=[sq, self.cb], W=[pb], inc=(c == NC_ - 1))
        self.rsqrt(rstd[:], pb[:, :], 1e-6, rstd, pb)

    def phase_mlp(self, L):
        nc, S, d = self.nc, self.S, self.d
        with ExitStack() as st:
            xs_ = [self.sb(st, f"mx{i}", [128, 8, TT], F32) for i in range(1)]
            sq = self.sb(st, "msq", [128, 8, TT], BF16)
            rstd = self.sb(st, "mrstd", [128, TT], F32)
            xn = [self.sb(st, f"mxn{i}", [128, 8, TT], BF16) for i in range(1)]
            h1 = self.sb(st, "mh1", [128, 32, TT], BF16)
            rl = [self.sb(st, f"mrl{i}", [128, TT], F32) for i in range(2)]
            w1b = [self.sb(st, f"mw1_{i}", [128, 8, 1024], BF16) for i in range(2)]
            w2b = [self.sb(st, f"mw2_{i}", [128, 32, 256], BF16) for i in range(2)]
            w1d = d["mlp_w1"][L].rearrange("(kc p) f -> p kc f", p=128)
            w2d = d["mlp_w2"][L].rearrange("(fc p) c -> p fc c", p=128)
            i1 = i2 = 0
            for ti in range(self.NT):
                t0 = ti * TT
                x = xs_[0]
                S.I("sp", "dma_start", out=x[:], in_=self.xview(d["yT"], t0), R=[self.XD[ti]], W=[x], dma=x.b.name)
                self.rms_rstd(x, sq, rstd)
                xnn = xn[0]
                for c in range(NC_):
                    S.I("dve", "scalar_tensor_tensor", out=xnn[:, c, :], in0=x[:, c, :], scalar=self.vcol(f"ln_mlp{L}", c), in1=rstd[:], op0=ALU.mult, op1=ALU.mult,
                         R=[x, rstd, self.vec], W=[xnn])
                for q in range(4):
                    w = w1b[i1 % 2]
                    i1 += 1
                    self.load_cast(w, w[:], w1d[:, :, q * 1024:(q + 1) * 1024])
                    for fc in range(8):
                        pb = self.bank()
                        for kc in range(8):
                            S.I("pe", "matmul", pb[:, :], lhsT=w[:, kc, fc * 128:(fc + 1) * 128], rhs=xnn[:, kc, :], start=(kc == 0), stop=(kc == 7),
                                 R=[w, xnn], W=[pb], inc=(kc == 7))
                        r = rl[fc % 2]
                        S.I("act", "activation", out=r[:], in_=pb[:, :], func=AF.Relu, R=[pb], W=[r])
                        S.I("dve", "tensor_tensor", out=h1[:, q * 8 + fc, :], in0=r[:], in1=r[:], op=ALU.mult, R=[r], W=[h1])
                for q in range(4):
                    w = w2b[i2 % 2]
                    i2 += 1
                    self.load_cast(w, w[:], w2d[:, :, q * 256:(q + 1) * 256])
                    for cc in range(2):
                        co = q * 2 + cc
                        pb = self.bank()
                        for fc in range(32):
                            S.I("pe", "matmul", pb[:, :], lhsT=w[:, fc, cc * 128:(cc + 1) * 128], rhs=h1[:, fc, :], start=(fc == 0), stop=(fc == 31),
                                 R=[w, h1], W=[pb], inc=(fc == 31))
                        S.I("dve", "tensor_tensor", out=x[:, co, :], in0=pb[:, :], in1=x[:, co, :], op=ALU.add, R=[pb, x], W=[x])
                S.I("sp", "dma_start", out=self.xview(d["yT"], t0), in_=x[:], R=[x], W=[self.XD[ti]], dma="xst")


def prep_shared(inp):
    f = lambda a: np.asarray(a, dtype=np.float32)
    vecs = np.zeros((NV, C), np.float32)
    for i in range(4):
        vecs[VIDX[f"ln_mix{i}"]] = f(inp["ln_mix"])[i]
        vecs[VIDX[f"ln_mlp{i}"]] = f(inp["ln_mlp"])[i]
    for j in range(2):
        for n in range(6):
            vecs[VIDX[f"mu{j}_{n}"]] = f(inp["a_mu"])[j, n]
        vecs[VIDX[f"w0_{j}"]] = f(inp["a_w0"])[j]
        vecs[VIDX[f"a0_{j}"]] = f(inp["a_a0"])[j]
        vecs[VIDX[f"kk_{j}"]] = f(inp["a_k_k"])[j]
        vecs[VIDX[f"ka_{j}"]] = f(inp["a_k_a"])[j]
        vecs[VIDX[f"rk_{j}"]] = f(inp["a_r_k"])[j].reshape(-1)
        vecs[VIDX[f"gng_{j}"]] = f(inp["a_gn_g"])[j]
        vecs[VIDX[f"gnb_{j}"]] = f(inp["a_gn_b"])[j]
    vecs[VIDX["v0"]] = f(inp["a_v0"])[0]
    vecs[VIDX["kvn"]] = f(inp["kv_norm"])
    vecs_l = np.ascontiguousarray(vecs.reshape(NV, 8, 128).transpose(2, 0, 1))
    hvec = np.zeros((128, 8), np.float32)
    hvec[:, 0] = np.tile(f(inp["b_q_gain"])[0], 2)
    hvec[:, 1] = np.tile(f(inp["b_q_gain"])[1], 2)
    hvec[:, 2] = np.tile(f(inp["k_gain"]), 2)
    sk = f(inp["b_sinks"])
    sr = sk.reshape(2, 2, 4, 2).transpose(0, 3, 1, 2)
    sinkrow = np.ascontiguousarray(np.broadcast_to(sr[None, :, :, :, :, None], (1, 2, 2, 2, 4, 128))).astype(np.float32)
    cs = make_consts()
    cmat = np.stack([cs[k][:, 0:128] for k in CONST_ORDER] + [cs["m_abrb"][:, 128:256]], 1)
    sh = {"vecs": vecs_l, "hvec": hvec, "sinkrow": sinkrow, "cmat": np.ascontiguousarray(cmat.astype(np.float32)),
          "alibi": cs["alibi"]}
    for k in ["mlp_w1", "mlp_w2", "a_w_rkv", "a_w1", "a_w2", "a_a1", "a_a2", "a_g1", "a_g2", "a_wo", "a_v1", "a_v2", "w_kv", "b_wq", "b_wo"]:
        sh[k] = np.ascontiguousarray(f(inp[k]))
    return sh


FULL_PHASES = [("rwkv", 0), ("mlp", 0), ("rwkv", 1), ("mlp", 1), ("kv",), ("attn", 0, 2), ("mlp", 2), ("attn", 1, 3), ("mlp", 3)]


def run(inp, phases, T, ncores, dbg=None):
    sh = prep_shared(inp)
    x = np.asarray(inp["x"], dtype=np.float32)
    in_maps = []
    for b in range(ncores):
        m = dict(sh)
        m["xT"] = np.ascontiguousarray(x[b, :T].T)
        in_maps.append(m)
    nc = Builder(T, phases, dbg).build()
    res = run_bass_kernel_spmd(nc, in_maps, core_ids=list(range(ncores)))
    return res


def kernel(**inputs):
    res = run(inputs, FULL_PHASES, 4096, 8)
    out = np.stack([np.ascontiguousarray(r["yT"].T) for r in res.results], 0)
    return out.astype(np.float32)


TR = 256
NJ = 2


def _phase_rwkv(self, L):
    nc, S, d = self.nc, self.S, self.d
    NTR = self.T // TR
    with ExitStack() as st:
        sb = lambda name, shape, dt: self.sb(st, name, shape, dt)
        V = lambda name, c: self.vcol(f"{name}_{L}", c)
        cview = lambda ap: ap.rearrange("(kc p) n -> p kc n", p=128)
        wo = sb("wo", [128, 8, 1024], BF16)
        for i in range(3):
            S.I("pool", "dma_start", out=d["wrkv_bf"][L, i], in_=d["a_w_rkv"][L, i], W=[self.WBF], dma="wcast")
        self.load_cast(wo, wo[:], cview(d["a_wo"][L]))
        wslot = [sb(f"wslot{i}", [128, 3, 8, 128], BF16) for i in range(2)]
        w1 = sb("w1", [128, 8, 64], BF16)
        a1 = sb("a1", [128, 8, 64], BF16)
        g1 = sb("g1", [128, 8, 160], BF16)
        self.load_cast(w1, w1[:], cview(d["a_w1"][L]))
        self.load_cast(a1, a1[:], cview(d["a_a1"][L]))
        self.load_cast(g1, g1[:], cview(d["a_g1"][L]))
        w2 = sb("w2", [64, 1024], BF16)
        a2 = sb("a2", [64, 1024], BF16)
        self.load_cast(w2, w2[:], d["a_w2"][L])
        self.load_cast(a2, a2[:], d["a_a2"][L])
        g2 = sb("g2", [128, 2, 1024], BF16)
        self.load_cast(g2, g2[:, 0, :], d["a_g2"][L][0:128, :])
        self.load_cast(g2, g2[0:32, 1, :], d["a_g2"][L][128:160, :])
        if L == 1:
            v1 = sb("v1", [128, 8, 32], BF16)
            v2 = sb("v2", [32, 1024], BF16)
            self.load_cast(v1, v1[:], cview(d["a_v1"][0]))
            self.load_cast(v2, v2[:], d["a_v2"][0])
        vx = sb("vx", [128, 3, 8], F32)
        for i, nm in enumerate([f"w0_{L}", f"a0_{L}", "v0"]):
            S.I("dve", "tensor_scalar", out=vx[:, i, :], in0=self.vec[:, VIDX[nm], :], scalar1=-1.0, scalar2=None, op0=ALU.mult, R=[self.vec], W=[vx])
        ones = sb("ones", [128, 128], F32)
        S.I("pool", "memset", ones[:], 1.0, W=[ones])
        mask4 = sb("mask4", [128, 4, 128], BF16)
        ia, ib = CONST_ORDER.index("m_abrb"), len(CONST_ORDER)
        for q in range(4):
            S.I("pool", "tensor_copy", mask4[:, q, :], self.cm[:, (ia if q % 2 == 0 else ib), :], R=[self.cm], W=[mask4])
        msl = self.cm[:, CONST_ORDER.index("m_sl"), :]
        bd1f = self.cm[:, CONST_ORDER.index("bd1"), :]
        identb = self.cmat("ident")
        bd1 = self.cmat("bd1")
        bd64 = self.cmat("bd64")
        Mf = [sb(f"Mf{c}", [128, 64], F32) for c in range(8)]
        Mb = [[sb(f"Mb{c}_{p}", [128, 128], BF16) for p in range(2)] for c in range(8)]
        for c in range(8):
            for p in range(2):
                if p == 0:
                    S.I("pool", "memset", Mf[c][:], 0.0, W=[Mf[c]])
                S.I("pool", "memset", Mb[c][p][:], 0.0, W=[Mb[c][p]])
        x = sb("x", [128, 8, TR], F32)
        rstd = sb("rstd", [128, TR], F32)
        h = sb("h", [128, 8, TR + 1], F32)
        S.I("pool", "memset", h[:, :, 0:1], 0.0, W=[h])
        carry = sb("carry", [128, 8, 1], F32)
        xx = sb("xx", [128, 8, TR], BF16)
        xsr, xsk, xsv = [sb(n, [128, 8, TR], BF16) for n in ("xsr", "xsk", "xsv")]
        xso = [sb(f"xso{i}", [128, 8, TR], BF16) for i in range(2)]
        sq = xso[1]
        lw = sb("lw", [64, TR], BF16)
        al = sb("al", [64, TR], BF16)
        vl = sb("vl", [32, TR], BF16)
        sg = sb("sg", [128, 2, TR], BF16)
        sgt = sb("sgt", [128, 2, TR], BF16)
        ygb = sb("ygb", [128, 8, TR], BF16)
        PLt = sb("PLt", [128, 8, NJ], F32)
        NTMP = 14
        tmp = [[sb(f"t{s}_{i}", [128, TR], F32) for i in range(NTMP)] for s in range(1)]
        opb = [[sb(f"o{s}_{i}", [128, TR], BF16) for i in range(2)] for s in range(1)]
        ar = [sb(f"ar{s}", [128, NJ, 2, 128], BF16) for s in range(4)]
        vft = [sb(f"vf{s}", [128, TR], F32) for s in range(2)]
        par3 = [[sb(f"p3_{s}_{i}", [128, TR], F32) for i in range(3)] for s in range(4)]
        opb2 = [[sb(f"ob{s}_{i}", [128, TR], BF16) for i in range(5)] for s in range(4)]
        epf = [sb(f"epf{i}", [128, TR], F32) for i in range(2)]
        epb = [sb(f"epb{i}", [128, TR], BF16) for i in range(2)]
        def wset(s):
            w = {}
            w["AX"] = sb(f"AX{s}", [128, 2, 128], BF16)
            w["BKV"] = sb(f"BKV{s}", [128, 3, 128], BF16)
            w["Vpad"] = sb(f"Vpad{s}", [128, 2, 128], BF16)
            w["AA"] = sb(f"AA{s}", [128, 2, 4, 128], BF16)
            DD = F32R if os.environ.get("RW_FP32R", "1") == "1" else F32
            w["N0"] = sb(f"N0{s}", [128, 2, 128], DD)
            w["Q0"] = sb(f"Q0{s}", [128, 2, 128], DD)
            w["QN"] = [sb(f"QN{s}_{i}", [128, 2, 2, 128], DD) for i in range(2)]
            w["Z"] = [sb(f"Z{s}_{i}", [128, 2, 128], DD) for i in range(2)]
            w["Zfin"] = sb(f"Zfin{s}", [128, 2, 128], BF16)
            w["Ap"] = sb(f"Ap{s}", [128, 128], BF16)
            w["UV"] = sb(f"UV{s}", [128, 128], BF16)
            w["UVpad"] = sb(f"UVpad{s}", [128, 2, 128], BF16)
            w["GT"] = sb(f"GT{s}", [128, 128], BF16)
            w["RT"] = sb(f"RT{s}", [128, 128], BF16)
            for k in ("Vpad", "UVpad"):
                S.I("pool", "memset", w[k][:], 0.0, W=[w[k]])
            return w
        ws = [wset(s) for s in range(4)]
        if os.environ.get("DBG_SBUF"):
            print("rwkv sbuf remaining", nc.sbuf_bytes_remaining)

        def diag(tl, off=0):
            a = tl[:]
            ps_ = a.ap[0][0]
            return bass.AP(a.tensor, a.offset + off, [[ps_, 128], [192, 2], [1, 64]])

        def strided(tl, off, stride):
            a = tl[:]
            ps_ = a.ap[0][0]
            return bass.AP(a.tensor, a.offset + off, [[ps_, 128], [stride, 2], [1, 64]])

        self._evi = 0

        def evac_copy(out_ap, in_ap, R, W):
            self._evi += 1
            if self._evi % 2:
                S.I("act", "copy", out_ap, in_ap, R=R, W=W)
            else:
                S.I("dve", "tensor_copy", out_ap, in_ap, R=R, W=W)

        def proj(pb, col, wt, xs, co, n=128, kparts=128):
            for kc in range(8):
                S.I("pe", "matmul", pb[0:n, col:col + TR], lhsT=wt[:, kc, co:co + n], rhs=xs[:, kc, :], start=(kc == 0), stop=(kc == 7),
                     R=[wt, xs], W=[pb], inc=(kc == 7))

        def projs(pb, col, slot, n_, xs):
            for kc in range(8):
                S.I("pe", "matmul", pb[:, col:col + TR], lhsT=slot[:, n_, kc, :], rhs=xs[:, kc, :], start=(kc == 0), stop=(kc == 7),
                    R=[slot, xs], W=[pb], inc=(kc == 7))

        def mix(dst, n):
            for c in range(8):
                S.I("dve", "scalar_tensor_tensor", out=dst[:, c, :], in0=xx[:, c, :], scalar=self.vcol(f"mu{L}_{n}", c), in1=h[:, c, 1:TR + 1], op0=ALU.mult, op1=ALU.add,
                     R=[xx, h, self.vec], W=[dst])

        cj = 0
        for ti in range(NTR):
            t0 = ti * TR
            xd = self.XD[t0 // TT]
            S.I("sp", "dma_start", out=x[:], in_=self.xview(d["yT"], t0, TR), R=[xd], W=[x], dma="rx")
            self.rms_rstd_n(x, sq, rstd, TR)
            if ti > 0:
                S.I("pool", "tensor_copy", h[:, :, 0:1], carry[:], R=[carry], W=[h])
            for c in range(8):
                S.I("dve", "scalar_tensor_tensor", out=h[:, c, 1:TR + 1], in0=x[:, c, :], scalar=self.vcol(f"ln_mix{L}", c), in1=rstd[:, 0:TR], op0=ALU.mult, op1=ALU.mult,
                     R=[x, rstd, self.vec], W=[h])
            S.I("pool", "tensor_copy", carry[:], h[:, :, TR:TR + 1], R=[h], W=[carry])
            S.I("dve", "tensor_tensor", out=xx[:], in0=h[:, :, 0:TR], in1=h[:, :, 1:TR + 1], op=ALU.subtract, R=[h], W=[xx])
            mix(xsr, 0)
            mix(xsk, 1)
            mix(xsv, 2)
            xw = xso[0]
            mix(xw, 3)
            pb = self.bank()
            proj(pb, 0, w1, xw, 0, n=64)
            S.I("act", "activation", out=lw[:], in_=pb[0:64, 0:TR], func=AF.Tanh, R=[pb], W=[lw])
            xa = xso[1]
            mix(xa, 4)
            proj(pb, 256, a1, xa, 0, n=64)
            S.I("act", "copy", al[:], pb[0:64, 256:256 + TR], R=[pb], W=[al])
            def g_stage():
                xg = xso[0]
                mix(xg, 5)
                yield
                pb = self.bank()
                proj(pb, 0, g1, xg, 0, n=128)
                proj(pb, 256, g1, xg, 128, n=32)
                S.I("act", "activation", out=sgt[:, 0, :], in_=pb[:, 0:TR], func=AF.Exp, scale=-1.0, R=[pb], W=[sgt])
                S.I("act", "activation", out=sgt[0:32, 1, :], in_=pb[0:32, 256:256 + TR], func=AF.Exp, scale=-1.0, R=[pb], W=[sgt])
                yield
                S.I("act", "activation", out=sgt[:, 0, :], in_=sgt[:, 0, :], func=AF.Ln, bias=self.epsc(1.0), R=[sgt, self.epst], W=[sgt])
                S.I("act", "activation", out=sgt[0:32, 1, :], in_=sgt[0:32, 1, :], func=AF.Ln, bias=self.epst[0:32, 3:4], R=[sgt, self.epst], W=[sgt])
                yield
                S.I("act", "activation", out=sg[:, 0, :], in_=sgt[:, 0, :], func=AF.Exp, scale=-1.0, R=[sgt], W=[sg])
                S.I("act", "activation", out=sg[0:32, 1, :], in_=sgt[0:32, 1, :], func=AF.Exp, scale=-1.0, R=[sgt], W=[sg])
            if L == 1:
                pbv = self.bank()
                proj(pbv, 0, v1, xsv, 0, n=32)
                S.I("act", "copy", vl[:], pbv[0:32, 0:TR], R=[pbv], W=[vl])

            def make_c(c, s, holder):
                T_ = tmp[0]
                r_sb, k_sb, logw, a_sb, kkr, rn, kk, kmod, bvec, lp, E1, Em, Ep, Eh = T_
                v_sb, bonus, ysb = par3[s]
                bt, kt, bh, kh, vbf = opb2[s]
                prod, kq = opb[0][0], opb[0][1]
                arr = ar[s]
                cs = slice(c * 128, (c + 1) * 128)
                slot = wslot[s % 2]
                wsrc = lambda n_: d["wrkv_bf"][L, n_].rearrange("(kc p) m -> p kc m", p=128)[:, :, c * 128:(c + 1) * 128]
                for n_ in range(3):
                    S.I("sp", "dma_start", out=slot[:, n_, :, :], in_=wsrc(n_), R=[self.WBF], W=[slot], dma=slot.b.name)
                pb1 = self.bank()
                projs(pb1, 0, slot, 0, xsr)
                projs(pb1, 256, slot, 1, xsk)
                pb2 = self.bank()
                projs(pb2, 0, slot, 2, xsv)
                S.I("pe", "matmul", pb2[:, 256:256 + TR], lhsT=w2[:, cs], rhs=lw[:], start=True, stop=True, R=[w2, lw], W=[pb2])
                pb3 = self.bank()
                S.I("pe", "matmul", pb3[:, 0:TR], lhsT=a2[:, cs], rhs=al[:], start=True, stop=True, R=[a2, al], W=[pb3], inc=(L == 0))
                if L == 1:
                    S.I("pe", "matmul", pb3[:, 256:256 + TR], lhsT=v2[:, cs], rhs=vl[:], start=True, stop=True, R=[v2, vl], W=[pb3])
                S.I("act", "copy", r_sb[:], pb1[:, 0:TR], R=[pb1], W=[r_sb])
                S.I("act", "copy", k_sb[:], pb1[:, 256:256 + TR], R=[pb1], W=[k_sb])
                S.I("act", "copy", v_sb[:], pb2[:, 0:TR], R=[pb2], W=[v_sb])
                vrows = d["vfirst"][c * 128:(c + 1) * 128, t0:t0 + TR]
                if L == 0:
                    S.I("sp", "dma_start", out=vrows, in_=v_sb[:], R=[v_sb], W=[self.VFD], dma="vfst")
                else:
                    vf = vft[s % 2]
                    S.I("sp", "dma_start", out=vf[:], in_=vrows, R=[self.VFD], W=[vf], dma=vf.b.name)
                    S.I("act", "activation", out=E1[:], in_=pb3[:, 256:256 + TR], func=AF.Exp, bias=vx[:, 2, c:c + 1], scale=-1.0, R=[pb3, vx], W=[E1])
                    S.I("act", "activation", out=E1[:], in_=E1[:], func=AF.Ln, bias=self.epsc(1.0), R=[E1, self.epst], W=[E1])
                    S.I("act", "activation", out=E1[:], in_=E1[:], func=AF.Exp, scale=-1.0, R=[E1], W=[E1])
                    S.I("dve", "tensor_tensor", out=vf[:], in0=vf[:], in1=v_sb[:], op=ALU.subtract, R=[vf, v_sb], W=[vf])
                    S.I("dve", "tensor_tensor", out=vf[:], in0=vf[:], in1=E1[:], op=ALU.mult, R=[vf, E1], W=[vf])
                    S.I("dve", "tensor_tensor", out=v_sb[:], in0=v_sb[:], in1=vf[:], op=ALU.add, R=[vf, v_sb], W=[v_sb])
                S.I("act", "activation", out=logw[:], in_=pb2[:, 256:256 + TR], func=AF.Exp, bias=vx[:, 0, c:c + 1], scale=-1.0, R=[pb2, vx], W=[logw])
                S.I("act", "activation", out=logw[:], in_=logw[:], func=AF.Ln, bias=self.epsc(1.0), R=[logw, self.epst], W=[logw])
                S.I("act", "activation", out=logw[:], in_=logw[:], func=AF.Exp, bias=self.epsc(self.eps_list[4]), scale=-1.0, R=[logw, self.epst], W=[logw])
                S.I("act", "activation", out=a_sb[:], in_=pb3[:, 0:TR], func=AF.Exp, bias=vx[:, 1, c:c + 1], scale=-1.0, R=[pb3, vx], W=[a_sb])
                yield
                S.I("act", "activation", out=a_sb[:], in_=a_sb[:], func=AF.Ln, bias=self.epsc(1.0), R=[a_sb, self.epst], W=[a_sb])
                S.I("act", "activation", out=a_sb[:], in_=a_sb[:], func=AF.Exp, scale=-1.0, R=[a_sb], W=[a_sb])
                yield
                S.I("dve", "tensor_scalar", out=kkr[:], in0=k_sb[:], scalar1=V("kk", c), scalar2=None, op0=ALU.mult, R=[k_sb, self.vec], W=[kkr])
                S.I("act", "activation", out=kq[:], in_=kkr[:], func=AF.Square, R=[kkr], W=[kq])
                pbs = self.bank()
                S.I("pe", "matmul", pbs[:, 0:TR], lhsT=bd1, rhs=kq[:], start=True, stop=True, R=[kq, self.cb], W=[pbs])
                self.rsqrt(rn[:], pbs[:, 0:TR], 1e-24, rn, pbs)
                S.I("pool", "tensor_tensor", out=kk[:], in0=kkr[:], in1=rn[:], op=ALU.mult, R=[kkr, rn], W=[kk])
                yield
                S.I("dve", "tensor_scalar", out=kmod[:], in0=a_sb[:], scalar1=-1.0, scalar2=V("ka", c), op0=ALU.add, op1=ALU.mult, R=[a_sb, self.vec], W=[kmod])
                S.I("dve", "scalar_tensor_tensor", out=kmod[:], in0=kmod[:], scalar=1.0, in1=k_sb[:], op0=ALU.add, op1=ALU.mult, R=[kmod, k_sb], W=[kmod])
                S.I("pool", "tensor_tensor", out=bvec[:], in0=kk[:], in1=a_sb[:], op=ALU.mult, R=[kk, a_sb], W=[bvec])
                yield
                for j in range(NJ):
                    js = slice(j * 128, (j + 1) * 128)
                    S.I("dve", "tensor_tensor_scan", out=lp[:, js], data0=ones[:], data1=logw[:, js], initial=0.0, op0=ALU.mult, op1=ALU.subtract, R=[logw, ones], W=[lp])
                S.I("pool", "tensor_tensor", out=Ep[:], in0=lp[:], in1=logw[:], op=ALU.add, R=[lp, logw], W=[Ep])
                S.I("act", "activation", out=Ep[:], in_=Ep[:], func=AF.Exp, R=[Ep], W=[Ep])
                yield
                S.I("act", "activation", out=E1[:], in_=lp[:], func=AF.Exp, R=[lp], W=[E1])
                S.I("act", "activation", out=Em[:], in_=lp[:], func=AF.Exp, scale=-1.0, R=[lp], W=[Em])
                yield
                for j in range(NJ):
                    js = slice(j * 128, (j + 1) * 128)
                    S.I("act", "activation", out=Eh[:, js], in_=lp[:, js], func=AF.Exp, bias=lp[:, j * 128 + 127:j * 128 + 128], scale=-1.0, R=[lp], W=[Eh])
                    S.I("act", "activation", out=PLt[:, c, j:j + 1], in_=lp[:, j * 128 + 127:j * 128 + 128], func=AF.Exp, R=[lp], W=[PLt])
                yield
                tv = lambda t: t[:].rearrange("p (j t) -> p j t", j=NJ)
                S.I("pool", "tensor_tensor", out=arr[:, :, 1, :], in0=tv(r_sb), in1=tv(E1), op=ALU.mult, R=[r_sb, E1], W=[arr])
                S.I("dve", "scalar_tensor_tensor", out=arr[:, :, 0, :], in0=tv(kk), scalar=-1.0, in1=tv(Ep), op0=ALU.mult, op1=ALU.mult, R=[kk, Ep], W=[arr])
                yield
                S.I("pool", "tensor_tensor", out=bt[:], in0=bvec[:], in1=Em[:], op=ALU.mult, R=[bvec, Em], W=[bt])
                S.I("dve", "tensor_tensor", out=kt[:], in0=kmod[:], in1=Em[:], op=ALU.mult, R=[kmod, Em], W=[kt])
                yield
                S.I("pool", "tensor_tensor", out=bh[:], in0=bvec[:], in1=Eh[:], op=ALU.mult, R=[bvec, Eh], W=[bh])
                S.I("dve", "tensor_tensor", out=kh[:], in0=kmod[:], in1=Eh[:], op=ALU.mult, R=[kmod, Eh], W=[kh])
                yield
                S.I("pool", "tensor_copy", vbf[:], v_sb[:], R=[v_sb], W=[vbf])
                S.I("dve", "scalar_tensor_tensor", out=prod[:], in0=r_sb[:], scalar=V("rk", c), in1=kmod[:], op0=ALU.mult, op1=ALU.mult, R=[r_sb, kmod, self.vec], W=[prod])
                pbs2 = self.bank()
                S.I("pe", "matmul", pbs2[:, 0:TR], lhsT=bd1, rhs=prod[:], start=True, stop=True, R=[prod, self.cb], W=[pbs2])
                S.I("act", "copy", bonus[:], pbs2[:, 0:TR], R=[pbs2], W=[bonus])
                def stream(j, w):
                    js = slice(j * 128, (j + 1) * 128)
                    AX, BKV, Vpad, AA, N0, QN, Z, Ap, UV, UVpad, GT, RT = [w[k] for k in ("AX", "BKV", "Vpad", "AA", "N0", "QN", "Z", "Ap", "UV", "UVpad", "GT", "RT")]
                    Q0, Zfin = w["Q0"], w["Zfin"]
                    identf = self.cm[:, CONST_ORDER.index("ident"), :]
                    par = (ti * NJ + j) % 2
                    Mfc = Mf[c]
                    Mb0, Mb1 = Mb[c][par], Mb[c][1 - par]
                    pt = self.bank()
                    ptb = pt[:].bitcast(BF16)
                    srcs = [(arr, arr[:, j, 0, :]), (bh, bh[:, js]), (kh, kh[:, js]), (vbf, vbf[:, js])]
                    for i, (tl, ap) in enumerate(srcs):
                        S.I("pe", "transpose", ptb[:, i * 128:(i + 1) * 128], ap, identb, R=[tl, self.cb], W=[pt], inc=(i == 3))
                    if True:
                        S.I("dve", "tensor_copy", AX[:, :, 0:64], ptb[:, 0:128].rearrange("p (e k) -> p e k", e=2), R=[pt], W=[AX])
                    S.I("dve", "tensor_copy", BKV[:].rearrange("p a b -> p (a b)"), ptb[:, 128:512], R=[pt], W=[BKV])
                    if True:
                        S.I("dve", "tensor_copy", diag(Vpad), ptb[:, 384:512].rearrange("p (e k) -> p e k", e=2), R=[pt], W=[Vpad])
                    yield
                    Eb = [self.bank(), self.bank()]
                    Nb = [self.bank(), self.bank()]
                    for e_ in range(2):
                        hs = slice(64 * e_, 64 * e_ + 64)
                        arj = arr[hs, j, :, :].rearrange("p a b -> p (a b)")
                        S.I("pe", "matmul", Eb[e_][:, 0:256], lhsT=bt[hs, js], rhs=arj, start=True, stop=True, R=[bt, arr], W=[Eb[e_]], inc=False)
                        S.I("pe", "matmul", Eb[e_][:, 256:512], lhsT=kt[hs, js], rhs=arj, start=True, stop=True, R=[kt, arr], W=[Eb[e_]])
                        S.I("pe", "matmul", Nb[e_][:, 0:128], lhsT=arr[hs, j, 0, :], rhs=bt[hs, js], start=True, stop=True, R=[bt, arr], W=[Nb[e_]])
                        if True:
                            S.I("dve", "tensor_tensor", out=AA[:, e_, :, :], in0=Eb[e_][:, :].rearrange("p (a b) -> p a b", a=4), in1=mask4[:], op=ALU.mult, R=[Eb[e_], mask4], W=[AA])
                        if True:
                            S.I("dve", "tensor_tensor", out=N0[:, e_, :], in0=Nb[e_][:, 0:128], in1=msl, op=ALU.mult, R=[Nb[e_], self.cm], W=[N0])
                        if True:
                            S.I("dve", "tensor_tensor", out=Q0[:, e_, :], in0=Eb[e_][:, 0:128], in1=mask4[:, 0, :], op=ALU.mult, R=[Eb[e_], mask4], W=[Q0])
                            S.I("pool", "tensor_tensor", out=Z[0][:, e_, :], in0=Q0[:, e_, :], in1=identf, op=ALU.add, R=[Q0, self.cm], W=[Z[0]])
                    yield
                    for lv in range(1, 7):
                        qn_prev = QN[(lv - 1) % 2]
                        qn = QN[lv % 2]
                        B = self.bank()
                        for e_ in range(2):
                            Qp = Q0[:, e_, :] if lv == 1 else qn_prev[:, e_, 0, :]
                            Np = N0[:, e_, :] if lv == 1 else qn_prev[:, e_, 1, :]
                            Rr = [Q0, N0] if lv == 1 else [qn_prev]
                            if lv < 6:
                                S.I("pe", "matmul", B[:, e_ * 256:e_ * 256 + 128], lhsT=Np, rhs=Qp, start=True, stop=True, R=Rr, W=[B], inc=False)
                            S.I("pe", "matmul", B[:, e_ * 256 + 128:e_ * 256 + 256], lhsT=Qp, rhs=Np, start=True, stop=True, R=Rr, W=[B], inc=(e_ == 1))
                        if lv < 6:
                            evac_copy(qn[:].rearrange("p a b c -> p (a b c)"), B[:, :], [B], [qn])
                        else:
                            evac_copy(qn[:, :, 1, :], B[:, :].rearrange("p (a b c) -> p a b c", a=2, b=2)[:, :, 1, :], [B], [qn])
                        yield
                        Zb = self.bank()
                        zp, zn = Z[(lv - 1) % 2], (Z[lv % 2] if lv < 6 else Zfin)
                        for e_ in range(2):
                            S.I("pe", "matmul", Zb[:, e_ * 128:(e_ + 1) * 128], lhsT=qn[:, e_, 1, :], rhs=zp[:, e_, :], start=True, stop=True, R=[qn, zp], W=[Zb], inc=(e_ == 1))
                        S.I("dve", "tensor_tensor", out=zn[:].rearrange("p a b -> p (a b)"), in0=Zb[:, 0:256], in1=zp[:].rearrange("p a b -> p (a b)"), op=ALU.add, R=[Zb, zp], W=[zn])
                    Zf = Zfin
                    yield
                    Xb = self.bank()
                    for e_ in range(2):
                        S.I("pe", "matmul", Xb[:, e_ * 128:(e_ + 1) * 128], lhsT=AA[:, e_, 2, :], rhs=BKV[:, 2, :], start=True, stop=True, R=[AA, BKV], W=[Xb], inc=(e_ == 1))
                    S.I("dve", "tensor_copy", AX[:, :, 64:128], diag(Xb), R=[Xb], W=[AX])
                    yield
                    Ub = self.bank()
                    for e_ in range(2):
                        S.I("pe", "matmul", Ub[:, e_ * 128:(e_ + 1) * 128], lhsT=Zf[:, e_, :], rhs=AX[:, e_, :], start=True, stop=True, R=[Zf, AX], W=[Ub], inc=(e_ == 1))
                    S.I("dve", "tensor_copy", Ap[:].rearrange("p (e k) -> p e k", e=2), strided(Ub, 0, 128), R=[Ub], W=[Ap])
                    S.I("dve", "tensor_copy", UV[:].rearrange("p (e k) -> p e k", e=2), strided(Ub, 64, 128), R=[Ub], W=[UV])
                    S.I("dve", "tensor_copy", diag(UVpad), strided(Ub, 64, 128), R=[Ub], W=[UVpad])
                    yield
                    Gb = self.bank()
                    S.I("pe", "matmul", Gb[:, 0:128], lhsT=Ap[:], rhs=BKV[:, 0, :], start=True, stop=True, R=[Ap, BKV], W=[Gb])
                    S.I("dve", "tensor_tensor", out=GT[:], in0=Gb[:, 0:128], in1=bd1f, op=ALU.mult, R=[Gb, self.cm], W=[GT])
                    Rb = self.bank()
                    for e_ in range(2):
                        S.I("pe", "matmul", Rb[:, e_ * 128:(e_ + 1) * 128], lhsT=Ap[:], rhs=AA[:, e_, 1, :], start=True, stop=True, R=[Ap, AA], W=[Rb], inc=(e_ == 1))
                    for e_ in range(2):
                        hs = slice(64 * e_, 64 * e_ + 64)
                        S.I("dve", "tensor_tensor", out=RT[hs, :], in0=Rb[hs, e_ * 128:(e_ + 1) * 128], in1=arr[hs, j, 1, :], op=ALU.add, R=[Rb, arr], W=[RT])
                    yield
                    Yb = self.bank()
                    ymm = [(UVpad[:, 0, :], AA[:, 0, 1, :]), (UVpad[:, 1, :], AA[:, 1, 1, :]), (Vpad[:, 0, :], AA[:, 0, 3, :]), (Vpad[:, 1, :], AA[:, 1, 3, :]), (Mb0[:], RT[:])]
                    for i, (l_, r_) in enumerate(ymm):
                        S.I("pe", "matmul", Yb[:, 0:128], lhsT=l_, rhs=r_, start=(i == 0), stop=(i == 4), R=[UVpad, Vpad, AA, Mb0, RT], W=[Yb], inc=(i == 4))
                    S.I("act", "copy", ysb[:, js], Yb[:, 0:128], R=[Yb], W=[ysb])
                    Mbk = self.bank()
                    cmm = [(BKV[:, 0, :], UV[:]), (BKV[:, 1, :], BKV[:, 2, :]), (GT[:], Mb0[:])]
                    for i, (l_, r_) in enumerate(cmm):
                        S.I("pe", "matmul", Mbk[:, 0:128], lhsT=l_, rhs=r_, start=(i == 0), stop=(i == 2), R=[BKV, UV, GT, Mb0], W=[Mbk], inc=(i == 2))
                    for e_ in range(2):
                        hs = slice(64 * e_, 64 * e_ + 64)
                        S.I("dve", "scalar_tensor_tensor", out=Mfc[hs, :], in0=Mfc[hs, :], scalar=PLt[hs, c, j:j + 1], in1=Mbk[hs, hs], op0=ALU.mult, op1=ALU.add, R=[Mbk, Mfc, PLt], W=[Mfc])
                        S.I("pool", "tensor_copy", Mb1[hs, hs], Mfc[hs, :], R=[Mfc], W=[Mb1])
                def epilogue():
                    ybf, yc, ysq, rs = epb[0], epf[0], epb[1], epf[1]
                    S.I("pool", "tensor_copy", ybf[:], ysb[:], R=[ysb], W=[ybf])
                    pg = self.bank()
                    S.I("pe", "matmul", pg[:, 0:TR], lhsT=bd64, rhs=ybf[:], start=True, stop=True, R=[ybf, self.cb], W=[pg])
                    S.I("dve", "tensor_tensor", out=yc[:], in0=ysb[:], in1=pg[:, 0:TR], op=ALU.subtract, R=[pg, ysb], W=[yc])
                    S.I("act", "activation", out=ysq[:], in_=yc[:], func=AF.Square, R=[yc], W=[ysq])
                    S.I("pe", "matmul", pg[:, 256:256 + TR], lhsT=bd64, rhs=ysq[:], start=True, stop=True, R=[ysq, self.cb], W=[pg])
                    self.rsqrt(rs[:], pg[:, 256:256 + TR], 64e-5, rs, pg)
                    yield
                    S.I("pool", "tensor_tensor", out=yc[:], in0=yc[:], in1=rs[:], op=ALU.mult, R=[yc, rs], W=[yc])
                    S.I("act", "activation", out=yc[:], in_=yc[:], func=AF.Identity, bias=V("gnb", c), scale=V("gng", c), R=[yc, self.vec], W=[yc])
                    yield
                    S.I("pool", "tensor_tensor", out=bonus[:], in0=bonus[:], in1=v_sb[:], op=ALU.mult, R=[bonus, v_sb], W=[bonus])
                    S.I("dve", "tensor_tensor", out=yc[:], in0=yc[:], in1=bonus[:], op=ALU.add, R=[yc, bonus], W=[yc])
                    yield
                    pgg = self.bank()
                    S.I("pe", "matmul", pgg[:, 0:TR], lhsT=g2[:, 0, cs], rhs=sg[:, 0, :], start=True, stop=False, R=[g2, sg], W=[pgg], inc=False)
                    S.I("pe", "matmul", pgg[:, 0:TR], lhsT=g2[0:32, 1, cs], rhs=sg[0:32, 1, :], start=False, stop=True, R=[g2, sg], W=[pgg])
                    S.I("dve", "tensor_tensor", out=ygb[:, c, :], in0=yc[:], in1=pgg[:, 0:TR], op=ALU.mult, R=[pgg, yc], W=[ygb])
                holder["stream"] = stream
                holder["epilogue"] = epilogue

            def rr(gens):
                gens = list(gens)
                while gens:
                    for g_ in list(gens):
                        try:
                            next(g_)
                        except StopIteration:
                            gens.remove(g_)

            def delayed(g_, n_):
                for _ in range(n_):
                    yield
                yield from g_

            def chain(*gs):
                for g_ in gs:
                    yield from g_

            holders = {}
            ready = set()
            done = {cp: 0 for cp in range(4)}

            def pro(cp):
                hs = [{}, {}]
                holders[cp] = hs
                yield from make_c(2 * cp, (cp % 2) * 2, hs[0])
                yield from make_c(2 * cp + 1, (cp % 2) * 2 + 1, hs[1])
                ready.add(cp)

            def aux_gen():
                yield from pro(0)
                yield from g_stage()
                yield from pro(1)
                for p_ in range(4):
                    while done[p_] < 4:
                        yield
                    yield from holders[p_][0]["epilogue"]()
                    yield from holders[p_][1]["epilogue"]()
                    if p_ + 2 < 4:
                        yield from pro(p_ + 2)

            SD = int(os.environ.get("RW_STAGGER", "4"))
            pend = []
            for cp in range(4):
                pend += [(cp, 0, 0), (cp, 1, 0), (cp, 0, 1), (cp, 1, 1)]
            slots = [None] * 4
            aux = aux_gen()
            aux_alive = True
            cycle = 0
            last_start = -SD
            while pend or any(x is not None for x in slots) or aux_alive:
                if pend and cycle - last_start >= SD and pend[0][0] in ready:
                    for k_ in range(4):
                        if slots[k_] is None:
                            cp, hi, j_ = pend.pop(0)
                            slots[k_] = (holders[cp][hi]["stream"](j_, ws[k_]), cp)
                            last_start = cycle
                            break
                for k_ in range(4):
                    if slots[k_] is not None:
                        try:
                            next(slots[k_][0])
                        except StopIteration:
                            done[slots[k_][1]] += 1
                            slots[k_] = None
                if aux_alive:
                    try:
                        next(aux)
                    except StopIteration:
                        aux_alive = False
                cycle += 1
                assert cycle < 100000
            for co in range(8):
                pb = self.bank() if co % 2 == 0 else pb
                col = (co % 2) * 256
                for kc in range(8):
                    S.I("pe", "matmul", pb[:, col:col + TR], lhsT=wo[:, kc, co * 128:(co + 1) * 128], rhs=ygb[:, kc, :], start=(kc == 0), stop=(kc == 7),
                         R=[wo, ygb], W=[pb], inc=(kc == 7))
                S.I("dve", "tensor_tensor", out=x[:, co, :], in0=pb[:, col:col + TR], in1=x[:, co, :], op=ALU.add, R=[pb, x], W=[x])
            S.I("sp", "dma_start", out=self.xview(d["yT"], t0, TR), in_=x[:], R=[x], W=[xd], dma="xst")


def _rms_rstd_n(self, x, sq, rstd, n):
    S = self.S
    S.I("act", "activation", out=sq[:], in_=x[:], func=AF.Square, R=[x], W=[sq])
    pb = self.bank()
    om = self.cmat("onesm")
    for c in range(NC_):
        S.I("pe", "matmul", pb[:, 0:n], lhsT=om, rhs=sq[:, c, :], start=(c == 0), stop=(c == NC_ - 1),
             R=[sq, self.cb], W=[pb], inc=(c == NC_ - 1))
    self.rsqrt(rstd[:, 0:n], pb[:, 0:n], 1e-6, rstd, pb)


def _rsqrt(self, out_ap, in_ap, eps, out_tl, in_tl):
    S = self.S
    S.I("act", "activation", out=out_ap, in_=in_ap, func=AF.Ln, bias=self.epsc(eps), R=[in_tl, self.epst], W=[out_tl])
    S.I("act", "activation", out=out_ap, in_=out_ap, func=AF.Exp, scale=-0.5, R=[out_tl], W=[out_tl])


def _epsc(self, eps):
    return self.epst[:, self.eps_list.index(eps):self.eps_list.index(eps) + 1]


Builder.rsqrt = _rsqrt
Builder.epsc = _epsc
Builder.phase_rwkv = _phase_rwkv
Builder.rms_rstd_n = _rms_rstd_n


def _phase_kv(self):
    nc, S, d = self.nc, self.S, self.d
    g = self.kvstack
    T = self.T
    self.kTp = self.sb(g, "kTp", [128, 2, 2, T], BF16)
    self.Vt = self.sb(g, "Vt", [128, T // 128, 256], BF16)
    S.I("pool", "memset", self.kTp[:].rearrange("p a b t -> p (a b t)"), 0.0, W=[self.kTp])
    with ExitStack() as st:
        sb = lambda name, shape, dt: self.sb(st, name, shape, dt)
        wkv = sb("wkv", [128, 8, 256], BF16)
        self.load_cast(wkv, wkv[:], d["w_kv"].rearrange("(kc p) n -> p kc n", p=128))
        wkd = sb("wkd", [128, 2, 8, 128], BF16)
        wvd = sb("wvd", [128, 8, 256], BF16)
        for g_ in range(2):
            for dup in range(2):
                S.I("pool", "tensor_copy", wkd[:, g_, :, dup * 64:(dup + 1) * 64], wkv[:, :, g_ * 64:(g_ + 1) * 64], R=[wkv], W=[wkd])
                S.I("pool", "tensor_copy", wvd[:, :, g_ * 128 + dup * 64:g_ * 128 + (dup + 1) * 64], wkv[:, :, 128 + g_ * 64:128 + (g_ + 1) * 64], R=[wkv], W=[wvd])
        x = sb("x", [128, 8, TT], F32)
        sq = sb("sq", [128, 8, TT], BF16)
        rstd = sb("rstd", [128, TT], F32)
        xn = sb("xn", [128, 8, TT], BF16)
        k_sb = sb("k_sb", [128, TT], F32)
        kq = sb("kq", [128, TT], BF16)
        rk = sb("rk", [128, TT], F32)
        bd64 = self.cmat("bd64")
        for ti in range(self.NT):
            t0 = ti * TT
            S.I("sp", "dma_start", out=x[:], in_=self.xview(d["yT"], t0), R=[self.XD[ti]], W=[x], dma="kvx")
            self.rms_rstd_n(x, sq, rstd, TT)
            for c in range(8):
                S.I("dve", "scalar_tensor_tensor", out=xn[:, c, :], in0=x[:, c, :], scalar=self.vcol("kvn", c), in1=rstd[:], op0=ALU.mult, op1=ALU.mult, R=[x, rstd, self.vec], W=[xn])
            for g_ in range(2):
                pb = self.bank()
                for kc in range(8):
                    S.I("pe", "matmul", pb[:, :], lhsT=wkd[:, g_, kc, :], rhs=xn[:, kc, :], start=(kc == 0), stop=(kc == 7), R=[wkd, xn], W=[pb], inc=(kc == 7))
                S.I("act", "copy", k_sb[:], pb[:, :], R=[pb], W=[k_sb])
                S.I("act", "activation", out=kq[:], in_=k_sb[:], func=AF.Square, R=[k_sb], W=[kq])
                pb2 = self.bank()
                S.I("pe", "matmul", pb2[:, :], lhsT=bd64, rhs=kq[:], start=True, stop=True, R=[kq, self.cb], W=[pb2])
                self.rsqrt(rk[:], pb2[:, :], 1e-6, rk, pb2)
                S.I("pool", "tensor_tensor", out=k_sb[:], in0=k_sb[:], in1=rk[:], op=ALU.mult, R=[rk, k_sb], W=[k_sb])
                for e_ in range(2):
                    hs = slice(64 * e_, 64 * e_ + 64)
                    S.I("dve", "tensor_scalar", out=self.kTp[hs, e_, g_, t0:t0 + TT], in0=k_sb[hs, :], scalar1=self.hvec[hs, 2:3], scalar2=None, op0=ALU.mult, R=[k_sb, self.hvec], W=[self.kTp])
            for blk in range(4):
                pb = self.bank()
                for kc in range(8):
                    S.I("pe", "matmul", pb[:, 0:256], lhsT=xn[:, kc, blk * 128:(blk + 1) * 128], rhs=wvd[:, kc, :], start=(kc == 0), stop=(kc == 7), R=[wvd, xn], W=[pb], inc=(kc == 7))
                S.I("act", "copy", self.Vt[:, ti * 4 + blk, :], pb[:, 0:256], R=[pb], W=[self.Vt])


def _phase_attn(self, j, i):
    nc, S, d = self.nc, self.S, self.d
    with ExitStack() as st:
        sb = lambda name, shape, dt: self.sb(st, name, shape, dt)
        cview = lambda ap: ap.rearrange("(kc p) n -> p kc n", p=128)
        wq = sb("wq", [128, 8, 1024], BF16)
        wo = sb("wo", [128, 8, 1024], BF16)
        self.load_cast(wq, wq[:], cview(d["b_wq"][j]))
        self.load_cast(wo, wo[:], cview(d["b_wo"][j]))
        alibi = sb("alibi", [128, 2, 2, 8, 128], F32)
        S.I("sp", "dma_start", out=alibi[:], in_=d["alibi"], W=[alibi], dma="alibi")
        sk = sb("sk", [1, 2, 2, 512], F32)
        skb = sb("skb", [1, 2, 2, 512], BF16)
        S.I("sp", "dma_start", out=sk[:], in_=d["sinkrow"][:, j].rearrange("o e g h q -> o e g (h q)"), W=[sk], dma="sk")
        S.I("act", "activation", out=skb[:].rearrange("o e g n -> o (e g n)"), in_=sk[:].rearrange("o e g n -> o (e g n)"), func=AF.Exp, R=[sk], W=[skb])
        qg = sb("qg", [128, 1], F32)
        S.I("dve", "tensor_scalar", out=qg[:], in0=self.hvec[:, j:j + 1], scalar1=0.125, scalar2=None, op0=ALU.mult, R=[self.hvec], W=[qg])
        x = sb("x", [128, 8, TT], F32)
        rstd = sb("rstd", [128, TT], F32)
        h = sb("h", [128, 8, TT], BF16)
        q_all = sb("q_all", [128, 8, TT], F32)
        qq_all = sb("qq_all", [128, 8, TT], BF16)
        sq = qq_all
        rqs = [sb(f"rq{k}", [128, TT], F32) for k in range(2)]
        qT = sb("qT", [128, 8, TT], BF16)
        attT = sb("attT", [128, 8, TT], BF16)
        sc = [sb(f"sc{k}", [128, 512], F32) for k in range(4)]
        P = [sb(f"P{k}", [128, 512], BF16) for k in range(4)]
        rec = [sb(f"rec{k}", [128, 512], F32) for k in range(2)]
        bd64 = self.cmat("bd64")
        ones1 = self.cmat("ones1")
        for ti in range(self.NT):
            t0 = ti * TT
            S.I("sp", "dma_start", out=x[:], in_=self.xview(d["yT"], t0), R=[self.XD[ti]], W=[x], dma="atx")
            self.rms_rstd_n(x, sq, rstd, TT)
            for c in range(8):
                S.I("dve", "scalar_tensor_tensor", out=h[:, c, :], in0=x[:, c, :], scalar=self.vcol(f"ln_mix{i}", c), in1=rstd[:], op0=ALU.mult, op1=ALU.mult, R=[x, rstd, self.vec], W=[h])
            for co in range(8):
                pb = self.bank()
                for kc in range(8):
                    S.I("pe", "matmul", pb[:, :], lhsT=wq[:, kc, co * 128:(co + 1) * 128], rhs=h[:, kc, :], start=(kc == 0), stop=(kc == 7), R=[wq, h], W=[pb], inc=(kc == 7))
                S.I("act", "copy", q_all[:, co, :], pb[:, :], R=[pb], W=[q_all])
                S.I("act", "activation", out=qq_all[:, co, :], in_=q_all[:, co, :], func=AF.Square, R=[q_all], W=[qq_all])
            for co in range(8):
                rq_ = rqs[co % 2]
                pb2 = self.bank()
                S.I("pe", "matmul", pb2[:, :], lhsT=bd64, rhs=qq_all[:, co, :], start=True, stop=True, R=[qq_all, self.cb], W=[pb2])
                self.rsqrt(rq_[:], pb2[:, :], 1e-6, rq_, pb2)
                S.I("pool", "tensor_tensor", out=rq_[:], in0=q_all[:, co, :], in1=rq_[:], op=ALU.mult, R=[rq_, q_all], W=[rq_])
                S.I("dve", "tensor_scalar", out=qT[:, co, :], in0=rq_[:], scalar1=qg[:, 0:1], scalar2=None, op0=ALU.mult, R=[rq_, qg], W=[qT])
            items = [(blk, e_, g_) for blk in range(4) for e_ in range(2) for g_ in range(2)]

            def stageA(it, slot):
                blk, e_, g_ = it
                n = ti * 4 + blk
                bs = slice(blk * 128, (blk + 1) * 128)
                srcs = [(1, n)] + ([(0, n - 1)] if n > 0 else [])
                for k, (which, kb) in enumerate(srcs):
                    pbS = self.bank()
                    sck = sc[(self._sci) % len(sc)]
                    self._sci += 1
                    Pk = P[slot * 2 + k]
                    S.I("pe", "matmul", pbS[:, :], lhsT=self.kTp[:, e_, g_, kb * 128:(kb + 1) * 128], rhs=qT[:, 4 * g_:4 * g_ + 4, bs], start=True, stop=True, R=[self.kTp, qT], W=[pbS])
                    S.I("dve", "tensor_tensor", out=sck[:].rearrange("p (h q) -> p h q", h=4), in0=pbS[:, :].rearrange("p (h q) -> p h q", h=4), in1=alibi[:, which, e_, 4 * g_:4 * g_ + 4, :], op=ALU.add, R=[pbS, alibi], W=[sck])
                    S.I("act", "activation", out=Pk[:], in_=sck[:], func=AF.Exp, R=[sck], W=[Pk])
                return srcs

            def stageB(it, slot, srcs):
                blk, e_, g_ = it
                bs = slice(blk * 128, (blk + 1) * 128)
                hs = slice(64 * e_, 64 * e_ + 64)
                pbO = self.bank()
                pbD = self.bank()
                for k, (which, kb) in enumerate(srcs):
                    Pk = P[slot * 2 + k]
                    S.I("pe", "matmul", pbO[:, :], lhsT=self.Vt[:, kb, g_ * 128:(g_ + 1) * 128], rhs=Pk[:], start=(k == 0), stop=(k == len(srcs) - 1), R=[self.Vt, Pk], W=[pbO], inc=(k == len(srcs) - 1))
                for k, (which, kb) in enumerate(srcs):
                    Pk = P[slot * 2 + k]
                    S.I("pe", "matmul", pbD[:, :], lhsT=ones1, rhs=Pk[:], start=(k == 0), stop=False, R=[self.cb, Pk], W=[pbD], inc=False)
                S.I("pe", "matmul", pbD[:, :], lhsT=ones1[0:1, :], rhs=skb[0:1, e_, g_, :], start=False, stop=True, R=[self.cb, skb], W=[pbD])
                rc = rec[slot]
                S.I("act", "activation", out=rc[:], in_=pbD[:, :], func=AF.Ln, R=[pbD], W=[rc])
                S.I("act", "activation", out=rc[:], in_=rc[:], func=AF.Exp, scale=-1.0, R=[rc], W=[rc])
                S.I("dve", "tensor_tensor", out=attT[hs, 4 * g_:4 * g_ + 4, bs], in0=pbO[hs, :].rearrange("p (h q) -> p h q", h=4), in1=rc[hs, :].rearrange("p (h q) -> p h q", h=4), op=ALU.mult, R=[pbO, rc], W=[attT])

            self._sci = 0
            prev = None
            for idx, it in enumerate(items):
                srcs = stageA(it, idx % 2)
                if prev is not None:
                    stageB(*prev)
                prev = (it, idx % 2, srcs)
            stageB(*prev)
            for co in range(8):
                pb = self.bank()
                for kc in range(8):
                    S.I("pe", "matmul", pb[:, :], lhsT=wo[:, kc, co * 128:(co + 1) * 128], rhs=attT[:, kc, :], start=(kc == 0), stop=(kc == 7), R=[wo, attT], W=[pb], inc=(kc == 7))
                S.I("dve", "tensor_tensor", out=x[:, co, :], in0=pb[:, :], in1=x[:, co, :], op=ALU.add, R=[pb, x], W=[x])
            S.I("sp", "dma_start", out=self.xview(d["yT"], t0), in_=x[:], R=[x], W=[self.XD[ti]], dma="xst")


Builder.phase_kv = _phase_kv
Builder.phase_attn = _phase_attn
```

```python
import os
import numpy as np
from contextlib import ExitStack
import concourse.bass as bass
import concourse.mybir as mybir
from concourse.bass_utils import run_bass_kernel_spmd

F32 = mybir.dt.float32
BF16 = mybir.dt.bfloat16
F32R = mybir.dt.float32r
ALU = mybir.AluOpType
AF = mybir.ActivationFunctionType

C = 1024
NC_ = 8
TT = 512
FF = 4096
ENGS = ("pe", "act", "dve", "pool", "sp")


class Buf:
    __slots__ = ("name", "w", "r")

    def __init__(self, name):
        self.name = name
        self.w = None
        self.r = {}


class Tl:
    def __init__(self, h, name):
        self.h = h
        self.b = Buf(name)

    def __getitem__(self, k):
        return self.h[k]


class Sched:
    def __init__(self, nc):
        self.nc = nc
        self.q = {e: [] for e in ENGS}
        self.cnt = {}
        self.known = {e: {} for e in ENGS}
        self.pending = {e: False for e in ENGS}
        self.sems = {}
        for e in ENGS:
            self.sems[e] = nc.alloc_semaphore("sem_" + e)
        self.nins = 0
        self.dman = {}

    def _sem(self, k):
        if k not in self.sems:
            self.sems[k] = self.nc.alloc_semaphore(k.replace(":", "_"))
        return self.sems[k]

    def _need(self, eng, waits, tok):
        if tok is None:
            return
        k, v = tok
        if k == eng and eng in ("pe", "sp"):
            return
        if self.known[eng].get(k, 0) >= v:
            return
        if waits.get(k, 0) < v:
            waits[k] = v

    def op(self, eng, fn, R=(), W=(), inc=True, dma=None):
        waits = {}
        for t in R:
            self._need(eng, waits, t.b.w)
        for t in W:
            self._need(eng, waits, t.b.w)
            for k, v in t.b.r.items():
                self._need(eng, waits, (k, v))
        if dma is not None:
            K = 24 if eng == "sp" else 12
            n_ = self.dman.get(eng, 0)
            self.dman[eng] = n_ + 1
            key = f"dma:{eng}{n_ % K}"
            self._sem(key)
            if self.cnt.get(key, 0):
                self._need(eng, waits, (key, self.cnt[key]))
        for k, v in waits.items():
            self.known[eng][k] = v
        self.nins += 1
        if dma is not None:
            self.cnt[key] = self.cnt.get(key, 0) + 16
            tok = (key, self.cnt[key])
            self.q[eng].append((waits, fn, key, 16))
        elif inc:
            self.cnt[eng] = self.cnt.get(eng, 0) + 1
            tok = (eng, self.cnt[eng])
            self.pending[eng] = False
            self.q[eng].append((waits, fn, eng, 1))
        else:
            tok = (eng, self.cnt.get(eng, 0) + 1)
            self.pending[eng] = True
            self.q[eng].append((waits, fn, None, 0))
        for t in R:
            k, v = tok
            if t.b.r.get(k, 0) < v:
                t.b.r[k] = v
        for t in W:
            t.b.w = tok
            t.b.r = {}
        return tok

    def I(self, eng, meth, *args, R=(), W=(), inc=True, dma=None, **kw):
        return self.op(eng, lambda e: getattr(e, meth)(*args, **kw), R=R, W=W, inc=inc, dma=dma)

    def barrier(self):
        for e in ENGS:
            waits = {}
            for k, v in self.cnt.items():
                self._need(e, waits, (k, v))
            for k, v in waits.items():
                self.known[e][k] = v
            if waits:
                self.q[e].append((waits, None, None, 0))

    def flush(self, final=False):
        nc = self.nc
        for e in ENGS:
            assert not self.pending[e], e
        sems = self.sems
        q = self.q
        self.q = {e: [] for e in ENGS}
        cnt = dict(self.cnt)
        with nc.Block() as block:
            def run(engkey, engobj):
                for waits, fn, inck, incv in q[engkey]:
                    for k, v in waits.items():
                        engobj.wait_ge(sems[k], v)
                    if fn is None:
                        continue
                    ins = fn(engobj)
                    if inck is not None:
                        ins.then_inc(sems[inck], incv)
                if final and engkey == "sp":
                    for k, v in cnt.items():
                        if k.startswith("dma:"):
                            engobj.wait_ge(sems[k], v)

            @block.tensor
            def _(e):
                run("pe", e)

            @block.scalar
            def _(e):
                run("act", e)

            @block.vector
            def _(e):
                run("dve", e)

            @block.gpsimd
            def _(e):
                run("pool", e)

            @block.sync
            def _(e):
                run("sp", e)


VEC_NAMES = []


def _vec_index():
    names = []
    for i in range(4):
        names += [f"ln_mix{i}", f"ln_mlp{i}"]
    for j in range(2):
        names += [f"mu{j}_{n}" for n in range(6)]
        names += [f"w0_{j}", f"a0_{j}", f"kk_{j}", f"ka_{j}", f"rk_{j}", f"gng_{j}", f"gnb_{j}"]
    names += ["v0", "kvn"]
    return {n: i for i, n in enumerate(names)}


VIDX = _vec_index()
NV = len(VIDX)


def make_consts():
    c = {}
    c["ident"] = np.eye(128, dtype=np.float32)
    s = np.arange(128)[:, None]
    t = np.arange(128)[None, :]
    su = (s < t).astype(np.float32)
    ule = (s <= t).astype(np.float32)
    sl = (s > t).astype(np.float32)
    c["m_abrb"] = np.concatenate([su, ule], 1)
    c["m_akrk"] = np.concatenate([su, ule], 1)
    c["m_sl"] = sl
    bd = np.zeros((128, 128), np.float32)
    bd[:64, :64] = 1
    bd[64:, 64:] = 1
    c["bd1"] = bd
    c["bd64"] = bd / 64.0
    c["onesm"] = np.full((128, 128), 1.0 / 1024.0, np.float32)
    c["ones1"] = np.ones((128, 128), np.float32)
    slopes = np.exp2(-8.0 * np.arange(1, 17, dtype=np.float32) / 16.0).astype(np.float32)
    j = np.arange(128)[:, None].astype(np.float32)
    i = np.arange(128)[None, :].astype(np.float32)
    dcur = i - j
    dprev = i + 128 - j
    bc = np.where(dcur[:, None, :] >= 0, -slopes[None, :, None] * dcur[:, None, :], -30000.0)
    bp = np.where(dprev[:, None, :] < 128, -slopes[None, :, None] * dprev[:, None, :], -30000.0)
    bc = bc.reshape(128, 8, 2, 128).transpose(0, 2, 1, 3)
    bp = bp.reshape(128, 8, 2, 128).transpose(0, 2, 1, 3)
    c["alibi"] = np.ascontiguousarray(np.stack([bp, bc], 1)).astype(np.float32)
    return c


CONST_ORDER = ["ident", "m_abrb", "m_sl", "bd1", "bd64", "onesm", "ones1"]


class Builder:
    def __init__(self, T, phases, dbg=None):
        self.T = T
        self.NT = T // TT
        self.phases = phases
        self.dbg = dbg or {}
        nc = self.nc = bass.Bass("TRN2", target_bir_lowering=False)
        self.S = Sched(nc)
        d = self.d = {}

        def din(name, shape, dt=F32):
            d[name] = nc.dram_tensor(name, list(shape), dt, kind="ExternalInput").ap()

        din("xT", [C, T])
        din("vecs", [128, NV, 8])
        din("hvec", [128, 8])
        din("sinkrow", [1, 2, 2, 2, 4, 128])
        din("cmat", [128, len(CONST_ORDER) + 1, 128])
        din("alibi", [128, 2, 2, 8, 128])
        din("mlp_w1", [4, C, FF])
        din("mlp_w2", [4, FF, C])
        din("a_w_rkv", [2, 3, C, C])
        din("a_w1", [2, C, 64])
        din("a_w2", [2, 64, C])
        din("a_a1", [2, C, 64])
        din("a_a2", [2, 64, C])
        din("a_g1", [2, C, 160])
        din("a_g2", [2, 160, C])
        din("a_wo", [2, C, C])
        din("a_v1", [1, C, 32])
        din("a_v2", [1, 32, C])
        din("w_kv", [C, 256])
        din("b_wq", [2, C, C])
        din("b_wo", [2, C, C])
        d["yT"] = nc.dram_tensor("yT", [C, T], F32, kind="ExternalOutput").ap()
        d["vfirst"] = nc.dram_tensor("vfirst", [C, T], F32, kind="Internal").ap()
        d["wrkv_bf"] = nc.dram_tensor("wrkv_bf", [2, 3, C, C], BF16, kind="Internal").ap()
        self.WBF = Tl(None, "wrkv_bf")
        for name, shape in self.dbg.items():
            d[name] = nc.dram_tensor(name, list(shape), F32, kind="ExternalOutput").ap()
        self.XD = [Tl(None, f"xres_dram{i}") for i in range(self.NT)]
        self.VFD = Tl(None, "vfirst_dram")
        self.DBG = Tl(None, "dbg")
        self.bank_i = 0
        self.dmaq = 0

    def sb(self, st, name, shape, dt):
        self.uid = getattr(self, "uid", 0) + 1
        name = f"s{self.uid}_{name}"
        h = st.enter_context(self.nc.sbuf_tensor(name, list(shape), dt))
        return Tl(h, name)

    def bank(self):
        b = self.ps[self.bank_i % 8]
        self.bank_i += 1
        return b

    def xview(self, ap, t0, n=TT):
        return ap[:, t0:t0 + n].rearrange("(c p) t -> p c t", p=128)

    def load_cast(self, tl, out_ap, in_ap):
        self.S.I("pool", "dma_start", out=out_ap, in_=in_ap, W=[tl], dma=tl.b.name)

    def build(self):
        nc, S, d = self.nc, self.S, self.d
        with ExitStack() as g:
            self.ps = [Tl(g.enter_context(nc.psum_tensor(f"ps{i}", [128, 512], F32)), f"ps{i}") for i in range(8)]
            cm = self.sb(g, "cmat32", [128, len(CONST_ORDER) + 1, 128], F32)
            S.I("sp", "dma_start", out=cm[:], in_=d["cmat"], W=[cm], dma="cm")
            self.cb = self.sb(g, "cmatb", [128, len(CONST_ORDER) + 1, 128], BF16)
            S.I("dve", "tensor_copy", self.cb[:], cm[:], R=[cm], W=[self.cb])
            self.cm = cm
            self.vec = self.sb(g, "vecs", [128, NV + 4, 8], F32)
            S.I("sp", "dma_start", out=self.vec[:, 0:NV, :], in_=d["vecs"], W=[self.vec], dma="vec")
            self.hvec = self.sb(g, "hvec", [128, 8], F32)
            S.I("sp", "dma_start", out=self.hvec[:], in_=d["hvec"], W=[self.hvec], dma="hvec")
            self.eps_list = [1e-6, 1e-24, 64e-5, 1.0, float(np.log(0.6065306597126334))]
            self.epst = self.sb(g, "epst", [128, 8], F32)
            for i_, v_ in enumerate(self.eps_list):
                S.I("pool", "memset", self.epst[:, i_:i_ + 1], v_, W=[self.epst])
            self.kvstack = None
            for ti in range(self.NT):
                S.I("sp", "dma_start", out=d["yT"][:, ti * TT:(ti + 1) * TT], in_=d["xT"][:, ti * TT:(ti + 1) * TT], W=[self.XD[ti]], dma="xinit")
            for ph in self.phases:
                kind = ph[0]
                if kind == "rwkv":
                    self.phase_rwkv(ph[1])
                elif kind == "mlp":
                    self.phase_mlp(ph[1])
                elif kind == "kv":
                    self.kvstack = g
                    self.phase_kv()
                elif kind == "attn":
                    self.phase_attn(ph[1], ph[2])
                S.barrier()
                S.flush()
            S.barrier()
            S.flush(final=True)
        return nc

    def cmat(self, name):
        i = CONST_ORDER.index(name)
        return self.cb[:, i, :]

    def vcol(self, name, c):
        i = VIDX[name]
        return self.vec[:, i, c:c + 1]

    def rms_rstd(self, x, sq, rstd):
        S = self.S
        S.I("act", "activation", out=sq[:], in_=x[:], func=AF.Square, R=[x], W=[sq])
        pb = self.bank()
        om = self.cmat("onesm")
        for c in range(NC_):
            S.I("pe", "matmul", pb[:, :], lhsT=om, rhs=sq[:, c, :], start=(c == 0), stop=(c == NC_ - 1),
                 R=[sq, self.cb], W=[pb], inc=(c == NC_ - 1))
        self.rsqrt(rstd[:], pb[:, :], 1e-6, rstd, pb)

    def phase_mlp(self, L):
        nc, S, d = self.nc, self.S, self.d
        with ExitStack() as st:
            xs_ = [self.sb(st, f"mx{i}", [128, 8, TT], F32) for i in range(1)]
            sq = self.sb(st, "msq", [128, 8, TT], BF16)
            rstd = self.sb(st, "mrstd", [128, TT], F32)
            xn = [self.sb(st, f"mxn{i}", [128, 8, TT], BF16) for i in range(1)]
            h1 = self.sb(st, "mh1", [128, 32, TT], BF16)
            rl = [self.sb(st, f"mrl{i}", [128, TT], F32) for i in range(2)]
            w1b = [self.sb(st, f"mw1_{i}", [128, 8, 1024], BF16) for i in range(2)]
            w2b = [self.sb(st, f"mw2_{i}", [128, 32, 256], BF16) for i in range(2)]
            w1d = d["mlp_w1"][L].rearrange("(kc p) f -> p kc f", p=128)
            w2d = d["mlp_w2"][L].rearrange("(fc p) c -> p fc c", p=128)
            i1 = i2 = 0
            for ti in range(self.NT):
                t0 = ti * TT
                x = xs_[0]
                S.I("sp", "dma_start", out=x[:], in_=self.xview(d["yT"], t0), R=[self.XD[ti]], W=[x], dma=x.b.name)
                self.rms_rstd(x, sq, rstd)
                xnn = xn[0]
                for c in range(NC_):
                    S.I("dve", "scalar_tensor_tensor", out=xnn[:, c, :], in0=x[:, c, :], scalar=self.vcol(f"ln_mlp{L}", c), in1=rstd[:], op0=ALU.mult, op1=ALU.mult,
                         R=[x, rstd, self.vec], W=[xnn])
                for q in range(4):
                    w = w1b[i1 % 2]
                    i1 += 1
                    self.load_cast(w, w[:], w1d[:, :, q * 1024:(q + 1) * 1024])
                    for fc in range(8):
                        pb = self.bank()
                        for kc in range(8):
                            S.I("pe", "matmul", pb[:, :], lhsT=w[:, kc, fc * 128:(fc + 1) * 128], rhs=xnn[:, kc, :], start=(kc == 0), stop=(kc == 7),
                                 R=[w, xnn], W=[pb], inc=(kc == 7))
                        r = rl[fc % 2]
                        S.I("act", "activation", out=r[:], in_=pb[:, :], func=AF.Relu, R=[pb], W=[r])
                        S.I("dve", "tensor_tensor", out=h1[:, q * 8 + fc, :], in0=r[:], in1=r[:], op=ALU.mult, R=[r], W=[h1])
                for q in range(4):
                    w = w2b[i2 % 2]
                    i2 += 1
                    self.load_cast(w, w[:], w2d[:, :, q * 256:(q + 1) * 256])
                    for cc in range(2):
                        co = q * 2 + cc
                        pb = self.bank()
                        for fc in range(32):
                            S.I("pe", "matmul", pb[:, :], lhsT=w[:, fc, cc * 128:(cc + 1) * 128], rhs=h1[:, fc, :], start=(fc == 0), stop=(fc == 31),
                                 R=[w, h1], W=[pb], inc=(fc == 31))
                        S.I("dve", "tensor_tensor", out=x[:, co, :], in0=pb[:, :], in1=x[:, co, :], op=ALU.add, R=[pb, x], W=[x])
                S.I("sp", "dma_start", out=self.xview(d["yT"], t0), in_=x[:], R=[x], W=[self.XD[ti]], dma="xst")


def prep_shared(inp):
    f = lambda a: np.asarray(a, dtype=np.float32)
    vecs = np.zeros((NV, C), np.float32)
    for i in range(4):
        vecs[VIDX[f"ln_mix{i}"]] = f(inp["ln_mix"])[i]
        vecs[VIDX[f"ln_mlp{i}"]] = f(inp["ln_mlp"])[i]
    for j in range(2):
        for n in range(6):
            vecs[VIDX[f"mu{j}_{n}"]] = f(inp["a_mu"])[j, n]
        vecs[VIDX[f"w0_{j}"]] = f(inp["a_w0"])[j]
        vecs[VIDX[f"a0_{j}"]] = f(inp["a_a0"])[j]
        vecs[VIDX[f"kk_{j}"]] = f(inp["a_k_k"])[j]
        vecs[VIDX[f"ka_{j}"]] = f(inp["a_k_a"])[j]
        vecs[VIDX[f"rk_{j}"]] = f(inp["a_r_k"])[j].reshape(-1)
        vecs[VIDX[f"gng_{j}"]] = f(inp["a_gn_g"])[j]
        vecs[VIDX[f"gnb_{j}"]] = f(inp["a_gn_b"])[j]
    vecs[VIDX["v0"]] = f(inp["a_v0"])[0]
    vecs[VIDX["kvn"]] = f(inp["kv_norm"])
    vecs_l = np.ascontiguousarray(vecs.reshape(NV, 8, 128).transpose(2, 0, 1))
    hvec = np.zeros((128, 8), np.float32)
    hvec[:, 0] = np.tile(f(inp["b_q_gain"])[0], 2)
    hvec[:, 1] = np.tile(f(inp["b_q_gain"])[1], 2)
    hvec[:, 2] = np.tile(f(inp["k_gain"]), 2)
    sk = f(inp["b_sinks"])
    sr = sk.reshape(2, 2, 4, 2).transpose(0, 3, 1, 2)
    sinkrow = np.ascontiguousarray(np.broadcast_to(sr[None, :, :, :, :, None], (1, 2, 2, 2, 4, 128))).astype(np.float32)
    cs = make_consts()
    cmat = np.stack([cs[k][:, 0:128] for k in CONST_ORDER] + [cs["m_abrb"][:, 128:256]], 1)
    sh = {"vecs": vecs_l, "hvec": hvec, "sinkrow": sinkrow, "cmat": np.ascontiguousarray(cmat.astype(np.float32)),
          "alibi": cs["alibi"]}
    for k in ["mlp_w1", "mlp_w2", "a_w_rkv", "a_w1", "a_w2", "a_a1", "a_a2", "a_g1", "a_g2", "a_wo", "a_v1", "a_v2", "w_kv", "b_wq", "b_wo"]:
        sh[k] = np.ascontiguousarray(f(inp[k]))
    return sh


FULL_PHASES = [("rwkv", 0), ("mlp", 0), ("rwkv", 1), ("mlp", 1), ("kv",), ("attn", 0, 2), ("mlp", 2), ("attn", 1, 3), ("mlp", 3)]


def run(inp, phases, T, ncores, dbg=None):
    sh = prep_shared(inp)
    x = np.asarray(inp["x"], dtype=np.float32)
    in_maps = []
    for b in range(ncores):
        m = dict(sh)
        m["xT"] = np.ascontiguousarray(x[b, :T].T)
        in_maps.append(m)
    nc = Builder(T, phases, dbg).build()
    res = run_bass_kernel_spmd(nc, in_maps, core_ids=list(range(ncores)))
    return res


def kernel(**inputs):
    res = run(inputs, FULL_PHASES, 4096, 8)
    out = np.stack([np.ascontiguousarray(r["yT"].T) for r in res.results], 0)
    return out.astype(np.float32)


TR = 256
NJ = 2


def _phase_rwkv(self, L):
    nc, S, d = self.nc, self.S, self.d
    NTR = self.T // TR
    with ExitStack() as st:
        sb = lambda name, shape, dt: self.sb(st, name, shape, dt)
        V = lambda name, c: self.vcol(f"{name}_{L}", c)
        cview = lambda ap: ap.rearrange("(kc p) n -> p kc n", p=128)
        wo = sb("wo", [128, 8, 1024], BF16)
        for i in range(3):
            S.I("pool", "dma_start", out=d["wrkv_bf"][L, i], in_=d["a_w_rkv"][L, i], W=[self.WBF], dma="wcast")
        self.load_cast(wo, wo[:], cview(d["a_wo"][L]))
        wslot = [sb(f"wslot{i}", [128, 3, 8, 128], BF16) for i in range(2)]
        w1 = sb("w1", [128, 8, 64], BF16)
        a1 = sb("a1", [128, 8, 64], BF16)
        g1 = sb("g1", [128, 8, 160], BF16)
        self.load_cast(w1, w1[:], cview(d["a_w1"][L]))
        self.load_cast(a1, a1[:], cview(d["a_a1"][L]))
        self.load_cast(g1, g1[:], cview(d["a_g1"][L]))
        w2 = sb("w2", [64, 1024], BF16)
        a2 = sb("a2", [64, 1024], BF16)
        self.load_cast(w2, w2[:], d["a_w2"][L])
        self.load_cast(a2, a2[:], d["a_a2"][L])
        g2 = sb("g2", [128, 2, 1024], BF16)
        self.load_cast(g2, g2[:, 0, :], d["a_g2"][L][0:128, :])
        self.load_cast(g2, g2[0:32, 1, :], d["a_g2"][L][128:160, :])
        if L == 1:
            v1 = sb("v1", [128, 8, 32], BF16)
            v2 = sb("v2", [32, 1024], BF16)
            self.load_cast(v1, v1[:], cview(d["a_v1"][0]))
            self.load_cast(v2, v2[:], d["a_v2"][0])
        vx = sb("vx", [128, 3, 8], F32)
        for i, nm in enumerate([f"w0_{L}", f"a0_{L}", "v0"]):
            S.I("dve", "tensor_scalar", out=vx[:, i, :], in0=self.vec[:, VIDX[nm], :], scalar1=-1.0, scalar2=None, op0=ALU.mult, R=[self.vec], W=[vx])
        ones = sb("ones", [128, 128], F32)
        S.I("pool", "memset", ones[:], 1.0, W=[ones])
        mask4 = sb("mask4", [128, 4, 128], BF16)
        ia, ib = CONST_ORDER.index("m_abrb"), len(CONST_ORDER)
        for q in range(4):
            S.I("pool", "tensor_copy", mask4[:, q, :], self.cm[:, (ia if q % 2 == 0 else ib), :], R=[self.cm], W=[mask4])
        msl = self.cm[:, CONST_ORDER.index("m_sl"), :]
        bd1f = self.cm[:, CONST_ORDER.index("bd1"), :]
        identb = self.cmat("ident")
        bd1 = self.cmat("bd1")
        bd64 = self.cmat("bd64")
        Mf = [sb(f"Mf{c}", [128, 64], F32) for c in range(8)]
        Mb = [[sb(f"Mb{c}_{p}", [128, 128], BF16) for p in range(2)] for c in range(8)]
        for c in range(8):
            for p in range(2):
                if p == 0:
                    S.I("pool", "memset", Mf[c][:], 0.0, W=[Mf[c]])
                S.I("pool", "memset", Mb[c][p][:], 0.0, W=[Mb[c][p]])
        x = sb("x", [128, 8, TR], F32)
        rstd = sb("rstd", [128, TR], F32)
        h = sb("h", [128, 8, TR + 1], F32)
        S.I("pool", "memset", h[:, :, 0:1], 0.0, W=[h])
        carry = sb("carry", [128, 8, 1], F32)
        xx = sb("xx", [128, 8, TR], BF16)
        xsr, xsk, xsv = [sb(n, [128, 8, TR], BF16) for n in ("xsr", "xsk", "xsv")]
        xso = [sb(f"xso{i}", [128, 8, TR], BF16) for i in range(2)]
        sq = xso[1]
        lw = sb("lw", [64, TR], BF16)
        al = sb("al", [64, TR], BF16)
        vl = sb("vl", [32, TR], BF16)
        sg = sb("sg", [128, 2, TR], BF16)
        sgt = sb("sgt", [128, 2, TR], BF16)
        ygb = sb("ygb", [128, 8, TR], BF16)
        PLt = sb("PLt", [128, 8, NJ], F32)
        NTMP = 14
        tmp = [[sb(f"t{s}_{i}", [128, TR], F32) for i in range(NTMP)] for s in range(1)]
        opb = [[sb(f"o{s}_{i}", [128, TR], BF16) for i in range(2)] for s in range(1)]
        ar = [sb(f"ar{s}", [128, NJ, 2, 128], BF16) for s in range(4)]
        vft = [sb(f"vf{s}", [128, TR], F32) for s in range(2)]
        par3 = [[sb(f"p3_{s}_{i}", [128, TR], F32) for i in range(3)] for s in range(4)]
        opb2 = [[sb(f"ob{s}_{i}", [128, TR], BF16) for i in range(5)] for s in range(4)]
        epf = [sb(f"epf{i}", [128, TR], F32) for i in range(2)]
        epb = [sb(f"epb{i}", [128, TR], BF16) for i in range(2)]
        def wset(s):
            w = {}
            w["AX"] = sb(f"AX{s}", [128, 2, 128], BF16)
            w["BKV"] = sb(f"BKV{s}", [128, 3, 128], BF16)
            w["Vpad"] = sb(f"Vpad{s}", [128, 2, 128], BF16)
            w["AA"] = sb(f"AA{s}", [128, 2, 4, 128], BF16)
            DD = F32R if os.environ.get("RW_FP32R", "1") == "1" else F32
            w["N0"] = sb(f"N0{s}", [128, 2, 128], DD)
            w["Q0"] = sb(f"Q0{s}", [128, 2, 128], DD)
            w["QN"] = [sb(f"QN{s}_{i}", [128, 2, 2, 128], DD) for i in range(2)]
            w["Z"] = [sb(f"Z{s}_{i}", [128, 2, 128], DD) for i in range(2)]
            w["Zfin"] = sb(f"Zfin{s}", [128, 2, 128], BF16)
            w["Ap"] = sb(f"Ap{s}", [128, 128], BF16)
            w["UV"] = sb(f"UV{s}", [128, 128], BF16)
            w["UVpad"] = sb(f"UVpad{s}", [128, 2, 128], BF16)
            w["GT"] = sb(f"GT{s}", [128, 128], BF16)
            w["RT"] = sb(f"RT{s}", [128, 128], BF16)
            for k in ("Vpad", "UVpad"):
                S.I("pool", "memset", w[k][:], 0.0, W=[w[k]])
            return w
        ws = [wset(s) for s in range(4)]
        if os.environ.get("DBG_SBUF"):
            print("rwkv sbuf remaining", nc.sbuf_bytes_remaining)

        def diag(tl, off=0):
            a = tl[:]
            ps_ = a.ap[0][0]
            return bass.AP(a.tensor, a.offset + off, [[ps_, 128], [192, 2], [1, 64]])

        def strided(tl, off, stride):
            a = tl[:]
            ps_ = a.ap[0][0]
            return bass.AP(a.tensor, a.offset + off, [[ps_, 128], [stride, 2], [1, 64]])

        self._evi = 0

        def evac_copy(out_ap, in_ap, R, W):
            self._evi += 1
            if self._evi % 2:
                S.I("act", "copy", out_ap, in_ap, R=R, W=W)
            else:
                S.I("dve", "tensor_copy", out_ap, in_ap, R=R, W=W)

        def proj(pb, col, wt, xs, co, n=128, kparts=128):
            for kc in range(8):
                S.I("pe", "matmul", pb[0:n, col:col + TR], lhsT=wt[:, kc, co:co + n], rhs=xs[:, kc, :], start=(kc == 0), stop=(kc == 7),
                     R=[wt, xs], W=[pb], inc=(kc == 7))

        def projs(pb, col, slot, n_, xs):
            for kc in range(8):
                S.I("pe", "matmul", pb[:, col:col + TR], lhsT=slot[:, n_, kc, :], rhs=xs[:, kc, :], start=(kc == 0), stop=(kc == 7),
                    R=[slot, xs], W=[pb], inc=(kc == 7))

        def mix(dst, n):
            for c in range(8):
                S.I("dve", "scalar_tensor_tensor", out=dst[:, c, :], in0=xx[:, c, :], scalar=self.vcol(f"mu{L}_{n}", c), in1=h[:, c, 1:TR + 1], op0=ALU.mult, op1=ALU.add,
                     R=[xx, h, self.vec], W=[dst])

        cj = 0
        for ti in range(NTR):
            t0 = ti * TR
            xd = self.XD[t0 // TT]
            S.I("sp", "dma_start", out=x[:], in_=self.xview(d["yT"], t0, TR), R=[xd], W=[x], dma="rx")
            self.rms_rstd_n(x, sq, rstd, TR)
            if ti > 0:
                S.I("pool", "tensor_copy", h[:, :, 0:1], carry[:], R=[carry], W=[h])
            for c in range(8):
                S.I("dve", "scalar_tensor_tensor", out=h[:, c, 1:TR + 1], in0=x[:, c, :], scalar=self.vcol(f"ln_mix{L}", c), in1=rstd[:, 0:TR], op0=ALU.mult, op1=ALU.mult,
                     R=[x, rstd, self.vec], W=[h])
            S.I("pool", "tensor_copy", carry[:], h[:, :, TR:TR + 1], R=[h], W=[carry])
            S.I("dve", "tensor_tensor", out=xx[:], in0=h[:, :, 0:TR], in1=h[:, :, 1:TR + 1], op=ALU.subtract, R=[h], W=[xx])
            mix(xsr, 0)
            mix(xsk, 1)
            mix(xsv, 2)
            xw = xso[0]
            mix(xw, 3)
            pb = self.bank()
            proj(pb, 0, w1, xw, 0, n=64)
            S.I("act", "activation", out=lw[:], in_=pb[0:64, 0:TR], func=AF.Tanh, R=[pb], W=[lw])
            xa = xso[1]
            mix(xa, 4)
            proj(pb, 256, a1, xa, 0, n=64)
            S.I("act", "copy", al[:], pb[0:64, 256:256 + TR], R=[pb], W=[al])
            xg = xso[0]
            mix(xg, 5)
            pb = self.bank()
            proj(pb, 0, g1, xg, 0, n=128)
            proj(pb, 256, g1, xg, 128, n=32)
            S.I("act", "activation", out=sgt[:, 0, :], in_=pb[:, 0:TR], func=AF.Exp, scale=-1.0, R=[pb], W=[sgt])
            S.I("act", "activation", out=sgt[0:32, 1, :], in_=pb[0:32, 256:256 + TR], func=AF.Exp, scale=-1.0, R=[pb], W=[sgt])
            S.I("act", "activation", out=sgt[:, 0, :], in_=sgt[:, 0, :], func=AF.Ln, bias=self.epsc(1.0), R=[sgt, self.epst], W=[sgt])
            S.I("act", "activation", out=sgt[0:32, 1, :], in_=sgt[0:32, 1, :], func=AF.Ln, bias=self.epst[0:32, 3:4], R=[sgt, self.epst], W=[sgt])
            S.I("act", "activation", out=sg[:, 0, :], in_=sgt[:, 0, :], func=AF.Exp, scale=-1.0, R=[sgt], W=[sg])
            S.I("act", "activation", out=sg[0:32, 1, :], in_=sgt[0:32, 1, :], func=AF.Exp, scale=-1.0, R=[sgt], W=[sg])
            if L == 1:
                pbv = self.bank()
                proj(pbv, 0, v1, xsv, 0, n=32)
                S.I("act", "copy", vl[:], pbv[0:32, 0:TR], R=[pbv], W=[vl])

            def make_c(c, s, holder):
                T_ = tmp[0]
                r_sb, k_sb, logw, a_sb, kkr, rn, kk, kmod, bvec, lp, E1, Em, Ep, Eh = T_
                v_sb, bonus, ysb = par3[s]
                bt, kt, bh, kh, vbf = opb2[s]
                prod, kq = opb[0][0], opb[0][1]
                arr = ar[s]
                cs = slice(c * 128, (c + 1) * 128)
                slot = wslot[s % 2]
                wsrc = lambda n_: d["wrkv_bf"][L, n_].rearrange("(kc p) m -> p kc m", p=128)[:, :, c * 128:(c + 1) * 128]
                for n_ in range(3):
                    S.I("sp", "dma_start", out=slot[:, n_, :, :], in_=wsrc(n_), R=[self.WBF], W=[slot], dma=slot.b.name)
                pb1 = self.bank()
                projs(pb1, 0, slot, 0, xsr)
                projs(pb1, 256, slot, 1, xsk)
                pb2 = self.bank()
                projs(pb2, 0, slot, 2, xsv)
                S.I("pe", "matmul", pb2[:, 256:256 + TR], lhsT=w2[:, cs], rhs=lw[:], start=True, stop=True, R=[w2, lw], W=[pb2])
                pb3 = self.bank()
                S.I("pe", "matmul", pb3[:, 0:TR], lhsT=a2[:, cs], rhs=al[:], start=True, stop=True, R=[a2, al], W=[pb3], inc=(L == 0))
                if L == 1:
                    S.I("pe", "matmul", pb3[:, 256:256 + TR], lhsT=v2[:, cs], rhs=vl[:], start=True, stop=True, R=[v2, vl], W=[pb3])
                S.I("act", "copy", r_sb[:], pb1[:, 0:TR], R=[pb1], W=[r_sb])
                S.I("act", "copy", k_sb[:], pb1[:, 256:256 + TR], R=[pb1], W=[k_sb])
                S.I("act", "copy", v_sb[:], pb2[:, 0:TR], R=[pb2], W=[v_sb])
                vrows = d["vfirst"][c * 128:(c + 1) * 128, t0:t0 + TR]
                if L == 0:
                    S.I("sp", "dma_start", out=vrows, in_=v_sb[:], R=[v_sb], W=[self.VFD], dma="vfst")
                else:
                    vf = vft[s % 2]
                    S.I("sp", "dma_start", out=vf[:], in_=vrows, R=[self.VFD], W=[vf], dma=vf.b.name)
                    S.I("act", "activation", out=E1[:], in_=pb3[:, 256:256 + TR], func=AF.Exp, bias=vx[:, 2, c:c + 1], scale=-1.0, R=[pb3, vx], W=[E1])
                    S.I("act", "activation", out=E1[:], in_=E1[:], func=AF.Ln, bias=self.epsc(1.0), R=[E1, self.epst], W=[E1])
                    S.I("act", "activation", out=E1[:], in_=E1[:], func=AF.Exp, scale=-1.0, R=[E1], W=[E1])
                    S.I("dve", "tensor_tensor", out=vf[:], in0=vf[:], in1=v_sb[:], op=ALU.subtract, R=[vf, v_sb], W=[vf])
                    S.I("dve", "tensor_tensor", out=vf[:], in0=vf[:], in1=E1[:], op=ALU.mult, R=[vf, E1], W=[vf])
                    S.I("dve", "tensor_tensor", out=v_sb[:], in0=v_sb[:], in1=vf[:], op=ALU.add, R=[vf, v_sb], W=[v_sb])
                S.I("act", "activation", out=logw[:], in_=pb2[:, 256:256 + TR], func=AF.Exp, bias=vx[:, 0, c:c + 1], scale=-1.0, R=[pb2, vx], W=[logw])
                S.I("act", "activation", out=logw[:], in_=logw[:], func=AF.Ln, bias=self.epsc(1.0), R=[logw, self.epst], W=[logw])
                S.I("act", "activation", out=logw[:], in_=logw[:], func=AF.Exp, bias=self.epsc(self.eps_list[4]), scale=-1.0, R=[logw, self.epst], W=[logw])
                S.I("act", "activation", out=a_sb[:], in_=pb3[:, 0:TR], func=AF.Exp, bias=vx[:, 1, c:c + 1], scale=-1.0, R=[pb3, vx], W=[a_sb])
                yield
                S.I("act", "activation", out=a_sb[:], in_=a_sb[:], func=AF.Ln, bias=self.epsc(1.0), R=[a_sb, self.epst], W=[a_sb])
                S.I("act", "activation", out=a_sb[:], in_=a_sb[:], func=AF.Exp, scale=-1.0, R=[a_sb], W=[a_sb])
                yield
                S.I("dve", "tensor_scalar", out=kkr[:], in0=k_sb[:], scalar1=V("kk", c), scalar2=None, op0=ALU.mult, R=[k_sb, self.vec], W=[kkr])
                S.I("act", "activation", out=kq[:], in_=kkr[:], func=AF.Square, R=[kkr], W=[kq])
                pbs = self.bank()
                S.I("pe", "matmul", pbs[:, 0:TR], lhsT=bd1, rhs=kq[:], start=True, stop=True, R=[kq, self.cb], W=[pbs])
                self.rsqrt(rn[:], pbs[:, 0:TR], 1e-24, rn, pbs)
                S.I("pool", "tensor_tensor", out=kk[:], in0=kkr[:], in1=rn[:], op=ALU.mult, R=[kkr, rn], W=[kk])
                yield
                S.I("dve", "tensor_scalar", out=kmod[:], in0=a_sb[:], scalar1=-1.0, scalar2=V("ka", c), op0=ALU.add, op1=ALU.mult, R=[a_sb, self.vec], W=[kmod])
                S.I("dve", "scalar_tensor_tensor", out=kmod[:], in0=kmod[:], scalar=1.0, in1=k_sb[:], op0=ALU.add, op1=ALU.mult, R=[kmod, k_sb], W=[kmod])
                S.I("pool", "tensor_tensor", out=bvec[:], in0=kk[:], in1=a_sb[:], op=ALU.mult, R=[kk, a_sb], W=[bvec])
                yield
                for j in range(NJ):
                    js = slice(j * 128, (j + 1) * 128)
                    S.I("dve", "tensor_tensor_scan", out=lp[:, js], data0=ones[:], data1=logw[:, js], initial=0.0, op0=ALU.mult, op1=ALU.subtract, R=[logw, ones], W=[lp])
                S.I("pool", "tensor_tensor", out=Ep[:], in0=lp[:], in1=logw[:], op=ALU.add, R=[lp, logw], W=[Ep])
                S.I("act", "activation", out=Ep[:], in_=Ep[:], func=AF.Exp, R=[Ep], W=[Ep])
                yield
                S.I("act", "activation", out=E1[:], in_=lp[:], func=AF.Exp, R=[lp], W=[E1])
                S.I("act", "activation", out=Em[:], in_=lp[:], func=AF.Exp, scale=-1.0, R=[lp], W=[Em])
                yield
                for j in range(NJ):
                    js = slice(j * 128, (j + 1) * 128)
                    S.I("act", "activation", out=Eh[:, js], in_=lp[:, js], func=AF.Exp, bias=lp[:, j * 128 + 127:j * 128 + 128], scale=-1.0, R=[lp], W=[Eh])
                    S.I("act", "activation", out=PLt[:, c, j:j + 1], in_=lp[:, j * 128 + 127:j * 128 + 128], func=AF.Exp, R=[lp], W=[PLt])
                yield
                tv = lambda t: t[:].rearrange("p (j t) -> p j t", j=NJ)
                S.I("pool", "tensor_tensor", out=arr[:, :, 1, :], in0=tv(r_sb), in1=tv(E1), op=ALU.mult, R=[r_sb, E1], W=[arr])
                S.I("dve", "scalar_tensor_tensor", out=arr[:, :, 0, :], in0=tv(kk), scalar=-1.0, in1=tv(Ep), op0=ALU.mult, op1=ALU.mult, R=[kk, Ep], W=[arr])
                yield
                S.I("pool", "tensor_tensor", out=bt[:], in0=bvec[:], in1=Em[:], op=ALU.mult, R=[bvec, Em], W=[bt])
                S.I("dve", "tensor_tensor", out=kt[:], in0=kmod[:], in1=Em[:], op=ALU.mult, R=[kmod, Em], W=[kt])
                yield
                S.I("pool", "tensor_tensor", out=bh[:], in0=bvec[:], in1=Eh[:], op=ALU.mult, R=[bvec, Eh], W=[bh])
                S.I("dve", "tensor_tensor", out=kh[:], in0=kmod[:], in1=Eh[:], op=ALU.mult, R=[kmod, Eh], W=[kh])
                yield
                S.I("pool", "tensor_copy", vbf[:], v_sb[:], R=[v_sb], W=[vbf])
                S.I("dve", "scalar_tensor_tensor", out=prod[:], in0=r_sb[:], scalar=V("rk", c), in1=kmod[:], op0=ALU.mult, op1=ALU.mult, R=[r_sb, kmod, self.vec], W=[prod])
                pbs2 = self.bank()
                S.I("pe", "matmul", pbs2[:, 0:TR], lhsT=bd1, rhs=prod[:], start=True, stop=True, R=[prod, self.cb], W=[pbs2])
                S.I("act", "copy", bonus[:], pbs2[:, 0:TR], R=[pbs2], W=[bonus])
                def stream(j, w):
                    js = slice(j * 128, (j + 1) * 128)
                    AX, BKV, Vpad, AA, N0, QN, Z, Ap, UV, UVpad, GT, RT = [w[k] for k in ("AX", "BKV", "Vpad", "AA", "N0", "QN", "Z", "Ap", "UV", "UVpad", "GT", "RT")]
                    Q0, Zfin = w["Q0"], w["Zfin"]
                    identf = self.cm[:, CONST_ORDER.index("ident"), :]
                    par = (ti * NJ + j) % 2
                    Mfc = Mf[c]
                    Mb0, Mb1 = Mb[c][par], Mb[c][1 - par]
                    pt = self.bank()
                    ptb = pt[:].bitcast(BF16)
                    srcs = [(arr, arr[:, j, 0, :]), (bh, bh[:, js]), (kh, kh[:, js]), (vbf, vbf[:, js])]
                    for i, (tl, ap) in enumerate(srcs):
                        S.I("pe", "transpose", ptb[:, i * 128:(i + 1) * 128], ap, identb, R=[tl, self.cb], W=[pt], inc=(i == 3))
                    if True:
                        S.I("dve", "tensor_copy", AX[:, :, 0:64], ptb[:, 0:128].rearrange("p (e k) -> p e k", e=2), R=[pt], W=[AX])
                    S.I("dve", "tensor_copy", BKV[:].rearrange("p a b -> p (a b)"), ptb[:, 128:512], R=[pt], W=[BKV])
                    if True:
                        S.I("dve", "tensor_copy", diag(Vpad), ptb[:, 384:512].rearrange("p (e k) -> p e k", e=2), R=[pt], W=[Vpad])
                    yield
                    Eb = [self.bank(), self.bank()]
                    Nb = [self.bank(), self.bank()]
                    for e_ in range(2):
                        hs = slice(64 * e_, 64 * e_ + 64)
                        arj = arr[hs, j, :, :].rearrange("p a b -> p (a b)")
                        S.I("pe", "matmul", Eb[e_][:, 0:256], lhsT=bt[hs, js], rhs=arj, start=True, stop=True, R=[bt, arr], W=[Eb[e_]], inc=False)
                        S.I("pe", "matmul", Eb[e_][:, 256:512], lhsT=kt[hs, js], rhs=arj, start=True, stop=True, R=[kt, arr], W=[Eb[e_]])
                        S.I("pe", "matmul", Nb[e_][:, 0:128], lhsT=arr[hs, j, 0, :], rhs=bt[hs, js], start=True, stop=True, R=[bt, arr], W=[Nb[e_]])
                        if True:
                            S.I("dve", "tensor_tensor", out=AA[:, e_, :, :], in0=Eb[e_][:, :].rearrange("p (a b) -> p a b", a=4), in1=mask4[:], op=ALU.mult, R=[Eb[e_], mask4], W=[AA])
                        if True:
                            S.I("dve", "tensor_tensor", out=N0[:, e_, :], in0=Nb[e_][:, 0:128], in1=msl, op=ALU.mult, R=[Nb[e_], self.cm], W=[N0])
                        if True:
                            S.I("dve", "tensor_tensor", out=Q0[:, e_, :], in0=Eb[e_][:, 0:128], in1=mask4[:, 0, :], op=ALU.mult, R=[Eb[e_], mask4], W=[Q0])
                            S.I("pool", "tensor_tensor", out=Z[0][:, e_, :], in0=Q0[:, e_, :], in1=identf, op=ALU.add, R=[Q0, self.cm], W=[Z[0]])
                    yield
                    for lv in range(1, 7):
                        qn_prev = QN[(lv - 1) % 2]
                        qn = QN[lv % 2]
                        B = self.bank()
                        for e_ in range(2):
                            Qp = Q0[:, e_, :] if lv == 1 else qn_prev[:, e_, 0, :]
                            Np = N0[:, e_, :] if lv == 1 else qn_prev[:, e_, 1, :]
                            Rr = [Q0, N0] if lv == 1 else [qn_prev]
                            if lv < 6:
                                S.I("pe", "matmul", B[:, e_ * 256:e_ * 256 + 128], lhsT=Np, rhs=Qp, start=True, stop=True, R=Rr, W=[B], inc=False)
                            S.I("pe", "matmul", B[:, e_ * 256 + 128:e_ * 256 + 256], lhsT=Qp, rhs=Np, start=True, stop=True, R=Rr, W=[B], inc=(e_ == 1))
                        if lv < 6:
                            evac_copy(qn[:].rearrange("p a b c -> p (a b c)"), B[:, :], [B], [qn])
                        else:
                            evac_copy(qn[:, :, 1, :], B[:, :].rearrange("p (a b c) -> p a b c", a=2, b=2)[:, :, 1, :], [B], [qn])
                        yield
                        Zb = self.bank()
                        zp, zn = Z[(lv - 1) % 2], (Z[lv % 2] if lv < 6 else Zfin)
                        for e_ in range(2):
                            S.I("pe", "matmul", Zb[:, e_ * 128:(e_ + 1) * 128], lhsT=qn[:, e_, 1, :], rhs=zp[:, e_, :], start=True, stop=True, R=[qn, zp], W=[Zb], inc=(e_ == 1))
                        S.I("dve", "tensor_tensor", out=zn[:].rearrange("p a b -> p (a b)"), in0=Zb[:, 0:256], in1=zp[:].rearrange("p a b -> p (a b)"), op=ALU.add, R=[Zb, zp], W=[zn])
                    Zf = Zfin
                    yield
                    Xb = self.bank()
                    for e_ in range(2):
                        S.I("pe", "matmul", Xb[:, e_ * 128:(e_ + 1) * 128], lhsT=AA[:, e_, 2, :], rhs=BKV[:, 2, :], start=True, stop=True, R=[AA, BKV], W=[Xb], inc=(e_ == 1))
                    S.I("dve", "tensor_copy", AX[:, :, 64:128], diag(Xb), R=[Xb], W=[AX])
                    yield
                    Ub = self.bank()
                    for e_ in range(2):
                        S.I("pe", "matmul", Ub[:, e_ * 128:(e_ + 1) * 128], lhsT=Zf[:, e_, :], rhs=AX[:, e_, :], start=True, stop=True, R=[Zf, AX], W=[Ub], inc=(e_ == 1))
                    S.I("dve", "tensor_copy", Ap[:].rearrange("p (e k) -> p e k", e=2), strided(Ub, 0, 128), R=[Ub], W=[Ap])
                    S.I("dve", "tensor_copy", UV[:].rearrange("p (e k) -> p e k", e=2), strided(Ub, 64, 128), R=[Ub], W=[UV])
                    S.I("dve", "tensor_copy", diag(UVpad), strided(Ub, 64, 128), R=[Ub], W=[UVpad])
                    yield
                    Gb = self.bank()
                    S.I("pe", "matmul", Gb[:, 0:128], lhsT=Ap[:], rhs=BKV[:, 0, :], start=True, stop=True, R=[Ap, BKV], W=[Gb])
                    S.I("dve", "tensor_tensor", out=GT[:], in0=Gb[:, 0:128], in1=bd1f, op=ALU.mult, R=[Gb, self.cm], W=[GT])
                    Rb = self.bank()
                    for e_ in range(2):
                        S.I("pe", "matmul", Rb[:, e_ * 128:(e_ + 1) * 128], lhsT=Ap[:], rhs=AA[:, e_, 1, :], start=True, stop=True, R=[Ap, AA], W=[Rb], inc=(e_ == 1))
                    for e_ in range(2):
                        hs = slice(64 * e_, 64 * e_ + 64)
                        S.I("dve", "tensor_tensor", out=RT[hs, :], in0=Rb[hs, e_ * 128:(e_ + 1) * 128], in1=arr[hs, j, 1, :], op=ALU.add, R=[Rb, arr], W=[RT])
                    yield
                    Yb = self.bank()
                    ymm = [(UVpad[:, 0, :], AA[:, 0, 1, :]), (UVpad[:, 1, :], AA[:, 1, 1, :]), (Vpad[:, 0, :], AA[:, 0, 3, :]), (Vpad[:, 1, :], AA[:, 1, 3, :]), (Mb0[:], RT[:])]
                    for i, (l_, r_) in enumerate(ymm):
                        S.I("pe", "matmul", Yb[:, 0:128], lhsT=l_, rhs=r_, start=(i == 0), stop=(i == 4), R=[UVpad, Vpad, AA, Mb0, RT], W=[Yb], inc=(i == 4))
                    S.I("act", "copy", ysb[:, js], Yb[:, 0:128], R=[Yb], W=[ysb])
                    Mbk = self.bank()
                    cmm = [(BKV[:, 0, :], UV[:]), (BKV[:, 1, :], BKV[:, 2, :]), (GT[:], Mb0[:])]
                    for i, (l_, r_) in enumerate(cmm):
                        S.I("pe", "matmul", Mbk[:, 0:128], lhsT=l_, rhs=r_, start=(i == 0), stop=(i == 2), R=[BKV, UV, GT, Mb0], W=[Mbk], inc=(i == 2))
                    for e_ in range(2):
                        hs = slice(64 * e_, 64 * e_ + 64)
                        S.I("dve", "scalar_tensor_tensor", out=Mfc[hs, :], in0=Mfc[hs, :], scalar=PLt[hs, c, j:j + 1], in1=Mbk[hs, hs], op0=ALU.mult, op1=ALU.add, R=[Mbk, Mfc, PLt], W=[Mfc])
                        S.I("pool", "tensor_copy", Mb1[hs, hs], Mfc[hs, :], R=[Mfc], W=[Mb1])
                def epilogue():
                    ybf, yc, ysq, rs = epb[0], epf[0], epb[1], epf[1]
                    S.I("pool", "tensor_copy", ybf[:], ysb[:], R=[ysb], W=[ybf])
                    pg = self.bank()
                    S.I("pe", "matmul", pg[:, 0:TR], lhsT=bd64, rhs=ybf[:], start=True, stop=True, R=[ybf, self.cb], W=[pg])
                    S.I("dve", "tensor_tensor", out=yc[:], in0=ysb[:], in1=pg[:, 0:TR], op=ALU.subtract, R=[pg, ysb], W=[yc])
                    S.I("act", "activation", out=ysq[:], in_=yc[:], func=AF.Square, R=[yc], W=[ysq])
                    S.I("pe", "matmul", pg[:, 256:256 + TR], lhsT=bd64, rhs=ysq[:], start=True, stop=True, R=[ysq, self.cb], W=[pg])
                    self.rsqrt(rs[:], pg[:, 256:256 + TR], 64e-5, rs, pg)
                    yield
                    S.I("pool", "tensor_tensor", out=yc[:], in0=yc[:], in1=rs[:], op=ALU.mult, R=[yc, rs], W=[yc])
                    S.I("act", "activation", out=yc[:], in_=yc[:], func=AF.Identity, bias=V("gnb", c), scale=V("gng", c), R=[yc, self.vec], W=[yc])
                    yield
                    S.I("pool", "tensor_tensor", out=bonus[:], in0=bonus[:], in1=v_sb[:], op=ALU.mult, R=[bonus, v_sb], W=[bonus])
                    S.I("dve", "tensor_tensor", out=yc[:], in0=yc[:], in1=bonus[:], op=ALU.add, R=[yc, bonus], W=[yc])
                    yield
                    pgg = self.bank()
                    S.I("pe", "matmul", pgg[:, 0:TR], lhsT=g2[:, 0, cs], rhs=sg[:, 0, :], start=True, stop=False, R=[g2, sg], W=[pgg], inc=False)
                    S.I("pe", "matmul", pgg[:, 0:TR], lhsT=g2[0:32, 1, cs], rhs=sg[0:32, 1, :], start=False, stop=True, R=[g2, sg], W=[pgg])
                    S.I("dve", "tensor_tensor", out=ygb[:, c, :], in0=yc[:], in1=pgg[:, 0:TR], op=ALU.mult, R=[pgg, yc], W=[ygb])
                holder["stream"] = stream
                holder["epilogue"] = epilogue

            def rr(gens):
                gens = list(gens)
                while gens:
                    for g_ in list(gens):
                        try:
                            next(g_)
                        except StopIteration:
                            gens.remove(g_)

            def delayed(g_, n_):
                for _ in range(n_):
                    yield
                yield from g_

            def chain(*gs):
                for g_ in gs:
                    yield from g_

            holders = {}
            ready = set()
            done = {cp: 0 for cp in range(4)}

            def pro(cp):
                hs = [{}, {}]
                holders[cp] = hs
                yield from make_c(2 * cp, (cp % 2) * 2, hs[0])
                yield from make_c(2 * cp + 1, (cp % 2) * 2 + 1, hs[1])
                ready.add(cp)

            def aux_gen():
                yield from pro(0)
                yield from pro(1)
                for p_ in range(4):
                    while done[p_] < 4:
                        yield
                    yield from holders[p_][0]["epilogue"]()
                    yield from holders[p_][1]["epilogue"]()
                    if p_ + 2 < 4:
                        yield from pro(p_ + 2)

            SD = int(os.environ.get("RW_STAGGER", "1"))
            pend = []
            for cp in range(4):
                pend += [(cp, 0, 0), (cp, 1, 0), (cp, 0, 1), (cp, 1, 1)]
            slots = [None] * 4
            aux = aux_gen()
            aux_alive = True
            cycle = 0
            last_start = -SD
            while pend or any(x is not None for x in slots) or aux_alive:
                if pend and cycle - last_start >= SD and pend[0][0] in ready:
                    for k_ in range(4):
                        if slots[k_] is None:
                            cp, hi, j_ = pend.pop(0)
                            slots[k_] = (holders[cp][hi]["stream"](j_, ws[k_]), cp)
                            last_start = cycle
                            break
                for k_ in range(4):
                    if slots[k_] is not None:
                        try:
                            next(slots[k_][0])
                        except StopIteration:
                            done[slots[k_][1]] += 1
                            slots[k_] = None
                if aux_alive:
                    try:
                        next(aux)
                    except StopIteration:
                        aux_alive = False
                cycle += 1
                assert cycle < 100000
            for co in range(8):
                pb = self.bank() if co % 2 == 0 else pb
                col = (co % 2) * 256
                for kc in range(8):
                    S.I("pe", "matmul", pb[:, col:col + TR], lhsT=wo[:, kc, co * 128:(co + 1) * 128], rhs=ygb[:, kc, :], start=(kc == 0), stop=(kc == 7),
                         R=[wo, ygb], W=[pb], inc=(kc == 7))
                S.I("dve", "tensor_tensor", out=x[:, co, :], in0=pb[:, col:col + TR], in1=x[:, co, :], op=ALU.add, R=[pb, x], W=[x])
            S.I("sp", "dma_start", out=self.xview(d["yT"], t0, TR), in_=x[:], R=[x], W=[xd], dma="xst")


def _rms_rstd_n(self, x, sq, rstd, n):
    S = self.S
    S.I("act", "activation", out=sq[:], in_=x[:], func=AF.Square, R=[x], W=[sq])
    pb = self.bank()
    om = self.cmat("onesm")
    for c in range(NC_):
        S.I("pe", "matmul", pb[:, 0:n], lhsT=om, rhs=sq[:, c, :], start=(c == 0), stop=(c == NC_ - 1),
             R=[sq, self.cb], W=[pb], inc=(c == NC_ - 1))
    self.rsqrt(rstd[:, 0:n], pb[:, 0:n], 1e-6, rstd, pb)


def _rsqrt(self, out_ap, in_ap, eps, out_tl, in_tl):
    S = self.S
    S.I("act", "activation", out=out_ap, in_=in_ap, func=AF.Ln, bias=self.epsc(eps), R=[in_tl, self.epst], W=[out_tl])
    S.I("act", "activation", out=out_ap, in_=out_ap, func=AF.Exp, scale=-0.5, R=[out_tl], W=[out_tl])


def _epsc(self, eps):
    return self.epst[:, self.eps_list.index(eps):self.eps_list.index(eps) + 1]


Builder.rsqrt = _rsqrt
Builder.epsc = _epsc
Builder.phase_rwkv = _phase_rwkv
Builder.rms_rstd_n = _rms_rstd_n


def _phase_kv(self):
    nc, S, d = self.nc, self.S, self.d
    g = self.kvstack
    T = self.T
    self.kTp = self.sb(g, "kTp", [128, 2, 2, T], BF16)
    self.Vt = self.sb(g, "Vt", [128, T // 128, 256], BF16)
    S.I("pool", "memset", self.kTp[:].rearrange("p a b t -> p (a b t)"), 0.0, W=[self.kTp])
    with ExitStack() as st:
        sb = lambda name, shape, dt: self.sb(st, name, shape, dt)
        wkv = sb("wkv", [128, 8, 256], BF16)
        self.load_cast(wkv, wkv[:], d["w_kv"].rearrange("(kc p) n -> p kc n", p=128))
        wkd = sb("wkd", [128, 2, 8, 128], BF16)
        wvd = sb("wvd", [128, 8, 256], BF16)
        for g_ in range(2):
            for dup in range(2):
                S.I("pool", "tensor_copy", wkd[:, g_, :, dup * 64:(dup + 1) * 64], wkv[:, :, g_ * 64:(g_ + 1) * 64], R=[wkv], W=[wkd])
                S.I("pool", "tensor_copy", wvd[:, :, g_ * 128 + dup * 64:g_ * 128 + (dup + 1) * 64], wkv[:, :, 128 + g_ * 64:128 + (g_ + 1) * 64], R=[wkv], W=[wvd])
        x = sb("x", [128, 8, TT], F32)
        sq = sb("sq", [128, 8, TT], BF16)
        rstd = sb("rstd", [128, TT], F32)
        xn = sb("xn", [128, 8, TT], BF16)
        k_sb = sb("k_sb", [128, TT], F32)
        kq = sb("kq", [128, TT], BF16)
        rk = sb("rk", [128, TT], F32)
        bd64 = self.cmat("bd64")
        for ti in range(self.NT):
            t0 = ti * TT
            S.I("sp", "dma_start", out=x[:], in_=self.xview(d["yT"], t0), R=[self.XD[ti]], W=[x], dma="kvx")
            self.rms_rstd_n(x, sq, rstd, TT)
            for c in range(8):
                S.I("dve", "scalar_tensor_tensor", out=xn[:, c, :], in0=x[:, c, :], scalar=self.vcol("kvn", c), in1=rstd[:], op0=ALU.mult, op1=ALU.mult, R=[x, rstd, self.vec], W=[xn])
            for g_ in range(2):
                pb = self.bank()
                for kc in range(8):
                    S.I("pe", "matmul", pb[:, :], lhsT=wkd[:, g_, kc, :], rhs=xn[:, kc, :], start=(kc == 0), stop=(kc == 7), R=[wkd, xn], W=[pb], inc=(kc == 7))
                S.I("act", "copy", k_sb[:], pb[:, :], R=[pb], W=[k_sb])
                S.I("act", "activation", out=kq[:], in_=k_sb[:], func=AF.Square, R=[k_sb], W=[kq])
                pb2 = self.bank()
                S.I("pe", "matmul", pb2[:, :], lhsT=bd64, rhs=kq[:], start=True, stop=True, R=[kq, self.cb], W=[pb2])
                self.rsqrt(rk[:], pb2[:, :], 1e-6, rk, pb2)
                S.I("pool", "tensor_tensor", out=k_sb[:], in0=k_sb[:], in1=rk[:], op=ALU.mult, R=[rk, k_sb], W=[k_sb])
                for e_ in range(2):
                    hs = slice(64 * e_, 64 * e_ + 64)
                    S.I("dve", "tensor_scalar", out=self.kTp[hs, e_, g_, t0:t0 + TT], in0=k_sb[hs, :], scalar1=self.hvec[hs, 2:3], scalar2=None, op0=ALU.mult, R=[k_sb, self.hvec], W=[self.kTp])
            for blk in range(4):
                pb = self.bank()
                for kc in range(8):
                    S.I("pe", "matmul", pb[:, 0:256], lhsT=xn[:, kc, blk * 128:(blk + 1) * 128], rhs=wvd[:, kc, :], start=(kc == 0), stop=(kc == 7), R=[wvd, xn], W=[pb], inc=(kc == 7))
                S.I("act", "copy", self.Vt[:, ti * 4 + blk, :], pb[:, 0:256], R=[pb], W=[self.Vt])


def _phase_attn(self, j, i):
    nc, S, d = self.nc, self.S, self.d
    with ExitStack() as st:
        sb = lambda name, shape, dt: self.sb(st, name, shape, dt)
        cview = lambda ap: ap.rearrange("(kc p) n -> p kc n", p=128)
        wq = sb("wq", [128, 8, 1024], BF16)
        wo = sb("wo", [128, 8, 1024], BF16)
        self.load_cast(wq, wq[:], cview(d["b_wq"][j]))
        self.load_cast(wo, wo[:], cview(d["b_wo"][j]))
        alibi = sb("alibi", [128, 2, 2, 8, 128], F32)
        S.I("sp", "dma_start", out=alibi[:], in_=d["alibi"], W=[alibi], dma="alibi")
        sk = sb("sk", [1, 2, 2, 512], F32)
        skb = sb("skb", [1, 2, 2, 512], BF16)
        S.I("sp", "dma_start", out=sk[:], in_=d["sinkrow"][:, j].rearrange("o e g h q -> o e g (h q)"), W=[sk], dma="sk")
        S.I("act", "activation", out=skb[:].rearrange("o e g n -> o (e g n)"), in_=sk[:].rearrange("o e g n -> o (e g n)"), func=AF.Exp, R=[sk], W=[skb])
        qg = sb("qg", [128, 1], F32)
        S.I("dve", "tensor_scalar", out=qg[:], in0=self.hvec[:, j:j + 1], scalar1=0.125, scalar2=None, op0=ALU.mult, R=[self.hvec], W=[qg])
        x = sb("x", [128, 8, TT], F32)
        rstd = sb("rstd", [128, TT], F32)
        h = sb("h", [128, 8, TT], BF16)
        q_all = sb("q_all", [128, 8, TT], F32)
        qq_all = sb("qq_all", [128, 8, TT], BF16)
        sq = qq_all
        rqs = [sb(f"rq{k}", [128, TT], F32) for k in range(2)]
        qT = sb("qT", [128, 8, TT], BF16)
        attT = sb("attT", [128, 8, TT], BF16)
        sc = [sb(f"sc{k}", [128, 512], F32) for k in range(4)]
        P = [sb(f"P{k}", [128, 512], BF16) for k in range(4)]
        rec = [sb(f"rec{k}", [128, 512], F32) for k in range(2)]
        bd64 = self.cmat("bd64")
        ones1 = self.cmat("ones1")
        for ti in range(self.NT):
            t0 = ti * TT
            S.I("sp", "dma_start", out=x[:], in_=self.xview(d["yT"], t0), R=[self.XD[ti]], W=[x], dma="atx")
            self.rms_rstd_n(x, sq, rstd, TT)
            for c in range(8):
                S.I("dve", "scalar_tensor_tensor", out=h[:, c, :], in0=x[:, c, :], scalar=self.vcol(f"ln_mix{i}", c), in1=rstd[:], op0=ALU.mult, op1=ALU.mult, R=[x, rstd, self.vec], W=[h])
            for co in range(8):
                pb = self.bank()
                for kc in range(8):
                    S.I("pe", "matmul", pb[:, :], lhsT=wq[:, kc, co * 128:(co + 1) * 128], rhs=h[:, kc, :], start=(kc == 0), stop=(kc == 7), R=[wq, h], W=[pb], inc=(kc == 7))
                S.I("act", "copy", q_all[:, co, :], pb[:, :], R=[pb], W=[q_all])
                S.I("act", "activation", out=qq_all[:, co, :], in_=q_all[:, co, :], func=AF.Square, R=[q_all], W=[qq_all])
            for co in range(8):
                rq_ = rqs[co % 2]
                pb2 = self.bank()
                S.I("pe", "matmul", pb2[:, :], lhsT=bd64, rhs=qq_all[:, co, :], start=True, stop=True, R=[qq_all, self.cb], W=[pb2])
                self.rsqrt(rq_[:], pb2[:, :], 1e-6, rq_, pb2)
                S.I("pool", "tensor_tensor", out=rq_[:], in0=q_all[:, co, :], in1=rq_[:], op=ALU.mult, R=[rq_, q_all], W=[rq_])
                S.I("dve", "tensor_scalar", out=qT[:, co, :], in0=rq_[:], scalar1=qg[:, 0:1], scalar2=None, op0=ALU.mult, R=[rq_, qg], W=[qT])
            items = [(blk, e_, g_) for blk in range(4) for e_ in range(2) for g_ in range(2)]

            def stageA(it, slot):
                blk, e_, g_ = it
                n = ti * 4 + blk
                bs = slice(blk * 128, (blk + 1) * 128)
                srcs = [(1, n)] + ([(0, n - 1)] if n > 0 else [])
                for k, (which, kb) in enumerate(srcs):
                    pbS = self.bank()
                    sck = sc[(self._sci) % len(sc)]
                    self._sci += 1
                    Pk = P[slot * 2 + k]
                    S.I("pe", "matmul", pbS[:, :], lhsT=self.kTp[:, e_, g_, kb * 128:(kb + 1) * 128], rhs=qT[:, 4 * g_:4 * g_ + 4, bs], start=True, stop=True, R=[self.kTp, qT], W=[pbS])
                    S.I("dve", "tensor_tensor", out=sck[:].rearrange("p (h q) -> p h q", h=4), in0=pbS[:, :].rearrange("p (h q) -> p h q", h=4), in1=alibi[:, which, e_, 4 * g_:4 * g_ + 4, :], op=ALU.add, R=[pbS, alibi], W=[sck])
                    S.I("act", "activation", out=Pk[:], in_=sck[:], func=AF.Exp, R=[sck], W=[Pk])
                return srcs

            def stageB(it, slot, srcs):
                blk, e_, g_ = it
                bs = slice(blk * 128, (blk + 1) * 128)
                hs = slice(64 * e_, 64 * e_ + 64)
                pbO = self.bank()
                pbD = self.bank()
                for k, (which, kb) in enumerate(srcs):
                    Pk = P[slot * 2 + k]
                    S.I("pe", "matmul", pbO[:, :], lhsT=self.Vt[:, kb, g_ * 128:(g_ + 1) * 128], rhs=Pk[:], start=(k == 0), stop=(k == len(srcs) - 1), R=[self.Vt, Pk], W=[pbO], inc=(k == len(srcs) - 1))
                for k, (which, kb) in enumerate(srcs):
                    Pk = P[slot * 2 + k]
                    S.I("pe", "matmul", pbD[:, :], lhsT=ones1, rhs=Pk[:], start=(k == 0), stop=False, R=[self.cb, Pk], W=[pbD], inc=False)
                S.I("pe", "matmul", pbD[:, :], lhsT=ones1[0:1, :], rhs=skb[0:1, e_, g_, :], start=False, stop=True, R=[self.cb, skb], W=[pbD])
                rc = rec[slot]
                S.I("act", "activation", out=rc[:], in_=pbD[:, :], func=AF.Ln, R=[pbD], W=[rc])
                S.I("act", "activation", out=rc[:], in_=rc[:], func=AF.Exp, scale=-1.0, R=[rc], W=[rc])
                S.I("dve", "tensor_tensor", out=attT[hs, 4 * g_:4 * g_ + 4, bs], in0=pbO[hs, :].rearrange("p (h q) -> p h q", h=4), in1=rc[hs, :].rearrange("p (h q) -> p h q", h=4), op=ALU.mult, R=[pbO, rc], W=[attT])

            self._sci = 0
            prev = None
            for idx, it in enumerate(items):
                srcs = stageA(it, idx % 2)
                if prev is not None:
                    stageB(*prev)
                prev = (it, idx % 2, srcs)
            stageB(*prev)
            for co in range(8):
                pb = self.bank()
                for kc in range(8):
                    S.I("pe", "matmul", pb[:, :], lhsT=wo[:, kc, co * 128:(co + 1) * 128], rhs=attT[:, kc, :], start=(kc == 0), stop=(kc == 7), R=[wo, attT], W=[pb], inc=(kc == 7))
                S.I("dve", "tensor_tensor", out=x[:, co, :], in0=pb[:, :], in1=x[:, co, :], op=ALU.add, R=[pb, x], W=[x])
            S.I("sp", "dma_start", out=self.xview(d["yT"], t0), in_=x[:], R=[x], W=[self.XD[ti]], dma="xst")


Builder.phase_kv = _phase_kv
Builder.phase_attn = _phase_attn
```

```python
import os
import numpy as np
from contextlib import ExitStack
import concourse.bass as bass
import concourse.mybir as mybir
from concourse.bass_utils import run_bass_kernel_spmd

F32 = mybir.dt.float32
BF16 = mybir.dt.bfloat16
F32R = mybir.dt.float32r
ALU = mybir.AluOpType
AF = mybir.ActivationFunctionType

C = 1024
NC_ = 8
TT = 512
FF = 4096
ENGS = ("pe", "act", "dve", "pool", "sp")


class Buf:
    __slots__ = ("name", "w", "r")

    def __init__(self, name):
        self.name = name
        self.w = None
        self.r = {}


class Tl:
    def __init__(self, h, name):
        self.h = h
        self.b = Buf(name)

    def __getitem__(self, k):
        return self.h[k]


class Sched:
    def __init__(self, nc):
        self.nc = nc
        self.q = {e: [] for e in ENGS}
        self.cnt = {}
        self.known = {e: {} for e in ENGS}
        self.pending = {e: False for e in ENGS}
        self.sems = {}
        for e in ENGS:
            self.sems[e] = nc.alloc_semaphore("sem_" + e)
        self.nins = 0
        self.dman = {}

    def _sem(self, k):
        if k not in self.sems:
            self.sems[k] = self.nc.alloc_semaphore(k.replace(":", "_"))
        return self.sems[k]

    def _need(self, eng, waits, tok):
        if tok is None:
            return
        k, v = tok
        if k == eng and eng in ("pe", "sp"):
            return
        if self.known[eng].get(k, 0) >= v:
            return
        if waits.get(k, 0) < v:
            waits[k] = v

    def op(self, eng, fn, R=(), W=(), inc=True, dma=None):
        waits = {}
        for t in R:
            self._need(eng, waits, t.b.w)
        for t in W:
            self._need(eng, waits, t.b.w)
            for k, v in t.b.r.items():
                self._need(eng, waits, (k, v))
        if dma is not None:
            K = 24 if eng == "sp" else 12
            n_ = self.dman.get(eng, 0)
            self.dman[eng] = n_ + 1
            key = f"dma:{eng}{n_ % K}"
            self._sem(key)
            if self.cnt.get(key, 0):
                self._need(eng, waits, (key, self.cnt[key]))
        for k, v in waits.items():
            self.known[eng][k] = v
        self.nins += 1
        if dma is not None:
            self.cnt[key] = self.cnt.get(key, 0) + 16
            tok = (key, self.cnt[key])
            self.q[eng].append((waits, fn, key, 16))
        elif inc:
            self.cnt[eng] = self.cnt.get(eng, 0) + 1
            tok = (eng, self.cnt[eng])
            self.pending[eng] = False
            self.q[eng].append((waits, fn, eng, 1))
        else:
            tok = (eng, self.cnt.get(eng, 0) + 1)
            self.pending[eng] = True
            self.q[eng].append((waits, fn, None, 0))
        for t in R:
            k, v = tok
            if t.b.r.get(k, 0) < v:
                t.b.r[k] = v
        for t in W:
            t.b.w = tok
            t.b.r = {}
        return tok

    def I(self, eng, meth, *args, R=(), W=(), inc=True, dma=None, **kw):
        return self.op(eng, lambda e: getattr(e, meth)(*args, **kw), R=R, W=W, inc=inc, dma=dma)

    def barrier(self):
        for e in ENGS:
            waits = {}
            for k, v in self.cnt.items():
                self._need(e, waits, (k, v))
            for k, v in waits.items():
                self.known[e][k] = v
            if waits:
                self.q[e].append((waits, None, None, 0))

    def flush(self, final=False):
        nc = self.nc
        for e in ENGS:
            assert not self.pending[e], e
        sems = self.sems
        q = self.q
        self.q = {e: [] for e in ENGS}
        cnt = dict(self.cnt)
        with nc.Block() as block:
            def run(engkey, engobj):
                for waits, fn, inck, incv in q[engkey]:
                    for k, v in waits.items():
                        engobj.wait_ge(sems[k], v)
                    if fn is None:
                        continue
                    ins = fn(engobj)
                    if inck is not None:
                        ins.then_inc(sems[inck], incv)
                if final and engkey == "sp":
                    for k, v in cnt.items():
                        if k.startswith("dma:"):
                            engobj.wait_ge(sems[k], v)

            @block.tensor
            def _(e):
                run("pe", e)

            @block.scalar
            def _(e):
                run("act", e)

            @block.vector
            def _(e):
                run("dve", e)

            @block.gpsimd
            def _(e):
                run("pool", e)

            @block.sync
            def _(e):
                run("sp", e)


VEC_NAMES = []


def _vec_index():
    names = []
    for i in range(4):
        names += [f"ln_mix{i}", f"ln_mlp{i}"]
    for j in range(2):
        names += [f"mu{j}_{n}" for n in range(6)]
        names += [f"w0_{j}", f"a0_{j}", f"kk_{j}", f"ka_{j}", f"rk_{j}", f"gng_{j}", f"gnb_{j}"]
    names += ["v0", "kvn"]
    return {n: i for i, n in enumerate(names)}


VIDX = _vec_index()
NV = len(VIDX)


def make_consts():
    c = {}
    c["ident"] = np.eye(128, dtype=np.float32)
    s = np.arange(128)[:, None]
    t = np.arange(128)[None, :]
    su = (s < t).astype(np.float32)
    ule = (s <= t).astype(np.float32)
    sl = (s > t).astype(np.float32)
    c["m_abrb"] = np.concatenate([su, ule], 1)
    c["m_akrk"] = np.concatenate([su, ule], 1)
    c["m_sl"] = sl
    bd = np.zeros((128, 128), np.float32)
    bd[:64, :64] = 1
    bd[64:, 64:] = 1
    c["bd1"] = bd
    c["bd64"] = bd / 64.0
    c["onesm"] = np.full((128, 128), 1.0 / 1024.0, np.float32)
    c["ones1"] = np.ones((128, 128), np.float32)
    slopes = np.exp2(-8.0 * np.arange(1, 17, dtype=np.float32) / 16.0).astype(np.float32)
    j = np.arange(128)[:, None].astype(np.float32)
    i = np.arange(128)[None, :].astype(np.float32)
    dcur = i - j
    dprev = i + 128 - j
    bc = np.where(dcur[:, None, :] >= 0, -slopes[None, :, None] * dcur[:, None, :], -30000.0)
    bp = np.where(dprev[:, None, :] < 128, -slopes[None, :, None] * dprev[:, None, :], -30000.0)
    bc = bc.reshape(128, 8, 2, 128).transpose(0, 2, 1, 3)
    bp = bp.reshape(128, 8, 2, 128).transpose(0, 2, 1, 3)
    c["alibi"] = np.ascontiguousarray(np.stack([bp, bc], 1)).astype(np.float32)
    return c


CONST_ORDER = ["ident", "m_abrb", "m_sl", "bd1", "bd64", "onesm", "ones1"]


class Builder:
    def __init__(self, T, phases, dbg=None):
        self.T = T
        self.NT = T // TT
        self.phases = phases
        self.dbg = dbg or {}
        nc = self.nc = bass.Bass("TRN2", target_bir_lowering=False)
        self.S = Sched(nc)
        d = self.d = {}

        def din(name, shape, dt=F32):
            d[name] = nc.dram_tensor(name, list(shape), dt, kind="ExternalInput").ap()

        din("xT", [C, T])
        din("vecs", [128, NV, 8])
        din("hvec", [128, 8])
        din("sinkrow", [1, 2, 2, 2, 4, 128])
        din("cmat", [128, len(CONST_ORDER) + 1, 128])
        din("alibi", [128, 2, 2, 8, 128])
        din("mlp_w1", [4, C, FF])
        din("mlp_w2", [4, FF, C])
        din("a_w_rkv", [2, 3, C, C])
        din("a_w1", [2, C, 64])
        din("a_w2", [2, 64, C])
        din("a_a1", [2, C, 64])
        din("a_a2", [2, 64, C])
        din("a_g1", [2, C, 160])
        din("a_g2", [2, 160, C])
        din("a_wo", [2, C, C])
        din("a_v1", [1, C, 32])
        din("a_v2", [1, 32, C])
        din("w_kv", [C, 256])
        din("b_wq", [2, C, C])
        din("b_wo", [2, C, C])
        d["yT"] = nc.dram_tensor("yT", [C, T], F32, kind="ExternalOutput").ap()
        d["vfirst"] = nc.dram_tensor("vfirst", [C, T], F32, kind="Internal").ap()
        d["wrkv_bf"] = nc.dram_tensor("wrkv_bf", [2, 3, C, C], BF16, kind="Internal").ap()
        self.WBF = Tl(None, "wrkv_bf")
        for name, shape in self.dbg.items():
            d[name] = nc.dram_tensor(name, list(shape), F32, kind="ExternalOutput").ap()
        self.XD = [Tl(None, f"xres_dram{i}") for i in range(self.NT)]
        self.VFD = Tl(None, "vfirst_dram")
        self.DBG = Tl(None, "dbg")
        self.bank_i = 0
        self.dmaq = 0

    def sb(self, st, name, shape, dt):
        self.uid = getattr(self, "uid", 0) + 1
        name = f"s{self.uid}_{name}"
        h = st.enter_context(self.nc.sbuf_tensor(name, list(shape), dt))
        return Tl(h, name)

    def bank(self):
        b = self.ps[self.bank_i % 8]
        self.bank_i += 1
        return b

    def xview(self, ap, t0, n=TT):
        return ap[:, t0:t0 + n].rearrange("(c p) t -> p c t", p=128)

    def load_cast(self, tl, out_ap, in_ap):
        self.S.I("pool", "dma_start", out=out_ap, in_=in_ap, W=[tl], dma=tl.b.name)

    def build(self):
        nc, S, d = self.nc, self.S, self.d
        with ExitStack() as g:
            self.ps = [Tl(g.enter_context(nc.psum_tensor(f"ps{i}", [128, 512], F32)), f"ps{i}") for i in range(8)]
            cm = self.sb(g, "cmat32", [128, len(CONST_ORDER) + 1, 128], F32)
            S.I("sp", "dma_start", out=cm[:], in_=d["cmat"], W=[cm], dma="cm")
            self.cb = self.sb(g, "cmatb", [128, len(CONST_ORDER) + 1, 128], BF16)
            S.I("dve", "tensor_copy", self.cb[:], cm[:], R=[cm], W=[self.cb])
            self.cm = cm
            self.vec = self.sb(g, "vecs", [128, NV + 4, 8], F32)
            S.I("sp", "dma_start", out=self.vec[:, 0:NV, :], in_=d["vecs"], W=[self.vec], dma="vec")
            self.hvec = self.sb(g, "hvec", [128, 8], F32)
            S.I("sp", "dma_start", out=self.hvec[:], in_=d["hvec"], W=[self.hvec], dma="hvec")
            self.eps_list = [1e-6, 1e-24, 64e-5, 1.0, float(np.log(0.6065306597126334))]
            self.epst = self.sb(g, "epst", [128, 8], F32)
            for i_, v_ in enumerate(self.eps_list):
                S.I("pool", "memset", self.epst[:, i_:i_ + 1], v_, W=[self.epst])
            self.kvstack = None
            for ti in range(self.NT):
                S.I("sp", "dma_start", out=d["yT"][:, ti * TT:(ti + 1) * TT], in_=d["xT"][:, ti * TT:(ti + 1) * TT], W=[self.XD[ti]], dma="xinit")
            for ph in self.phases:
                kind = ph[0]
                if kind == "rwkv":
                    self.phase_rwkv(ph[1])
                elif kind == "mlp":
                    self.phase_mlp(ph[1])
                elif kind == "kv":
                    self.kvstack = g
                    self.phase_kv()
                elif kind == "attn":
                    self.phase_attn(ph[1], ph[2])
                S.barrier()
                S.flush()
            S.barrier()
            S.flush(final=True)
        return nc

    def cmat(self, name):
        i = CONST_ORDER.index(name)
        return self.cb[:, i, :]

    def vcol(self, name, c):
        i = VIDX[name]
        return self.vec[:, i, c:c + 1]

    def rms_rstd(self, x, sq, rstd):
        S = self.S
        S.I("act", "activation", out=sq[:], in_=x[:], func=AF.Square, R=[x], W=[sq])
        pb = self.bank()
        om = self.cmat("onesm")
        for c in range(NC_):
            S.I("pe", "matmul", pb[:, :], lhsT=om, rhs=sq[:, c, :], start=(c == 0), stop=(c == NC_ - 1),
                 R=[sq, self.cb], W=[pb], inc=(c == NC_ - 1))
        self.rsqrt(rstd[:], pb[:, :], 1e-6, rstd, pb)

    def phase_mlp(self, L):
        nc, S, d = self.nc, self.S, self.d
        with ExitStack() as st:
            xs_ = [self.sb(st, f"mx{i}", [128, 8, TT], F32) for i in range(1)]
            sq = self.sb(st, "msq", [128, 8, TT], BF16)
            rstd = self.sb(st, "mrstd", [128, TT], F32)
            xn = [self.sb(st, f"mxn{i}", [128, 8, TT], BF16) for i in range(1)]
            h1 = self.sb(st, "mh1", [128, 32, TT], BF16)
            rl = [self.sb(st, f"mrl{i}", [128, TT], F32) for i in range(2)]
            w1b = [self.sb(st, f"mw1_{i}", [128, 8, 1024], BF16) for i in range(2)]
            w2b = [self.sb(st, f"mw2_{i}", [128, 32, 256], BF16) for i in range(2)]
            w1d = d["mlp_w1"][L].rearrange("(kc p) f -> p kc f", p=128)
            w2d = d["mlp_w2"][L].rearrange("(fc p) c -> p fc c", p=128)
            i1 = i2 = 0
            for ti in range(self.NT):
                t0 = ti * TT
                x = xs_[0]
                S.I("sp", "dma_start", out=x[:], in_=self.xview(d["yT"], t0), R=[self.XD[ti]], W=[x], dma=x.b.name)
                self.rms_rstd(x, sq, rstd)
                xnn = xn[0]
                for c in range(NC_):
                    S.I("dve", "scalar_tensor_tensor", out=xnn[:, c, :], in0=x[:, c, :], scalar=self.vcol(f"ln_mlp{L}", c), in1=rstd[:], op0=ALU.mult, op1=ALU.mult,
                         R=[x, rstd, self.vec], W=[xnn])
                for q in range(4):
                    w = w1b[i1 % 2]
                    i1 += 1
                    self.load_cast(w, w[:], w1d[:, :, q * 1024:(q + 1) * 1024])
                    for fc in range(8):
                        pb = self.bank()
                        for kc in range(8):
                            S.I("pe", "matmul", pb[:, :], lhsT=w[:, kc, fc * 128:(fc + 1) * 128], rhs=xnn[:, kc, :], start=(kc == 0), stop=(kc == 7),
                                 R=[w, xnn], W=[pb], inc=(kc == 7))
                        r = rl[fc % 2]
                        S.I("act", "activation", out=r[:], in_=pb[:, :], func=AF.Relu, R=[pb], W=[r])
                        S.I("dve", "tensor_tensor", out=h1[:, q * 8 + fc, :], in0=r[:], in1=r[:], op=ALU.mult, R=[r], W=[h1])
                for q in range(4):
                    w = w2b[i2 % 2]
                    i2 += 1
                    self.load_cast(w, w[:], w2d[:, :, q * 256:(q + 1) * 256])
                    for cc in range(2):
                        co = q * 2 + cc
                        pb = self.bank()
                        for fc in range(32):
                            S.I("pe", "matmul", pb[:, :], lhsT=w[:, fc, cc * 128:(cc + 1) * 128], rhs=h1[:, fc, :], start=(fc == 0), stop=(fc == 31),
                                 R=[w, h1], W=[pb], inc=(fc == 31))
                        S.I("dve", "tensor_tensor", out=x[:, co, :], in0=pb[:, :], in1=x[:, co, :], op=ALU.add, R=[pb, x], W=[x])
                S.I("sp", "dma_start", out=self.xview(d["yT"], t0), in_=x[:], R=[x], W=[self.XD[ti]], dma="xst")


def prep_shared(inp):
    f = lambda a: np.asarray(a, dtype=np.float32)
    vecs = np.zeros((NV, C), np.float32)
    for i in range(4):
        vecs[VIDX[f"ln_mix{i}"]] = f(inp["ln_mix"])[i]
        vecs[VIDX[f"ln_mlp{i}"]] = f(inp["ln_mlp"])[i]
    for j in range(2):
        for n in range(6):
            vecs[VIDX[f"mu{j}_{n}"]] = f(inp["a_mu"])[j, n]
        vecs[VIDX[f"w0_{j}"]] = f(inp["a_w0"])[j]
        vecs[VIDX[f"a0_{j}"]] = f(inp["a_a0"])[j]
        vecs[VIDX[f"kk_{j}"]] = f(inp["a_k_k"])[j]
        vecs[VIDX[f"ka_{j}"]] = f(inp["a_k_a"])[j]
        vecs[VIDX[f"rk_{j}"]] = f(inp["a_r_k"])[j].reshape(-1)
        vecs[VIDX[f"gng_{j}"]] = f(inp["a_gn_g"])[j]
        vecs[VIDX[f"gnb_{j}"]] = f(inp["a_gn_b"])[j]
    vecs[VIDX["v0"]] = f(inp["a_v0"])[0]
    vecs[VIDX["kvn"]] = f(inp["kv_norm"])
    vecs_l = np.ascontiguousarray(vecs.reshape(NV, 8, 128).transpose(2, 0, 1))
    hvec = np.zeros((128, 8), np.float32)
    hvec[:, 0] = np.tile(f(inp["b_q_gain"])[0], 2)
    hvec[:, 1] = np.tile(f(inp["b_q_gain"])[1], 2)
    hvec[:, 2] = np.tile(f(inp["k_gain"]), 2)
    sk = f(inp["b_sinks"])
    sr = sk.reshape(2, 2, 4, 2).transpose(0, 3, 1, 2)
    sinkrow = np.ascontiguousarray(np.broadcast_to(sr[None, :, :, :, :, None], (1, 2, 2, 2, 4, 128))).astype(np.float32)
    cs = make_consts()
    cmat = np.stack([cs[k][:, 0:128] for k in CONST_ORDER] + [cs["m_abrb"][:, 128:256]], 1)
    sh = {"vecs": vecs_l, "hvec": hvec, "sinkrow": sinkrow, "cmat": np.ascontiguousarray(cmat.astype(np.float32)),
          "alibi": cs["alibi"]}
    for k in ["mlp_w1", "mlp_w2", "a_w_rkv", "a_w1", "a_w2", "a_a1", "a_a2", "a_g1", "a_g2", "a_wo", "a_v1", "a_v2", "w_kv", "b_wq", "b_wo"]:
        sh[k] = np.ascontiguousarray(f(inp[k]))
    return sh


FULL_PHASES = [("rwkv", 0), ("mlp", 0), ("rwkv", 1), ("mlp", 1), ("kv",), ("attn", 0, 2), ("mlp", 2), ("attn", 1, 3), ("mlp", 3)]


def run(inp, phases, T, ncores, dbg=None):
    sh = prep_shared(inp)
    x = np.asarray(inp["x"], dtype=np.float32)
    in_maps = []
    for b in range(ncores):
        m = dict(sh)
        m["xT"] = np.ascontiguousarray(x[b, :T].T)
        in_maps.append(m)
    nc = Builder(T, phases, dbg).build()
    res = run_bass_kernel_spmd(nc, in_maps, core_ids=list(range(ncores)))
    return res


def kernel(**inputs):
    res = run(inputs, FULL_PHASES, 4096, 8)
    out = np.stack([np.ascontiguousarray(r["yT"].T) for r in res.results], 0)
    return out.astype(np.float32)


TR = 256
NJ = 2


def _phase_rwkv(self, L):
    nc, S, d = self.nc, self.S, self.d
    NTR = self.T // TR
    with ExitStack() as st:
        sb = lambda name, shape, dt: self.sb(st, name, shape, dt)
        V = lambda name, c: self.vcol(f"{name}_{L}", c)
        cview = lambda ap: ap.rearrange("(kc p) n -> p kc n", p=128)
        wo = sb("wo", [128, 8, 1024], BF16)
        for i in range(3):
            S.I("pool", "dma_start", out=d["wrkv_bf"][L, i], in_=d["a_w_rkv"][L, i], W=[self.WBF], dma="wcast")
        self.load_cast(wo, wo[:], cview(d["a_wo"][L]))
        wslot = [sb(f"wslot{i}", [128, 3, 8, 128], BF16) for i in range(2)]
        w1 = sb("w1", [128, 8, 64], BF16)
        a1 = sb("a1", [128, 8, 64], BF16)
        g1 = sb("g1", [128, 8, 160], BF16)
        self.load_cast(w1, w1[:], cview(d["a_w1"][L]))
        self.load_cast(a1, a1[:], cview(d["a_a1"][L]))
        self.load_cast(g1, g1[:], cview(d["a_g1"][L]))
        w2 = sb("w2", [64, 1024], BF16)
        a2 = sb("a2", [64, 1024], BF16)
        self.load_cast(w2, w2[:], d["a_w2"][L])
        self.load_cast(a2, a2[:], d["a_a2"][L])
        g2 = sb("g2", [128, 2, 1024], BF16)
        self.load_cast(g2, g2[:, 0, :], d["a_g2"][L][0:128, :])
        self.load_cast(g2, g2[0:32, 1, :], d["a_g2"][L][128:160, :])
        if L == 1:
            v1 = sb("v1", [128, 8, 32], BF16)
            v2 = sb("v2", [32, 1024], BF16)
            self.load_cast(v1, v1[:], cview(d["a_v1"][0]))
            self.load_cast(v2, v2[:], d["a_v2"][0])
        vx = sb("vx", [128, 3, 8], F32)
        for i, nm in enumerate([f"w0_{L}", f"a0_{L}", "v0"]):
            S.I("dve", "tensor_scalar", out=vx[:, i, :], in0=self.vec[:, VIDX[nm], :], scalar1=-1.0, scalar2=None, op0=ALU.mult, R=[self.vec], W=[vx])
        ones = sb("ones", [128, 128], F32)
        S.I("pool", "memset", ones[:], 1.0, W=[ones])
        mask4 = sb("mask4", [128, 4, 128], BF16)
        ia, ib = CONST_ORDER.index("m_abrb"), len(CONST_ORDER)
        for q in range(4):
            S.I("pool", "tensor_copy", mask4[:, q, :], self.cm[:, (ia if q % 2 == 0 else ib), :], R=[self.cm], W=[mask4])
        msl = self.cm[:, CONST_ORDER.index("m_sl"), :]
        bd1f = self.cm[:, CONST_ORDER.index("bd1"), :]
        identb = self.cmat("ident")
        bd1 = self.cmat("bd1")
        bd64 = self.cmat("bd64")
        Mf = [sb(f"Mf{c}", [128, 64], F32) for c in range(8)]
        Mb = [[sb(f"Mb{c}_{p}", [128, 128], BF16) for p in range(2)] for c in range(8)]
        for c in range(8):
            for p in range(2):
                if p == 0:
                    S.I("pool", "memset", Mf[c][:], 0.0, W=[Mf[c]])
                S.I("pool", "memset", Mb[c][p][:], 0.0, W=[Mb[c][p]])
        x = sb("x", [128, 8, TR], F32)
        rstd = sb("rstd", [128, TR], F32)
        h = sb("h", [128, 8, TR + 1], F32)
        S.I("pool", "memset", h[:, :, 0:1], 0.0, W=[h])
        carry = sb("carry", [128, 8, 1], F32)
        xx = sb("xx", [128, 8, TR], BF16)
        xsr, xsk, xsv = [sb(n, [128, 8, TR], BF16) for n in ("xsr", "xsk", "xsv")]
        xso = [sb(f"xso{i}", [128, 8, TR], BF16) for i in range(2)]
        sq = xso[1]
        lw = sb("lw", [64, TR], BF16)
        al = sb("al", [64, TR], BF16)
        vl = sb("vl", [32, TR], BF16)
        sg = sb("sg", [128, 2, TR], BF16)
        sgt = sb("sgt", [128, 2, TR], BF16)
        ygb = sb("ygb", [128, 8, TR], BF16)
        PLt = sb("PLt", [128, 8, NJ], F32)
        NTMP = 14
        tmp = [[sb(f"t{s}_{i}", [128, TR], F32) for i in range(NTMP)] for s in range(1)]
        opb = [[sb(f"o{s}_{i}", [128, TR], BF16) for i in range(2)] for s in range(1)]
        ar = [sb(f"ar{s}", [128, NJ, 2, 128], BF16) for s in range(4)]
        vft = [sb(f"vf{s}", [128, TR], F32) for s in range(2)]
        par3 = [[sb(f"p3_{s}_{i}", [128, TR], F32) for i in range(3)] for s in range(4)]
        opb2 = [[sb(f"ob{s}_{i}", [128, TR], BF16) for i in range(5)] for s in range(4)]
        epf = [sb(f"epf{i}", [128, TR], F32) for i in range(2)]
        epb = [sb(f"epb{i}", [128, TR], BF16) for i in range(2)]
        def wset(s):
            w = {}
            w["AX"] = sb(f"AX{s}", [128, 2, 128], BF16)
            w["BKV"] = sb(f"BKV{s}", [128, 3, 128], BF16)
            w["Vpad"] = sb(f"Vpad{s}", [128, 2, 128], BF16)
            w["AA"] = sb(f"AA{s}", [128, 2, 4, 128], BF16)
            DD = F32R if os.environ.get("RW_FP32R", "1") == "1" else F32
            w["N0"] = sb(f"N0{s}", [128, 2, 128], DD)
            w["Q0"] = sb(f"Q0{s}", [128, 2, 128], DD)
            w["QN"] = [sb(f"QN{s}_{i}", [128, 2, 2, 128], DD) for i in range(2)]
            w["Z"] = [sb(f"Z{s}_{i}", [128, 2, 128], DD) for i in range(2)]
            w["Zfin"] = sb(f"Zfin{s}", [128, 2, 128], BF16)
            w["Ap"] = sb(f"Ap{s}", [128, 128], BF16)
            w["UV"] = sb(f"UV{s}", [128, 128], BF16)
            w["UVpad"] = sb(f"UVpad{s}", [128, 2, 128], BF16)
            w["GT"] = sb(f"GT{s}", [128, 128], BF16)
            w["RT"] = sb(f"RT{s}", [128, 128], BF16)
            for k in ("Vpad", "UVpad"):
                S.I("pool", "memset", w[k][:], 0.0, W=[w[k]])
            return w
        ws = [wset(s) for s in range(4)]
        if os.environ.get("DBG_SBUF"):
            print("rwkv sbuf remaining", nc.sbuf_bytes_remaining)

        def diag(tl, off=0):
            a = tl[:]
            ps_ = a.ap[0][0]
            return bass.AP(a.tensor, a.offset + off, [[ps_, 128], [192, 2], [1, 64]])

        def strided(tl, off, stride):
            a = tl[:]
            ps_ = a.ap[0][0]
            return bass.AP(a.tensor, a.offset + off, [[ps_, 128], [stride, 2], [1, 64]])

        self._evi = 0

        def evac_copy(out_ap, in_ap, R, W):
            self._evi += 1
            if self._evi % 2:
                S.I("act", "copy", out_ap, in_ap, R=R, W=W)
            else:
                S.I("dve", "tensor_copy", out_ap, in_ap, R=R, W=W)

        def proj(pb, col, wt, xs, co, n=128, kparts=128):
            for kc in range(8):
                S.I("pe", "matmul", pb[0:n, col:col + TR], lhsT=wt[:, kc, co:co + n], rhs=xs[:, kc, :], start=(kc == 0), stop=(kc == 7),
                     R=[wt, xs], W=[pb], inc=(kc == 7))

        def projs(pb, col, slot, n_, xs):
            for kc in range(8):
                S.I("pe", "matmul", pb[:, col:col + TR], lhsT=slot[:, n_, kc, :], rhs=xs[:, kc, :], start=(kc == 0), stop=(kc == 7),
                    R=[slot, xs], W=[pb], inc=(kc == 7))

        def mix(dst, n):
            for c in range(8):
                S.I("dve", "scalar_tensor_tensor", out=dst[:, c, :], in0=xx[:, c, :], scalar=self.vcol(f"mu{L}_{n}", c), in1=h[:, c, 1:TR + 1], op0=ALU.mult, op1=ALU.add,
                     R=[xx, h, self.vec], W=[dst])

        cj = 0
        for ti in range(NTR):
            t0 = ti * TR
            xd = self.XD[t0 // TT]
            S.I("sp", "dma_start", out=x[:], in_=self.xview(d["yT"], t0, TR), R=[xd], W=[x], dma="rx")
            self.rms_rstd_n(x, sq, rstd, TR)
            if ti > 0:
                S.I("pool", "tensor_copy", h[:, :, 0:1], carry[:], R=[carry], W=[h])
            for c in range(8):
                S.I("dve", "scalar_tensor_tensor", out=h[:, c, 1:TR + 1], in0=x[:, c, :], scalar=self.vcol(f"ln_mix{L}", c), in1=rstd[:, 0:TR], op0=ALU.mult, op1=ALU.mult,
                     R=[x, rstd, self.vec], W=[h])
            S.I("pool", "tensor_copy", carry[:], h[:, :, TR:TR + 1], R=[h], W=[carry])
            S.I("dve", "tensor_tensor", out=xx[:], in0=h[:, :, 0:TR], in1=h[:, :, 1:TR + 1], op=ALU.subtract, R=[h], W=[xx])
            mix(xsr, 0)
            mix(xsk, 1)
            mix(xsv, 2)
            xw = xso[0]
            mix(xw, 3)
            pb = self.bank()
            proj(pb, 0, w1, xw, 0, n=64)
            S.I("act", "activation", out=lw[:], in_=pb[0:64, 0:TR], func=AF.Tanh, R=[pb], W=[lw])
            xa = xso[1]
            mix(xa, 4)
            proj(pb, 256, a1, xa, 0, n=64)
            S.I("act", "copy", al[:], pb[0:64, 256:256 + TR], R=[pb], W=[al])
            xg = xso[0]
            mix(xg, 5)
            pb = self.bank()
            proj(pb, 0, g1, xg, 0, n=128)
            proj(pb, 256, g1, xg, 128, n=32)
            S.I("act", "activation", out=sgt[:, 0, :], in_=pb[:, 0:TR], func=AF.Exp, scale=-1.0, R=[pb], W=[sgt])
            S.I("act", "activation", out=sgt[0:32, 1, :], in_=pb[0:32, 256:256 + TR], func=AF.Exp, scale=-1.0, R=[pb], W=[sgt])
            S.I("act", "activation", out=sgt[:, 0, :], in_=sgt[:, 0, :], func=AF.Ln, bias=self.epsc(1.0), R=[sgt, self.epst], W=[sgt])
            S.I("act", "activation", out=sgt[0:32, 1, :], in_=sgt[0:32, 1, :], func=AF.Ln, bias=self.epst[0:32, 3:4], R=[sgt, self.epst], W=[sgt])
            S.I("act", "activation", out=sg[:, 0, :], in_=sgt[:, 0, :], func=AF.Exp, scale=-1.0, R=[sgt], W=[sg])
            S.I("act", "activation", out=sg[0:32, 1, :], in_=sgt[0:32, 1, :], func=AF.Exp, scale=-1.0, R=[sgt], W=[sg])
            if L == 1:
                pbv = self.bank()
                proj(pbv, 0, v1, xsv, 0, n=32)
                S.I("act", "copy", vl[:], pbv[0:32, 0:TR], R=[pbv], W=[vl])

            def make_c(c, s, holder):
                T_ = tmp[0]
                r_sb, k_sb, logw, a_sb, kkr, rn, kk, kmod, bvec, lp, E1, Em, Ep, Eh = T_
                v_sb, bonus, ysb = par3[s]
                bt, kt, bh, kh, vbf = opb2[s]
                prod, kq = opb[0][0], opb[0][1]
                arr = ar[s]
                cs = slice(c * 128, (c + 1) * 128)
                slot = wslot[s % 2]
                wsrc = lambda n_: d["wrkv_bf"][L, n_].rearrange("(kc p) m -> p kc m", p=128)[:, :, c * 128:(c + 1) * 128]
                for n_ in range(3):
                    S.I("sp", "dma_start", out=slot[:, n_, :, :], in_=wsrc(n_), R=[self.WBF], W=[slot], dma=slot.b.name)
                pb1 = self.bank()
                projs(pb1, 0, slot, 0, xsr)
                projs(pb1, 256, slot, 1, xsk)
                pb2 = self.bank()
                projs(pb2, 0, slot, 2, xsv)
                S.I("pe", "matmul", pb2[:, 256:256 + TR], lhsT=w2[:, cs], rhs=lw[:], start=True, stop=True, R=[w2, lw], W=[pb2])
                pb3 = self.bank()
                S.I("pe", "matmul", pb3[:, 0:TR], lhsT=a2[:, cs], rhs=al[:], start=True, stop=True, R=[a2, al], W=[pb3], inc=(L == 0))
                if L == 1:
                    S.I("pe", "matmul", pb3[:, 256:256 + TR], lhsT=v2[:, cs], rhs=vl[:], start=True, stop=True, R=[v2, vl], W=[pb3])
                S.I("act", "copy", r_sb[:], pb1[:, 0:TR], R=[pb1], W=[r_sb])
                S.I("act", "copy", k_sb[:], pb1[:, 256:256 + TR], R=[pb1], W=[k_sb])
                S.I("act", "copy", v_sb[:], pb2[:, 0:TR], R=[pb2], W=[v_sb])
                vrows = d["vfirst"][c * 128:(c + 1) * 128, t0:t0 + TR]
                if L == 0:
                    S.I("sp", "dma_start", out=vrows, in_=v_sb[:], R=[v_sb], W=[self.VFD], dma="vfst")
                else:
                    vf = vft[s % 2]
                    S.I("sp", "dma_start", out=vf[:], in_=vrows, R=[self.VFD], W=[vf], dma=vf.b.name)
                    S.I("act", "activation", out=E1[:], in_=pb3[:, 256:256 + TR], func=AF.Exp, bias=vx[:, 2, c:c + 1], scale=-1.0, R=[pb3, vx], W=[E1])
                    S.I("act", "activation", out=E1[:], in_=E1[:], func=AF.Ln, bias=self.epsc(1.0), R=[E1, self.epst], W=[E1])
                    S.I("act", "activation", out=E1[:], in_=E1[:], func=AF.Exp, scale=-1.0, R=[E1], W=[E1])
                    S.I("dve", "tensor_tensor", out=vf[:], in0=vf[:], in1=v_sb[:], op=ALU.subtract, R=[vf, v_sb], W=[vf])
                    S.I("dve", "tensor_tensor", out=vf[:], in0=vf[:], in1=E1[:], op=ALU.mult, R=[vf, E1], W=[vf])
                    S.I("dve", "tensor_tensor", out=v_sb[:], in0=v_sb[:], in1=vf[:], op=ALU.add, R=[vf, v_sb], W=[v_sb])
                S.I("act", "activation", out=logw[:], in_=pb2[:, 256:256 + TR], func=AF.Exp, bias=vx[:, 0, c:c + 1], scale=-1.0, R=[pb2, vx], W=[logw])
                S.I("act", "activation", out=logw[:], in_=logw[:], func=AF.Ln, bias=self.epsc(1.0), R=[logw, self.epst], W=[logw])
                S.I("act", "activation", out=logw[:], in_=logw[:], func=AF.Exp, bias=self.epsc(self.eps_list[4]), scale=-1.0, R=[logw, self.epst], W=[logw])
                S.I("act", "activation", out=a_sb[:], in_=pb3[:, 0:TR], func=AF.Exp, bias=vx[:, 1, c:c + 1], scale=-1.0, R=[pb3, vx], W=[a_sb])
                yield
                S.I("act", "activation", out=a_sb[:], in_=a_sb[:], func=AF.Ln, bias=self.epsc(1.0), R=[a_sb, self.epst], W=[a_sb])
                S.I("act", "activation", out=a_sb[:], in_=a_sb[:], func=AF.Exp, scale=-1.0, R=[a_sb], W=[a_sb])
                yield
                S.I("dve", "tensor_scalar", out=kkr[:], in0=k_sb[:], scalar1=V("kk", c), scalar2=None, op0=ALU.mult, R=[k_sb, self.vec], W=[kkr])
                S.I("act", "activation", out=kq[:], in_=kkr[:], func=AF.Square, R=[kkr], W=[kq])
                pbs = self.bank()
                S.I("pe", "matmul", pbs[:, 0:TR], lhsT=bd1, rhs=kq[:], start=True, stop=True, R=[kq, self.cb], W=[pbs])
                self.rsqrt(rn[:], pbs[:, 0:TR], 1e-24, rn, pbs)
                S.I("pool", "tensor_tensor", out=kk[:], in0=kkr[:], in1=rn[:], op=ALU.mult, R=[kkr, rn], W=[kk])
                yield
                S.I("dve", "tensor_scalar", out=kmod[:], in0=a_sb[:], scalar1=-1.0, scalar2=V("ka", c), op0=ALU.add, op1=ALU.mult, R=[a_sb, self.vec], W=[kmod])
                S.I("dve", "scalar_tensor_tensor", out=kmod[:], in0=kmod[:], scalar=1.0, in1=k_sb[:], op0=ALU.add, op1=ALU.mult, R=[kmod, k_sb], W=[kmod])
                S.I("pool", "tensor_tensor", out=bvec[:], in0=kk[:], in1=a_sb[:], op=ALU.mult, R=[kk, a_sb], W=[bvec])
                yield
                for j in range(NJ):
                    js = slice(j * 128, (j + 1) * 128)
                    S.I("dve", "tensor_tensor_scan", out=lp[:, js], data0=ones[:], data1=logw[:, js], initial=0.0, op0=ALU.mult, op1=ALU.subtract, R=[logw, ones], W=[lp])
                S.I("pool", "tensor_tensor", out=Ep[:], in0=lp[:], in1=logw[:], op=ALU.add, R=[lp, logw], W=[Ep])
                S.I("act", "activation", out=Ep[:], in_=Ep[:], func=AF.Exp, R=[Ep], W=[Ep])
                yield
                S.I("act", "activation", out=E1[:], in_=lp[:], func=AF.Exp, R=[lp], W=[E1])
                S.I("act", "activation", out=Em[:], in_=lp[:], func=AF.Exp, scale=-1.0, R=[lp], W=[Em])
                yield
                for j in range(NJ):
                    js = slice(j * 128, (j + 1) * 128)
                    S.I("act", "activation", out=Eh[:, js], in_=lp[:, js], func=AF.Exp, bias=lp[:, j * 128 + 127:j * 128 + 128], scale=-1.0, R=[lp], W=[Eh])
                    S.I("act", "activation", out=PLt[:, c, j:j + 1], in_=lp[:, j * 128 + 127:j * 128 + 128], func=AF.Exp, R=[lp], W=[PLt])
                yield
                tv = lambda t: t[:].rearrange("p (j t) -> p j t", j=NJ)
                S.I("pool", "tensor_tensor", out=arr[:, :, 1, :], in0=tv(r_sb), in1=tv(E1), op=ALU.mult, R=[r_sb, E1], W=[arr])
                S.I("dve", "scalar_tensor_tensor", out=arr[:, :, 0, :], in0=tv(kk), scalar=-1.0, in1=tv(Ep), op0=ALU.mult, op1=ALU.mult, R=[kk, Ep], W=[arr])
                yield
                S.I("pool", "tensor_tensor", out=bt[:], in0=bvec[:], in1=Em[:], op=ALU.mult, R=[bvec, Em], W=[bt])
                S.I("dve", "tensor_tensor", out=kt[:], in0=kmod[:], in1=Em[:], op=ALU.mult, R=[kmod, Em], W=[kt])
                yield
                S.I("pool", "tensor_tensor", out=bh[:], in0=bvec[:], in1=Eh[:], op=ALU.mult, R=[bvec, Eh], W=[bh])
                S.I("dve", "tensor_tensor", out=kh[:], in0=kmod[:], in1=Eh[:], op=ALU.mult, R=[kmod, Eh], W=[kh])
                yield
                S.I("pool", "tensor_copy", vbf[:], v_sb[:], R=[v_sb], W=[vbf])
                S.I("dve", "scalar_tensor_tensor", out=prod[:], in0=r_sb[:], scalar=V("rk", c), in1=kmod[:], op0=ALU.mult, op1=ALU.mult, R=[r_sb, kmod, self.vec], W=[prod])
                pbs2 = self.bank()
                S.I("pe", "matmul", pbs2[:, 0:TR], lhsT=bd1, rhs=prod[:], start=True, stop=True, R=[prod, self.cb], W=[pbs2])
                S.I("act", "copy", bonus[:], pbs2[:, 0:TR], R=[pbs2], W=[bonus])
                def stream(j, w):
                    js = slice(j * 128, (j + 1) * 128)
                    AX, BKV, Vpad, AA, N0, QN, Z, Ap, UV, UVpad, GT, RT = [w[k] for k in ("AX", "BKV", "Vpad", "AA", "N0", "QN", "Z", "Ap", "UV", "UVpad", "GT", "RT")]
                    Q0, Zfin = w["Q0"], w["Zfin"]
                    identf = self.cm[:, CONST_ORDER.index("ident"), :]
                    par = (ti * NJ + j) % 2
                    Mfc = Mf[c]
                    Mb0, Mb1 = Mb[c][par], Mb[c][1 - par]
                    pt = self.bank()
                    ptb = pt[:].bitcast(BF16)
                    srcs = [(arr, arr[:, j, 0, :]), (bh, bh[:, js]), (kh, kh[:, js]), (vbf, vbf[:, js])]
                    for i, (tl, ap) in enumerate(srcs):
                        S.I("pe", "transpose", ptb[:, i * 128:(i + 1) * 128], ap, identb, R=[tl, self.cb], W=[pt], inc=(i == 3))
                    if True:
                        S.I("dve", "tensor_copy", AX[:, :, 0:64], ptb[:, 0:128].rearrange("p (e k) -> p e k", e=2), R=[pt], W=[AX])
                    S.I("dve", "tensor_copy", BKV[:].rearrange("p a b -> p (a b)"), ptb[:, 128:512], R=[pt], W=[BKV])
                    if True:
                        S.I("dve", "tensor_copy", diag(Vpad), ptb[:, 384:512].rearrange("p (e k) -> p e k", e=2), R=[pt], W=[Vpad])
                    yield
                    Eb = [self.bank(), self.bank()]
                    Nb = [self.bank(), self.bank()]
                    for e_ in range(2):
                        hs = slice(64 * e_, 64 * e_ + 64)
                        arj = arr[hs, j, :, :].rearrange("p a b -> p (a b)")
                        S.I("pe", "matmul", Eb[e_][:, 0:256], lhsT=bt[hs, js], rhs=arj, start=True, stop=True, R=[bt, arr], W=[Eb[e_]], inc=False)
                        S.I("pe", "matmul", Eb[e_][:, 256:512], lhsT=kt[hs, js], rhs=arj, start=True, stop=True, R=[kt, arr], W=[Eb[e_]])
                        S.I("pe", "matmul", Nb[e_][:, 0:128], lhsT=arr[hs, j, 0, :], rhs=bt[hs, js], start=True, stop=True, R=[bt, arr], W=[Nb[e_]])
                        if True:
                            S.I("dve", "tensor_tensor", out=AA[:, e_, 1:4, :], in0=Eb[e_][:, 128:512].rearrange("p (a b) -> p a b", a=3), in1=mask4[:, 1:4, :], op=ALU.mult, R=[Eb[e_], mask4], W=[AA])
                        if True:
                            S.I("dve", "tensor_tensor", out=N0[:, e_, :], in0=Nb[e_][:, 0:128], in1=msl, op=ALU.mult, R=[Nb[e_], self.cm], W=[N0])
                        if True:
                            S.I("dve", "tensor_tensor", out=Q0[:, e_, :], in0=Eb[e_][:, 0:128], in1=mask4[:, 0, :], op=ALU.mult, R=[Eb[e_], mask4], W=[Q0])
                            S.I("pool", "tensor_tensor", out=Z[0][:, e_, :], in0=Q0[:, e_, :], in1=identf, op=ALU.add, R=[Q0, self.cm], W=[Z[0]])
                    yield
                    for lv in range(1, 7):
                        qn_prev = QN[(lv - 1) % 2]
                        qn = QN[lv % 2]
                        B = self.bank()
                        for e_ in range(2):
                            Qp = Q0[:, e_, :] if lv == 1 else qn_prev[:, e_, 0, :]
                            Np = N0[:, e_, :] if lv == 1 else qn_prev[:, e_, 1, :]
                            Rr = [Q0, N0] if lv == 1 else [qn_prev]
                            if lv < 6:
                                S.I("pe", "matmul", B[:, e_ * 256:e_ * 256 + 128], lhsT=Np, rhs=Qp, start=True, stop=True, R=Rr, W=[B], inc=False)
                            S.I("pe", "matmul", B[:, e_ * 256 + 128:e_ * 256 + 256], lhsT=Qp, rhs=Np, start=True, stop=True, R=Rr, W=[B], inc=(e_ == 1))
                        if lv < 6:
                            evac_copy(qn[:].rearrange("p a b c -> p (a b c)"), B[:, :], [B], [qn])
                        else:
                            evac_copy(qn[:, :, 1, :], B[:, :].rearrange("p (a b c) -> p a b c", a=2, b=2)[:, :, 1, :], [B], [qn])
                        yield
                        Zb = self.bank()
                        zp, zn = Z[(lv - 1) % 2], (Z[lv % 2] if lv < 6 else Zfin)
                        for e_ in range(2):
                            S.I("pe", "matmul", Zb[:, e_ * 128:(e_ + 1) * 128], lhsT=qn[:, e_, 1, :], rhs=zp[:, e_, :], start=True, stop=True, R=[qn, zp], W=[Zb], inc=(e_ == 1))
                        S.I("dve", "tensor_tensor", out=zn[:].rearrange("p a b -> p (a b)"), in0=Zb[:, 0:256], in1=zp[:].rearrange("p a b -> p (a b)"), op=ALU.add, R=[Zb, zp], W=[zn])
                    Zf = Zfin
                    yield
                    Xb = self.bank()
                    for e_ in range(2):
                        S.I("pe", "matmul", Xb[:, e_ * 128:(e_ + 1) * 128], lhsT=AA[:, e_, 2, :], rhs=BKV[:, 2, :], start=True, stop=True, R=[AA, BKV], W=[Xb], inc=(e_ == 1))
                    S.I("dve", "tensor_copy", AX[:, :, 64:128], diag(Xb), R=[Xb], W=[AX])
                    yield
                    Ub = self.bank()
                    for e_ in range(2):
                        S.I("pe", "matmul", Ub[:, e_ * 128:(e_ + 1) * 128], lhsT=Zf[:, e_, :], rhs=AX[:, e_, :], start=True, stop=True, R=[Zf, AX], W=[Ub], inc=(e_ == 1))
                    S.I("dve", "tensor_copy", Ap[:].rearrange("p (e k) -> p e k", e=2), strided(Ub, 0, 128), R=[Ub], W=[Ap])
                    S.I("dve", "tensor_copy", UV[:].rearrange("p (e k) -> p e k", e=2), strided(Ub, 64, 128), R=[Ub], W=[UV])
                    S.I("dve", "tensor_copy", diag(UVpad), strided(Ub, 64, 128), R=[Ub], W=[UVpad])
                    yield
                    Gb = self.bank()
                    S.I("pe", "matmul", Gb[:, 0:128], lhsT=Ap[:], rhs=BKV[:, 0, :], start=True, stop=True, R=[Ap, BKV], W=[Gb])
                    S.I("dve", "tensor_tensor", out=GT[:], in0=Gb[:, 0:128], in1=bd1f, op=ALU.mult, R=[Gb, self.cm], W=[GT])
                    Rb = self.bank()
                    for e_ in range(2):
                        S.I("pe", "matmul", Rb[:, e_ * 128:(e_ + 1) * 128], lhsT=Ap[:], rhs=AA[:, e_, 1, :], start=True, stop=True, R=[Ap, AA], W=[Rb], inc=(e_ == 1))
                    for e_ in range(2):
                        hs = slice(64 * e_, 64 * e_ + 64)
                        S.I("dve", "tensor_tensor", out=RT[hs, :], in0=Rb[hs, e_ * 128:(e_ + 1) * 128], in1=arr[hs, j, 1, :], op=ALU.add, R=[Rb, arr], W=[RT])
                    yield
                    Yb = self.bank()
                    ymm = [(UVpad[:, 0, :], AA[:, 0, 1, :]), (UVpad[:, 1, :], AA[:, 1, 1, :]), (Vpad[:, 0, :], AA[:, 0, 3, :]), (Vpad[:, 1, :], AA[:, 1, 3, :]), (Mb0[:], RT[:])]
                    for i, (l_, r_) in enumerate(ymm):
                        S.I("pe", "matmul", Yb[:, 0:128], lhsT=l_, rhs=r_, start=(i == 0), stop=(i == 4), R=[UVpad, Vpad, AA, Mb0, RT], W=[Yb], inc=(i == 4))
                    S.I("act", "copy", ysb[:, js], Yb[:, 0:128], R=[Yb], W=[ysb])
                    Mbk = self.bank()
                    cmm = [(BKV[:, 0, :], UV[:]), (BKV[:, 1, :], BKV[:, 2, :]), (GT[:], Mb0[:])]
                    for i, (l_, r_) in enumerate(cmm):
                        S.I("pe", "matmul", Mbk[:, 0:128], lhsT=l_, rhs=r_, start=(i == 0), stop=(i == 2), R=[BKV, UV, GT, Mb0], W=[Mbk], inc=(i == 2))
                    for e_ in range(2):
                        hs = slice(64 * e_, 64 * e_ + 64)
                        S.I("dve", "scalar_tensor_tensor", out=Mfc[hs, :], in0=Mfc[hs, :], scalar=PLt[hs, c, j:j + 1], in1=Mbk[hs, hs], op0=ALU.mult, op1=ALU.add, R=[Mbk, Mfc, PLt], W=[Mfc])
                        S.I("pool", "tensor_copy", Mb1[hs, hs], Mfc[hs, :], R=[Mfc], W=[Mb1])
                def epilogue():
                    ybf, yc, ysq, rs = epb[0], epf[0], epb[1], epf[1]
                    S.I("pool", "tensor_copy", ybf[:], ysb[:], R=[ysb], W=[ybf])
                    pg = self.bank()
                    S.I("pe", "matmul", pg[:, 0:TR], lhsT=bd64, rhs=ybf[:], start=True, stop=True, R=[ybf, self.cb], W=[pg])
                    S.I("dve", "tensor_tensor", out=yc[:], in0=ysb[:], in1=pg[:, 0:TR], op=ALU.subtract, R=[pg, ysb], W=[yc])
                    S.I("act", "activation", out=ysq[:], in_=yc[:], func=AF.Square, R=[yc], W=[ysq])
                    S.I("pe", "matmul", pg[:, 256:256 + TR], lhsT=bd64, rhs=ysq[:], start=True, stop=True, R=[ysq, self.cb], W=[pg])
                    self.rsqrt(rs[:], pg[:, 256:256 + TR], 64e-5, rs, pg)
                    yield
                    S.I("pool", "tensor_tensor", out=yc[:], in0=yc[:], in1=rs[:], op=ALU.mult, R=[yc, rs], W=[yc])
                    S.I("act", "activation", out=yc[:], in_=yc[:], func=AF.Identity, bias=V("gnb", c), scale=V("gng", c), R=[yc, self.vec], W=[yc])
                    yield
                    S.I("pool", "tensor_tensor", out=bonus[:], in0=bonus[:], in1=v_sb[:], op=ALU.mult, R=[bonus, v_sb], W=[bonus])
                    S.I("dve", "tensor_tensor", out=yc[:], in0=yc[:], in1=bonus[:], op=ALU.add, R=[yc, bonus], W=[yc])
                    yield
                    pgg = self.bank()
                    S.I("pe", "matmul", pgg[:, 0:TR], lhsT=g2[:, 0, cs], rhs=sg[:, 0, :], start=True, stop=False, R=[g2, sg], W=[pgg], inc=False)
                    S.I("pe", "matmul", pgg[:, 0:TR], lhsT=g2[0:32, 1, cs], rhs=sg[0:32, 1, :], start=False, stop=True, R=[g2, sg], W=[pgg])
                    S.I("dve", "tensor_tensor", out=ygb[:, c, :], in0=yc[:], in1=pgg[:, 0:TR], op=ALU.mult, R=[pgg, yc], W=[ygb])
                holder["stream"] = stream
                holder["epilogue"] = epilogue

            def rr(gens):
                gens = list(gens)
                while gens:
                    for g_ in list(gens):
                        try:
                            next(g_)
                        except StopIteration:
                            gens.remove(g_)

            def delayed(g_, n_):
                for _ in range(n_):
                    yield
                yield from g_

            def chain(*gs):
                for g_ in gs:
                    yield from g_

            holders = {}
            ready = set()
            done = {cp: 0 for cp in range(4)}

            def pro(cp):
                hs = [{}, {}]
                holders[cp] = hs
                yield from make_c(2 * cp, (cp % 2) * 2, hs[0])
                yield from make_c(2 * cp + 1, (cp % 2) * 2 + 1, hs[1])
                ready.add(cp)

            def aux_gen():
                yield from pro(0)
                yield from pro(1)
                for p_ in range(4):
                    while done[p_] < 4:
                        yield
                    yield from holders[p_][0]["epilogue"]()
                    yield from holders[p_][1]["epilogue"]()
                    if p_ + 2 < 4:
                        yield from pro(p_ + 2)

            SD = int(os.environ.get("RW_STAGGER", "1"))
            pend = []
            for cp in range(4):
                pend += [(cp, 0, 0), (cp, 1, 0), (cp, 0, 1), (cp, 1, 1)]
            slots = [None] * 4
            aux = aux_gen()
            aux_alive = True
            cycle = 0
            last_start = -SD
            while pend or any(x is not None for x in slots) or aux_alive:
                if pend and cycle - last_start >= SD and pend[0][0] in ready:
                    for k_ in range(4):
                        if slots[k_] is None:
                            cp, hi, j_ = pend.pop(0)
                            slots[k_] = (holders[cp][hi]["stream"](j_, ws[k_]), cp)
                            last_start = cycle
                            break
                for k_ in range(4):
                    if slots[k_] is not None:
                        try:
                            next(slots[k_][0])
                        except StopIteration:
                            done[slots[k_][1]] += 1
                            slots[k_] = None
                if aux_alive:
                    try:
                        next(aux)
                    except StopIteration:
                        aux_alive = False
                cycle += 1
                assert cycle < 100000
            for co in range(8):
                pb = self.bank() if co % 2 == 0 else pb
                col = (co % 2) * 256
                for kc in range(8):
                    S.I("pe", "matmul", pb[:, col:col + TR], lhsT=wo[:, kc, co * 128:(co + 1) * 128], rhs=ygb[:, kc, :], start=(kc == 0), stop=(kc == 7),
                         R=[wo, ygb], W=[pb], inc=(kc == 7))
                S.I("dve", "tensor_tensor", out=x[:, co, :], in0=pb[:, col:col + TR], in1=x[:, co, :], op=ALU.add, R=[pb, x], W=[x])
            S.I("sp", "dma_start", out=self.xview(d["yT"], t0, TR), in_=x[:], R=[x], W=[xd], dma="xst")


def _rms_rstd_n(self, x, sq, rstd, n):
    S = self.S
    S.I("act", "activation", out=sq[:], in_=x[:], func=AF.Square, R=[x], W=[sq])
    pb = self.bank()
    om = self.cmat("onesm")
    for c in range(NC_):
        S.I("pe", "matmul", pb[:, 0:n], lhsT=om, rhs=sq[:, c, :], start=(c == 0), stop=(c == NC_ - 1),
             R=[sq, self.cb], W=[pb], inc=(c == NC_ - 1))
    self.rsqrt(rstd[:, 0:n], pb[:, 0:n], 1e-6, rstd, pb)


def _rsqrt(self, out_ap, in_ap, eps, out_tl, in_tl):
    S = self.S
    S.I("act", "activation", out=out_ap, in_=in_ap, func=AF.Ln, bias=self.epsc(eps), R=[in_tl, self.epst], W=[out_tl])
    S.I("act", "activation", out=out_ap, in_=out_ap, func=AF.Exp, scale=-0.5, R=[out_tl], W=[out_tl])


def _epsc(self, eps):
    return self.epst[:, self.eps_list.index(eps):self.eps_list.index(eps) + 1]


Builder.rsqrt = _rsqrt
Builder.epsc = _epsc
Builder.phase_rwkv = _phase_rwkv
Builder.rms_rstd_n = _rms_rstd_n


def _phase_kv(self):
    nc, S, d = self.nc, self.S, self.d
    g = self.kvstack
    T = self.T
    self.kTp = self.sb(g, "kTp", [128, 2, 2, T], BF16)
    self.Vt = self.sb(g, "Vt", [128, T // 128, 256], BF16)
    S.I("pool", "memset", self.kTp[:].rearrange("p a b t -> p (a b t)"), 0.0, W=[self.kTp])
    with ExitStack() as st:
        sb = lambda name, shape, dt: self.sb(st, name, shape, dt)
        wkv = sb("wkv", [128, 8, 256], BF16)
        self.load_cast(wkv, wkv[:], d["w_kv"].rearrange("(kc p) n -> p kc n", p=128))
        wkd = sb("wkd", [128, 2, 8, 128], BF16)
        wvd = sb("wvd", [128, 8, 256], BF16)
        for g_ in range(2):
            for dup in range(2):
                S.I("pool", "tensor_copy", wkd[:, g_, :, dup * 64:(dup + 1) * 64], wkv[:, :, g_ * 64:(g_ + 1) * 64], R=[wkv], W=[wkd])
                S.I("pool", "tensor_copy", wvd[:, :, g_ * 128 + dup * 64:g_ * 128 + (dup + 1) * 64], wkv[:, :, 128 + g_ * 64:128 + (g_ + 1) * 64], R=[wkv], W=[wvd])
        x = sb("x", [128, 8, TT], F32)
        sq = sb("sq", [128, 8, TT], BF16)
        rstd = sb("rstd", [128, TT], F32)
        xn = sb("xn", [128, 8, TT], BF16)
        k_sb = sb("k_sb", [128, TT], F32)
        kq = sb("kq", [128, TT], BF16)
        rk = sb("rk", [128, TT], F32)
        bd64 = self.cmat("bd64")
        for ti in range(self.NT):
            t0 = ti * TT
            S.I("sp", "dma_start", out=x[:], in_=self.xview(d["yT"], t0), R=[self.XD[ti]], W=[x], dma="kvx")
            self.rms_rstd_n(x, sq, rstd, TT)
            for c in range(8):
                S.I("dve", "scalar_tensor_tensor", out=xn[:, c, :], in0=x[:, c, :], scalar=self.vcol("kvn", c), in1=rstd[:], op0=ALU.mult, op1=ALU.mult, R=[x, rstd, self.vec], W=[xn])
            for g_ in range(2):
                pb = self.bank()
                for kc in range(8):
                    S.I("pe", "matmul", pb[:, :], lhsT=wkd[:, g_, kc, :], rhs=xn[:, kc, :], start=(kc == 0), stop=(kc == 7), R=[wkd, xn], W=[pb], inc=(kc == 7))
                S.I("act", "copy", k_sb[:], pb[:, :], R=[pb], W=[k_sb])
                S.I("act", "activation", out=kq[:], in_=k_sb[:], func=AF.Square, R=[k_sb], W=[kq])
                pb2 = self.bank()
                S.I("pe", "matmul", pb2[:, :], lhsT=bd64, rhs=kq[:], start=True, stop=True, R=[kq, self.cb], W=[pb2])
                self.rsqrt(rk[:], pb2[:, :], 1e-6, rk, pb2)
                S.I("pool", "tensor_tensor", out=k_sb[:], in0=k_sb[:], in1=rk[:], op=ALU.mult, R=[rk, k_sb], W=[k_sb])
                for e_ in range(2):
                    hs = slice(64 * e_, 64 * e_ + 64)
                    S.I("dve", "tensor_scalar", out=self.kTp[hs, e_, g_, t0:t0 + TT], in0=k_sb[hs, :], scalar1=self.hvec[hs, 2:3], scalar2=None, op0=ALU.mult, R=[k_sb, self.hvec], W=[self.kTp])
            for blk in range(4):
                pb = self.bank()
                for kc in range(8):
                    S.I("pe", "matmul", pb[:, 0:256], lhsT=xn[:, kc, blk * 128:(blk + 1) * 128], rhs=wvd[:, kc, :], start=(kc == 0), stop=(kc == 7), R=[wvd, xn], W=[pb], inc=(kc == 7))
                S.I("act", "copy", self.Vt[:, ti * 4 + blk, :], pb[:, 0:256], R=[pb], W=[self.Vt])


def _phase_attn(self, j, i):
    nc, S, d = self.nc, self.S, self.d
    with ExitStack() as st:
        sb = lambda name, shape, dt: self.sb(st, name, shape, dt)
        cview = lambda ap: ap.rearrange("(kc p) n -> p kc n", p=128)
        wq = sb("wq", [128, 8, 1024], BF16)
        wo = sb("wo", [128, 8, 1024], BF16)
        self.load_cast(wq, wq[:], cview(d["b_wq"][j]))
        self.load_cast(wo, wo[:], cview(d["b_wo"][j]))
        alibi = sb("alibi", [128, 2, 2, 8, 128], F32)
        S.I("sp", "dma_start", out=alibi[:], in_=d["alibi"], W=[alibi], dma="alibi")
        sk = sb("sk", [1, 2, 2, 512], F32)
        skb = sb("skb", [1, 2, 2, 512], BF16)
        S.I("sp", "dma_start", out=sk[:], in_=d["sinkrow"][:, j].rearrange("o e g h q -> o e g (h q)"), W=[sk], dma="sk")
        S.I("act", "activation", out=skb[:].rearrange("o e g n -> o (e g n)"), in_=sk[:].rearrange("o e g n -> o (e g n)"), func=AF.Exp, R=[sk], W=[skb])
        qg = sb("qg", [128, 1], F32)
        S.I("dve", "tensor_scalar", out=qg[:], in0=self.hvec[:, j:j + 1], scalar1=0.125, scalar2=None, op0=ALU.mult, R=[self.hvec], W=[qg])
        x = sb("x", [128, 8, TT], F32)
        rstd = sb("rstd", [128, TT], F32)
        h = sb("h", [128, 8, TT], BF16)
        q_all = sb("q_all", [128, 8, TT], F32)
        qq_all = sb("qq_all", [128, 8, TT], BF16)
        sq = qq_all
        rqs = [sb(f"rq{k}", [128, TT], F32) for k in range(2)]
        qT = sb("qT", [128, 8, TT], BF16)
        attT = sb("attT", [128, 8, TT], BF16)
        sc = [sb(f"sc{k}", [128, 512], F32) for k in range(4)]
        P = [sb(f"P{k}", [128, 512], BF16) for k in range(4)]
        rec = [sb(f"rec{k}", [128, 512], F32) for k in range(2)]
        bd64 = self.cmat("bd64")
        ones1 = self.cmat("ones1")
        for ti in range(self.NT):
            t0 = ti * TT
            S.I("sp", "dma_start", out=x[:], in_=self.xview(d["yT"], t0), R=[self.XD[ti]], W=[x], dma="atx")
            self.rms_rstd_n(x, sq, rstd, TT)
            for c in range(8):
                S.I("dve", "scalar_tensor_tensor", out=h[:, c, :], in0=x[:, c, :], scalar=self.vcol(f"ln_mix{i}", c), in1=rstd[:], op0=ALU.mult, op1=ALU.mult, R=[x, rstd, self.vec], W=[h])
            for co in range(8):
                pb = self.bank()
                for kc in range(8):
                    S.I("pe", "matmul", pb[:, :], lhsT=wq[:, kc, co * 128:(co + 1) * 128], rhs=h[:, kc, :], start=(kc == 0), stop=(kc == 7), R=[wq, h], W=[pb], inc=(kc == 7))
                S.I("act", "copy", q_all[:, co, :], pb[:, :], R=[pb], W=[q_all])
                S.I("act", "activation", out=qq_all[:, co, :], in_=q_all[:, co, :], func=AF.Square, R=[q_all], W=[qq_all])
            for co in range(8):
                rq_ = rqs[co % 2]
                pb2 = self.bank()
                S.I("pe", "matmul", pb2[:, :], lhsT=bd64, rhs=qq_all[:, co, :], start=True, stop=True, R=[qq_all, self.cb], W=[pb2])
                self.rsqrt(rq_[:], pb2[:, :], 1e-6, rq_, pb2)
                S.I("pool", "tensor_tensor", out=rq_[:], in0=q_all[:, co, :], in1=rq_[:], op=ALU.mult, R=[rq_, q_all], W=[rq_])
                S.I("dve", "tensor_scalar", out=qT[:, co, :], in0=rq_[:], scalar1=qg[:, 0:1], scalar2=None, op0=ALU.mult, R=[rq_, qg], W=[qT])
            items = [(blk, e_, g_) for blk in range(4) for e_ in range(2) for g_ in range(2)]

            def stageA(it, slot):
                blk, e_, g_ = it
                n = ti * 4 + blk
                bs = slice(blk * 128, (blk + 1) * 128)
                srcs = [(1, n)] + ([(0, n - 1)] if n > 0 else [])
                for k, (which, kb) in enumerate(srcs):
                    pbS = self.bank()
                    sck = sc[(self._sci) % len(sc)]
                    self._sci += 1
                    Pk = P[slot * 2 + k]
                    S.I("pe", "matmul", pbS[:, :], lhsT=self.kTp[:, e_, g_, kb * 128:(kb + 1) * 128], rhs=qT[:, 4 * g_:4 * g_ + 4, bs], start=True, stop=True, R=[self.kTp, qT], W=[pbS])
                    S.I("dve", "tensor_tensor", out=sck[:].rearrange("p (h q) -> p h q", h=4), in0=pbS[:, :].rearrange("p (h q) -> p h q", h=4), in1=alibi[:, which, e_, 4 * g_:4 * g_ + 4, :], op=ALU.add, R=[pbS, alibi], W=[sck])
                    S.I("act", "activation", out=Pk[:], in_=sck[:], func=AF.Exp, R=[sck], W=[Pk])
                return srcs

            def stageB(it, slot, srcs):
                blk, e_, g_ = it
                bs = slice(blk * 128, (blk + 1) * 128)
                hs = slice(64 * e_, 64 * e_ + 64)
                pbO = self.bank()
                pbD = self.bank()
                for k, (which, kb) in enumerate(srcs):
                    Pk = P[slot * 2 + k]
                    S.I("pe", "matmul", pbO[:, :], lhsT=self.Vt[:, kb, g_ * 128:(g_ + 1) * 128], rhs=Pk[:], start=(k == 0), stop=(k == len(srcs) - 1), R=[self.Vt, Pk], W=[pbO], inc=(k == len(srcs) - 1))
                for k, (which, kb) in enumerate(srcs):
                    Pk = P[slot * 2 + k]
                    S.I("pe", "matmul", pbD[:, :], lhsT=ones1, rhs=Pk[:], start=(k == 0), stop=False, R=[self.cb, Pk], W=[pbD], inc=False)
                S.I("pe", "matmul", pbD[:, :], lhsT=ones1[0:1, :], rhs=skb[0:1, e_, g_, :], start=False, stop=True, R=[self.cb, skb], W=[pbD])
                rc = rec[slot]
                S.I("act", "activation", out=rc[:], in_=pbD[:, :], func=AF.Ln, R=[pbD], W=[rc])
                S.I("act", "activation", out=rc[:], in_=rc[:], func=AF.Exp, scale=-1.0, R=[rc], W=[rc])
                S.I("dve", "tensor_tensor", out=attT[hs, 4 * g_:4 * g_ + 4, bs], in0=pbO[hs, :].rearrange("p (h q) -> p h q", h=4), in1=rc[hs, :].rearrange("p (h q) -> p h q", h=4), op=ALU.mult, R=[pbO, rc], W=[attT])

            self._sci = 0
            prev = None
            for idx, it in enumerate(items):
                srcs = stageA(it, idx % 2)
                if prev is not None:
                    stageB(*prev)
                prev = (it, idx % 2, srcs)
            stageB(*prev)
            for co in range(8):
                pb = self.bank()
                for kc in range(8):
                    S.I("pe", "matmul", pb[:, :], lhsT=wo[:, kc, co * 128:(co + 1) * 128], rhs=attT[:, kc, :], start=(kc == 0), stop=(kc == 7), R=[wo, attT], W=[pb], inc=(kc == 7))
                S.I("dve", "tensor_tensor", out=x[:, co, :], in0=pb[:, :], in1=x[:, co, :], op=ALU.add, R=[pb, x], W=[x])
            S.I("sp", "dma_start", out=self.xview(d["yT"], t0), in_=x[:], R=[x], W=[self.XD[ti]], dma="xst")


Builder.phase_kv = _phase_kv
Builder.phase_attn = _phase_attn
```
